# Optimizing a Trainium2 kernel written in Bass

```python
import math
import jax, jax.numpy as jnp
from jax import lax
import numpy as np

D_MODEL = 1024
BATCH = 16
SEQ = 2048
DEPTH = 1

MIX_WIDTH = D_MODEL
ATT_WIDTH = MIX_WIDTH // 2
HG_WIDTH = MIX_WIDTH - ATT_WIDTH
ATT_HEAD_DIM = 64
ATT_Q_HEADS = ATT_WIDTH // ATT_HEAD_DIM
ATT_KV_HEADS = 2
ATT_KV_COLS = ATT_KV_HEADS * ATT_HEAD_DIM
WINDOW = 128
ROPE_DIM = ATT_HEAD_DIM // 4
ROPE_THETA = 500000.0
HG_HEAD_DIM = 128
HG_HEADS = HG_WIDTH // HG_HEAD_DIM
HG_CHUNK = 32
IN_COLS = ATT_WIDTH + 2 * ATT_KV_COLS + 4 * HG_WIDTH
_SPLITS = list(np.cumsum([ATT_WIDTH, ATT_KV_COLS, ATT_KV_COLS, HG_WIDTH, HG_WIDTH, HG_WIDTH])[:].tolist())
D_FF = 4 * D_MODEL
N_MOD = 6
EPS = 1e-6

kernel_name = "hybrid_swa_sink_hgrn2_adaln_layer"


def rmsnorm(x, w):
    xf = x.astype(jnp.float32)
    y = xf * lax.rsqrt(jnp.mean(xf * xf, axis=-1, keepdims=True) + EPS)
    return (y * w.astype(jnp.float32)).astype(x.dtype)


def partial_rope(x):
    T = x.shape[1]
    half = ROPE_DIM // 2
    inv_freq = ROPE_THETA ** (-jnp.arange(0, ROPE_DIM, 2, dtype=jnp.float32) / ROPE_DIM)
    ang = jnp.arange(T, dtype=jnp.float32)[:, None] * inv_freq[None, :]
    cos = jnp.cos(ang)[None, :, None, :].astype(x.dtype)
    sin = jnp.sin(ang)[None, :, None, :].astype(x.dtype)
    x1, x2, rest = x[..., :half], x[..., half:ROPE_DIM], x[..., ROPE_DIM:]
    return jnp.concatenate([x1 * cos - x2 * sin, x2 * cos + x1 * sin, rest], axis=-1)


def sliding_window_sink_attention(q, k, v, sinks):
    B, T, Hq, D = q.shape
    nb = T // WINDOW
    G = Hq // ATT_KV_HEADS
    qb = q.reshape(B, nb, WINDOW, ATT_KV_HEADS, G, D)

    def band(a):
        ab = a.reshape(B, nb, WINDOW, ATT_KV_HEADS, D)
        prev = jnp.pad(ab, ((0, 0), (1, 0), (0, 0), (0, 0), (0, 0)))[:, :-1]
        return jnp.concatenate([prev, ab], axis=2)

    kk, vv = band(k), band(v)
    s = jnp.einsum('bnqhgd,bnkhd->bnhgqk', qb, kk).astype(jnp.float32) * (D ** -0.5)
    blk = jnp.arange(nb)[:, None]
    q_pos = blk * WINDOW + jnp.arange(WINDOW)[None, :]
    k_pos = (blk - 1) * WINDOW + jnp.arange(2 * WINDOW)[None, :]
    diff = q_pos[:, :, None] - k_pos[:, None, :]
    mask = (diff >= 0) & (diff < WINDOW) & (k_pos[:, None, :] >= 0)
    s = jnp.where(mask[None, :, None, None], s, jnp.finfo(jnp.float32).min)
    sink = sinks.astype(jnp.float32).reshape(ATT_KV_HEADS, G)[None, None, :, :, None, None]
    m = jnp.maximum(jnp.max(s, axis=-1, keepdims=True), sink)
    p = jnp.exp(s - m)
    p = p / (jnp.sum(p, axis=-1, keepdims=True) + jnp.exp(sink - m))
    o = jnp.einsum('bnhgqk,bnkhd->bnqhgd', p.astype(v.dtype), vv)
    return o.reshape(B, T, Hq * D)


def hgrn2_chunkwise(q, k, v, log_f):
    B, T, H, Dk = q.shape
    Dv = v.shape[-1]
    nc = T // HG_CHUNK

    def to_chunks(a):
        return a.astype(jnp.float32).reshape(B, nc, HG_CHUNK, H, a.shape[-1]).transpose(1, 0, 3, 2, 4)

    qc, kc, vc, gc = to_chunks(q), to_chunks(k), to_chunks(v), to_chunks(log_f)
    bc = jnp.cumsum(gc, axis=3)
    tri = jnp.tril(jnp.ones((HG_CHUNK, HG_CHUNK), dtype=bool))

    def step(S, inp):
        q_, k_, v_, b_ = inp
        b_last = b_[:, :, -1:, :]
        q_dec = q_ * jnp.exp(b_)
        k_dec = k_ * jnp.exp(-b_)
        a = jnp.where(tri, jnp.einsum('bhtk,bhsk->bhts', q_dec, k_dec), 0.0)
        o = jnp.einsum('bhts,bhsv->bhtv', a, v_) + jnp.einsum('bhtk,bhkv->bhtv', q_dec, S)
        S = S * jnp.exp(b_last[:, :, 0, :])[..., None] + \
            jnp.einsum('bhsk,bhsv->bhkv', k_ * jnp.exp(b_last - b_), v_)
        return S, o

    S0 = jnp.zeros((B, H, Dk, Dv), jnp.float32)
    _, o = lax.scan(step, S0, (qc, kc, vc, bc))
    return o.transpose(1, 0, 3, 2, 4).reshape(B, T, H, Dv)


def setup_inputs(seed: int = 0) -> dict:
    key = jax.random.key(seed)
    ks = jax.random.split(key, 17)
    f32 = jnp.float32

    def gain(k, shape):
        return (1.0 + 0.02 * jax.random.normal(k, shape)).astype(f32)

    return {
        "x": jax.random.normal(ks[0], (BATCH, SEQ, D_MODEL), f32),
        "c": jax.random.normal(ks[1], (BATCH, D_MODEL), f32),
        "w_ada": jax.random.normal(ks[2], (DEPTH, D_MODEL, N_MOD * D_MODEL), f32) * (0.5 * D_MODEL ** -0.5),
        "b_ada": jax.random.normal(ks[3], (DEPTH, N_MOD * D_MODEL), f32) * 0.02,
        "pre_w_mix": gain(ks[4], (DEPTH, D_MODEL)),
        "w_in": jax.random.normal(ks[5], (DEPTH, D_MODEL, IN_COLS), f32) * D_MODEL ** -0.5,
        "attn_sinks": jax.random.normal(ks[6], (DEPTH, ATT_Q_HEADS), f32) * 0.5,
        "attn_out_w": gain(ks[7], (DEPTH, ATT_WIDTH)),
        "lb_table": jax.random.normal(ks[8], (DEPTH + 1, HG_WIDTH), f32) * 0.1,
        "hg_norm_w": gain(ks[9], (DEPTH, HG_HEAD_DIM)),
        "w_out": jax.random.normal(ks[10], (DEPTH, MIX_WIDTH, D_MODEL), f32) * MIX_WIDTH ** -0.5,
        "post_w_mix": gain(ks[11], (DEPTH, D_MODEL)),
        "pre_w_mlp": gain(ks[12], (DEPTH, D_MODEL)),
        "w_up": jax.random.normal(ks[13], (DEPTH, D_MODEL, D_FF), f32) * D_MODEL ** -0.5,
        "w_down": jax.random.normal(ks[14], (DEPTH, D_FF, D_MODEL), f32) * D_FF ** -0.5,
        "post_w_mlp": gain(ks[15], (DEPTH, D_MODEL)),
    }


def reference(x, c, w_ada, b_ada, pre_w_mix, w_in, attn_sinks, attn_out_w, lb_table,
              hg_norm_w, w_out, post_w_mix, pre_w_mlp, w_up, w_down, post_w_mlp):
    B, T, _ = x.shape
    lb_p = jax.nn.softmax(lb_table.astype(jnp.float32), axis=0)
    lower_bounds = jnp.cumsum(lb_p, axis=0) - lb_p[0:1]
    c_act = jax.nn.silu(c)

    for l in range(DEPTH):
        mod = c_act @ w_ada[l] + b_ada[l]
        sh1, sc1, g1, sh2, sc2, g2 = [m[:, None, :] for m in jnp.split(mod, N_MOD, axis=-1)]

        h = rmsnorm(x, pre_w_mix[l]) * (1.0 + sc1) + sh1
        proj = h @ w_in[l]
        aq, ak, av, hq, hf, hi, hg = jnp.split(proj, _SPLITS, axis=-1)

        aq = partial_rope(aq.reshape(B, T, ATT_Q_HEADS, ATT_HEAD_DIM))
        ak = partial_rope(ak.reshape(B, T, ATT_KV_HEADS, ATT_HEAD_DIM))
        av = av.reshape(B, T, ATT_KV_HEADS, ATT_HEAD_DIM)
        attn = sliding_window_sink_attention(aq, ak, av, attn_sinks[l])
        attn = rmsnorm(attn, attn_out_w[l])

        lb = lower_bounds[l + 1].reshape(HG_HEADS, HG_HEAD_DIM)
        f = lb + (1.0 - lb) * jax.nn.sigmoid(hf.reshape(B, T, HG_HEADS, HG_HEAD_DIM).astype(jnp.float32))
        hq4 = jax.nn.silu(hq.reshape(B, T, HG_HEADS, HG_HEAD_DIM))
        hv4 = hi.reshape(B, T, HG_HEADS, HG_HEAD_DIM)
        rec = hgrn2_chunkwise(hq4, 1.0 - f, hv4, jnp.log(f)).astype(x.dtype)
        rec = rmsnorm(rec, hg_norm_w[l]) * jax.nn.silu(hg.reshape(B, T, HG_HEADS, HG_HEAD_DIM))
        rec = rec.reshape(B, T, HG_WIDTH)

        mix = jnp.concatenate([attn, rec], axis=-1) @ w_out[l]
        x = x + g1 * rmsnorm(mix, post_w_mix[l])

        h = rmsnorm(x, pre_w_mlp[l]) * (1.0 + sc2) + sh2
        u = jnp.square(jax.nn.relu(h @ w_up[l]))
        x = x + g2 * rmsnorm(u @ w_down[l], post_w_mlp[l])
    return x
```

```python
import numpy as np
from contextlib import ExitStack
import concourse.bass as bass
import concourse.mybir as mybir
from concourse.bass_utils import run_bass_kernel_spmd

F32 = mybir.dt.float32
BF16 = mybir.dt.bfloat16
AF = mybir.ActivationFunctionType
ALU = mybir.AluOpType
AX = mybir.AxisListType

NCORES = 8
D = 1024
SEQ = 2048
NSEQ = 2
TOK = NSEQ * SEQ
NT = TOK // 128
TPS = SEQ // 128
NG = NT // 2
GW = 256
DFF = 4096
INC = 2816
EPS = 1e-6
KRING = 6
import os as _os
BSTOP = float(_os.environ.get('KBSTOP', '99'))


class Buf:
    __slots__ = ("name", "last_w", "readers")

    def __init__(self, name):
        self.name = name
        self.last_w = None
        self.readers = []


class Op:
    __slots__ = ("eng", "fn", "deps", "idx", "ndma", "signal", "semval", "sem", "name", "prevval")


class Prog:
    ENGS = ("pe", "act", "dve", "pool", "sp")
    NDMASEM = 24

    def __init__(self, nc):
        self.nc = nc
        self.streams = {e: [] for e in self.ENGS}
        self.es = ExitStack()
        self.nbuf = 0
        self.dma_since_barrier = []

    def sb(self, name, shape, dt, stack=None):
        return (stack or self.es).enter_context(self.nc.sbuf_tensor("s_" + name, list(shape), dt))

    def ps(self, name, shape, dt):
        return self.es.enter_context(self.nc.psum_tensor("p_" + name, list(shape), dt))

    def buf(self, name=None):
        self.nbuf += 1
        return Buf(name or f"b{self.nbuf}")

    def bufs(self, n, name="b"):
        return [self.buf(f"{name}{i}") for i in range(n)]

    def op(self, eng, fn, reads=(), writes=(), ndma=0, name=None):
        o = Op()
        o.eng = eng
        o.fn = fn
        o.ndma = ndma
        o.signal = False
        o.semval = None
        o.sem = None
        o.name = name
        o.prevval = 0
        deps = []
        for b in reads:
            if b.last_w is not None:
                deps.append(b.last_w)
        for b in writes:
            if b.last_w is not None:
                deps.append(b.last_w)
            deps.extend(b.readers)
        seen = set()
        dd = []
        for d in deps:
            if id(d) in seen:
                continue
            seen.add(id(d))
            if eng == "pe" and d.eng == "pe" and d.ndma == 0:
                continue
            dd.append(d)
        o.deps = dd
        o.idx = len(self.streams[eng])
        self.streams[eng].append(o)
        for b in reads:
            b.readers.append(o)
        for b in writes:
            b.last_w = o
            b.readers = []
        if ndma > 0:
            self.dma_since_barrier.append(o)
        return o

    def barrier(self):
        lasts = {}
        for e in self.ENGS:
            for o in reversed(self.streams[e]):
                if o.ndma == 0 and o.fn is not None:
                    lasts[e] = o
                    break
        dmas = list(self.dma_since_barrier)
        self.dma_since_barrier = []
        for e in self.ENGS:
            o = Op()
            o.eng = e
            o.fn = None
            o.ndma = 0
            o.signal = False
            o.semval = None
            o.sem = None
            o.name = "barrier"
            o.prevval = 0
            o.deps = [lasts[x] for x in lasts if x != e] + dmas
            o.idx = len(self.streams[e])
            self.streams[e].append(o)

    def emit(self):
        nc = self.nc
        for e in self.ENGS:
            for o in self.streams[e]:
                for d in o.deps:
                    d.signal = True
        for e in self.ENGS:
            cnt = 0
            for o in self.streams[e]:
                if o.ndma == 0 and o.signal:
                    cnt += 1
                    o.semval = cnt
        dcount = [0] * self.NDMASEM
        dpool = {"sp": list(range(0, 16)), "pool": list(range(16, 24))}
        for e in self.ENGS:
            rr = 0
            for o in self.streams[e]:
                if o.ndma > 0:
                    s = dpool[e][rr % len(dpool[e])]
                    rr += 1
                    o.sem = s
                    o.prevval = dcount[s]
                    dcount[s] += 16 * o.ndma
                    o.semval = dcount[s]
        es = self.es
        esem = {e: es.enter_context(nc.semaphore(f"tl_{e}")) for e in self.ENGS}
        dsem = [es.enter_context(nc.semaphore(f"dma_{i}")) for i in range(self.NDMASEM)]
        block = es.enter_context(nc.Block())
        streams = self.streams

        def run_stream(e, engobj):
            seen = {}
            for o in streams[e]:
                waits = []
                for d in o.deps:
                    if d.ndma > 0:
                        key = ("d", d.sem)
                        sem = dsem[d.sem]
                    else:
                        key = ("e", d.eng)
                        sem = esem[d.eng]
                    v = d.semval
                    if seen.get(key, 0) >= v:
                        continue
                    waits.append((key, sem, v))
                if o.ndma > 0:
                    key = ("d", o.sem)
                    if o.prevval > 0 and seen.get(key, 0) < o.prevval:
                        waits.append((key, dsem[o.sem], o.prevval))
                best = {}
                for key, sem, v in waits:
                    if key not in best or best[key][1] < v:
                        best[key] = (sem, v)
                wl = list(best.items())
                for key, (sem, v) in wl:
                    seen[key] = v
                if o.ndma > 0:
                    for key, (sem, v) in wl:
                        engobj.wait_ge(sem, v)
                    ins = o.fn(engobj)
                    if not isinstance(ins, (list, tuple)):
                        ins = [ins]
                    assert len(ins) == o.ndma, (o.name, len(ins), o.ndma)
                    for i in ins:
                        i.then_inc(dsem[o.sem], 16)
                elif o.fn is None:
                    for key, (sem, v) in wl:
                        engobj.wait_ge(sem, v)
                else:
                    for key, (sem, v) in wl[1:]:
                        engobj.wait_ge(sem, v)
                    ins = o.fn(engobj)
                    if not isinstance(ins, (list, tuple)):
                        ins = [ins]
                    if wl:
                        ins[0]._wait_ge(wl[0][1][0], wl[0][1][1])
                    if o.signal:
                        ins[-1].then_inc(esem[e], 1)

        @block.tensor
        def _(eng):
            run_stream("pe", eng)

        @block.scalar
        def _(eng):
            run_stream("act", eng)

        @block.vector
        def _(eng):
            run_stream("dve", eng)

        @block.gpsimd
        def _(eng):
            run_stream("pool", eng)

        @block.sync
        def _(eng):
            run_stream("sp", eng)

        es.close()


def run_interleaved(gens):
    gens = list(gens)
    while gens:
        for g in list(gens):
            try:
                next(g)
            except StopIteration:
                gens.remove(g)


def _host_consts():
    c = {}
    c["ident"] = np.eye(128, dtype=np.float32)
    c["ones"] = np.ones((128, 128), dtype=np.float32)
    k = np.arange(128)[:, None]
    q = np.arange(128)[None, :]
    cur = (k <= q).astype(np.float32)
    prev = (k > q).astype(np.float32)
    c["mask2"] = np.ascontiguousarray(np.stack([prev, cur], axis=1).reshape(128, 256))
    c["maskT"] = np.ascontiguousarray(cur)
    rm = np.ones((128, GW), dtype=np.float32)
    rm[:, 0::128] = 0.0
    c["rmask"] = rm
    ps = np.zeros((128, 128), dtype=np.float32)
    for m in range(128):
        d = m % 64
        base = m - d
        if d < 8:
            s = base + d + 8
        elif d < 16:
            s = base + d - 8
        else:
            s = m
        ps[s, m] = 1.0
    c["pswap"] = ps
    inv_freq = (np.float32(500000.0) ** (-np.arange(0, 16, 2, dtype=np.float32) / np.float32(16))).astype(np.float32)
    ang = (np.arange(SEQ, dtype=np.float32)[:, None] * inv_freq[None, :]).astype(np.float32)
    cs = np.cos(ang).astype(np.float32)
    sn = np.sin(ang).astype(np.float32)
    cosF = np.ones((128, SEQ), dtype=np.float32)
    sinF = np.zeros((128, SEQ), dtype=np.float32)
    for p in range(128):
        d = p % 64
        if d < 8:
            cosF[p] = cs[:, d]
            sinF[p] = -sn[:, d]
        elif d < 16:
            cosF[p] = cs[:, d - 8]
            sinF[p] = sn[:, d - 8]
    c["cosF"] = cosF
    c["sinF"] = sinF
    return c


def build_program(stop=99, ngroups=NG):
    nc = bass.Bass("TRN2", target_bir_lowering=False)
    P = Prog(nc)

    def din(name, shape):
        return nc.dram_tensor(name, list(shape), F32, kind="ExternalInput").ap()

    x_d = din("x", [TOK, D])
    cT_d = din("cT", [128, 8, 2])
    wada_d = din("w_ada", [D, 6 * D])
    bada_d = din("b_adaT", [128, 48])
    premix_d = din("pre_mixT", [128, 8])
    premlp_d = din("pre_mlpT", [128, 8])
    postmix_d = din("post_mixT", [128, 8])
    postmlp_d = din("post_mlpT", [128, 8])
    win_d = din("w_in", [D, INC])
    wout_d = din("w_out", [D, D])
    wup_d = din("w_up", [D, DFF])
    wdn_d = din("w_down", [DFF, D])
    sinks_d = din("sinks_rep", [128, 8])
    aow_d = din("aow_rep", [128, 512])
    hgw_d = din("hgw_rep", [128, 512])
    lbT_d = din("lbT", [128, 2, 4])
    ident_d = din("ident", [128, 128])
    ones_d = din("ones", [128, 128])
    mask2_d = din("mask2", [128, 256])
    maskT_d = din("maskT", [128, 128])
    rmask_d = din("rmask", [128, GW])
    pswap_d = din("pswap", [128, 128])
    cosF_d = din("cosF", [128, SEQ])
    sinF_d = din("sinF", [128, SEQ])
    out_d = nc.dram_tensor("out", [TOK, D], F32, kind="ExternalOutput").ap()
    x1_d = nc.dram_tensor("x1_scratch", [TOK, D], F32).ap()

    ident_bf = P.sb("ident_bf", [128, 128], BF16)
    ident_f = P.sb("ident_f", [128, 128], F32)
    ones_f = P.sb("ones_f", [128, 128], F32)
    pswap_bf = P.sb("pswap_bf", [128, 128], BF16)
    mask2_bf = P.sb("mask2_bf", [128, 2, 128], BF16)
    maskT_f = P.sb("maskT_f", [128, 128], F32)
    rmask = P.sb("rmask", [128, GW], F32)
    epsc = P.sb("epsc", [128, 1], F32)
    esink = P.sb("esink", [128, 8], F32)
    modT = P.sb("modT", [128, 48, 2], F32)
    a1 = P.sb("a1", [128, 8, 2], F32)
    a2 = P.sb("a2", [128, 8, 2], F32)
    gw1 = P.sb("gw1", [128, 8, 2], F32)
    gw2 = P.sb("gw2", [128, 8, 2], F32)
    lbv = P.sb("lbv", [128, 4], F32)
    oml = P.sb("oml", [128, 4], F32)
    lbm1 = P.sb("lbm1", [128, 4], F32)
    gv2 = [P.sb(f"gv2_{b}", [128, D], F32) for b in range(2)]
    ss1 = P.sb("ss1", [128, NT], F32)
    rs1 = P.sb("rs1", [128, NT, 2], F32)
    ssa = P.sb("ssa", [128, NT], F32)
    rsa = P.sb("rsa", [128, NT, 2], F32)
    ssm = P.sb("ssm", [128, NT, 4], F32)
    ssr = P.sb("ssr", [128, NT, 4], F32)
    rsr = P.sb("rsr", [128, NT, 8], F32)
    den = P.sb("den", [128, NT, 2, 4], F32)
    rden = P.sb("rden", [128, NT, 2, 4], F32)
    ss2 = P.sb("ss2", [128, NT], F32)
    rs2 = P.sb("rs2", [128, NT, 2], F32)
    ssy = P.sb("ssy", [128, NT, 4], F32)
    xs = [P.sb(f"xs{i}", [128, D], F32) for i in range(4)]
    xn = [P.sb(f"xn{i}", [128, D], BF16) for i in range(2)]
    junk = P.sb("junk", [128, D], BF16)
    hT = [P.sb(f"hT{i}", [128, 8, GW], BF16) for i in range(2)]
    tmpx = [P.sb(f"tmpx{i}", [128, D], F32) for i in range(2)]

    pp = ExitStack()
    aow = P.sb("aow", [128, 512], F32, pp)
    hgw = P.sb("hgw", [128, 512], F32, pp)
    gv1 = [P.sb(f"gv1_{b}", [128, D], F32, pp) for b in range(2)]

    pb = [P.ps(f"pb{i}", [128, 512], F32) for i in range(7)]
    ptr = P.ps("ptr", [128, 1024], BF16)
    B_pb = P.bufs(7, "pb")
    B_ptr = P.buf("ptr")

    B_const = P.buf("const")
    B_mod = P.buf("mod")
    B_xs = P.bufs(4, "xs")
    B_xn = P.bufs(2, "xn")
    B_junk = P.buf("junk")
    B_hT = P.bufs(2, "hT")
    B_tmpx = P.bufs(2, "tmpx")
    B_stat = [P.buf(f"stat{t}") for t in range(NT)]
    B_x1d = [P.buf(f"x1d{t}") for t in range(NT)]
    B_out = [P.buf(f"outd{t}") for t in range(NT)]

    def ld(eng, dst, src, b=B_const, n=1):
        P.op(eng, lambda e: e.dma_start(out=dst, in_=src), writes=[b], ndma=n)

    ld("pool", ident_bf[:], ident_d[:, :])
    ld("sp", ident_f[:], ident_d[:, :])
    ld("sp", ones_f[:], ones_d[:, :])
    ld("pool", pswap_bf[:], pswap_d[:, :])
    ld("pool", mask2_bf[:].rearrange("p a b -> p (a b)"), mask2_d[:, :])
    ld("sp", maskT_f[:], maskT_d[:, :])
    ld("sp", rmask[:], rmask_d[:, :])
    ld("sp", esink[:], sinks_d[:, :])
    ld("sp", aow[:], aow_d[:, :])
    ld("sp", hgw[:], hgw_d[:, :])
    P.op("pool", lambda e: e.memset(epsc[:], EPS), writes=[B_const])
    P.op("act", lambda e: e.activation(out=esink[:], in_=esink[:], func=AF.Exp), reads=[B_const], writes=[B_const])

    st = ExitStack()
    cTs = P.sb("cTs", [128, 8, 2], F32, st)
    ca = P.sb("ca", [128, 8, 2], F32, st)
    badaT = P.sb("badaT", [128, 48], F32, st)
    premix = P.sb("premix", [128, 8], F32, st)
    premlp = P.sb("premlp", [128, 8], F32, st)
    postmix = P.sb("postmix", [128, 8], F32, st)
    postmlp = P.sb("postmlp", [128, 8], F32, st)
    lbT = P.sb("lbT", [128, 2, 4], F32, st)
    lbd = P.sb("lbd", [128, 4], F32, st)
    wa = [P.sb(f"wa{i}", [128, 8, 512], F32, st) for i in range(2)]
    dg = [P.sb(f"dg{i}", [128, 128], F32, st) for i in range(2)]
    B_wa = P.bufs(2, "wa")
    B_dg = P.bufs(2, "dg")
    B_set = P.buf("setup")

    ld("sp", cTs[:], cT_d[:, :, :], B_set)
    ld("sp", badaT[:], bada_d[:, :], B_set)
    ld("sp", premix[:], premix_d[:, :], B_set)
    ld("sp", premlp[:], premlp_d[:, :], B_set)
    ld("sp", postmix[:], postmix_d[:, :], B_set)
    ld("sp", postmlp[:], postmlp_d[:, :], B_set)
    ld("sp", lbT[:], lbT_d[:, :, :], B_set)
    P.op("act", lambda e: e.activation(out=ca[:], in_=cTs[:], func=AF.Silu), reads=[B_set], writes=[B_set])
    P.op("dve", lambda e: e.tensor_tensor(lbd[:], lbT[:, 1, :], lbT[:, 0, :], ALU.subtract), reads=[B_set], writes=[B_set])
    P.op("act", lambda e: e.activation(out=lbv[:], in_=lbd[:], func=AF.Sigmoid), reads=[B_set], writes=[B_mod])
    P.op("dve", lambda e: e.tensor_scalar(oml[:], lbv[:], -1.0, 1.0, ALU.mult, ALU.add), reads=[B_mod], writes=[B_mod])
    P.op("dve", lambda e: e.tensor_scalar(lbm1[:], lbv[:], 1.0, -1.0, ALU.mult, ALU.add), reads=[B_mod], writes=[B_mod])

    wada_v = wada_d.rearrange("(j p) c -> p j c", p=128)
    pmod = pb[0][:, 0:96]
    for blk in range(12):
        s = blk % 2
        P.op("sp", (lambda e, s=s, blk=blk: e.dma_start(out=wa[s][:], in_=wada_v[:, :, blk * 512:(blk + 1) * 512])),
             writes=[B_wa[s]], ndma=1)

        def mm_mod(e, s=s, blk=blk):
            ins = []
            for mm in range(4):
                m = blk * 4 + mm
                for k in range(8):
                    ins.append(e.matmul(pmod[:, 2 * m:2 * m + 2], wa[s][:, k, mm * 128:(mm + 1) * 128], ca[:, k, :],
                                        start=(k == 0), stop=(k == 7)))
            return ins
        P.op("pe", mm_mod, reads=[B_wa[s], B_set], writes=[B_pb[0]])
    P.op("dve", lambda e: e.tensor_tensor(modT[:], pmod.rearrange("p (m b) -> p m b", b=2),
                                          badaT[:].unsqueeze(2).broadcast_to([128, 48, 2]), ALU.add),
         reads=[B_pb[0], B_set], writes=[B_mod])

    def bc2(v):
        return v[:].unsqueeze(2).broadcast_to([128, 8, 2])
    P.op("dve", lambda e: e.scalar_tensor_tensor(a1[:], modT[:, 8:16, :], 1.0, bc2(premix), ALU.add, ALU.mult),
         reads=[B_mod, B_set], writes=[B_mod])
    P.op("dve", lambda e: e.scalar_tensor_tensor(a2[:], modT[:, 32:40, :], 1.0, bc2(premlp), ALU.add, ALU.mult),
         reads=[B_mod, B_set], writes=[B_mod])
    P.op("dve", lambda e: e.tensor_tensor(gw1[:], modT[:, 16:24, :], bc2(postmix), ALU.mult), reads=[B_mod, B_set], writes=[B_mod])
    P.op("dve", lambda e: e.tensor_tensor(gw2[:], modT[:, 40:48, :], bc2(postmlp), ALU.mult), reads=[B_mod, B_set], writes=[B_mod])
    cnt = 0
    for gwt, gvt in ((gw1, gv1), (gw2, gv2)):
        for b in range(2):
            for jh in range(2):
                bank = 1 + (cnt % 2)
                for jj in range(4):
                    j = jh * 4 + jj
                    s = cnt % 2
                    cnt += 1
                    P.op("dve", (lambda e, s=s, gwt=gwt, j=j, b=b: e.tensor_scalar(dg[s][:], ident_f[:], gwt[:, j, b:b + 1], None, ALU.mult)),
                         reads=[B_mod, B_const], writes=[B_dg[s]])
                    P.op("pe", (lambda e, s=s, bank=bank, jj=jj: e.matmul(pb[bank][:, jj * 128:(jj + 1) * 128], ones_f[:], dg[s][:], start=True, stop=True)),
                         reads=[B_dg[s], B_const], writes=[B_pb[bank]])
                P.op("act", (lambda e, bank=bank, gvt=gvt, b=b, jh=jh: e.copy(gvt[b][:, jh * 512:(jh + 1) * 512], pb[bank][:])),
                     reads=[B_pb[bank]], writes=[B_mod])
    P.barrier()
    st.close()

    p1 = ExitStack()
    win = P.sb("win", [128, 8, INC], BF16, p1)
    wkd = P.sb("wkd", [128, 8, 256], BF16, p1)
    wout = P.sb("wout", [128, 8, D], BF16, p1)
    cosS = [P.sb(f"cosS{i}", [128, GW], F32, p1) for i in range(2)]
    sinS = [P.sb(f"sinS{i}", [128, GW], F32, p1) for i in range(2)]
    qraw = [P.sb(f"qraw{i}", [128, GW], BF16, p1) for i in range(2)]
    rt1 = [P.sb(f"rt1_{i}", [128, GW], F32, p1) for i in range(2)]
    rt2 = [P.sb(f"rt2_{i}", [128, GW], F32, p1) for i in range(2)]
    qT = [P.sb(f"qT{i}", [128, 4, GW], BF16, p1) for i in range(2)]
    kT = P.sb("kT", [128, 2, KRING * 128], BF16, p1)
    vaug = P.sb("vaug", [128, KRING, 2, 72], BF16, p1)
    hA = [P.sb(f"hA{i}", [128, GW], F32, p1) for i in range(2)]
    hB = [P.sb(f"hB{i}", [128, GW], F32, p1) for i in range(2)]
    hC = [P.sb(f"hC{i}", [128, GW], F32, p1) for i in range(2)]
    qd = [P.sb(f"qd{i}", [128, 4, GW], BF16, p1) for i in range(2)]
    kd = [P.sb(f"kd{i}", [128, 4, GW], BF16, p1) for i in range(2)]
    dec = [P.sb(f"dec{i}", [128, 4, 2], F32, p1) for i in range(2)]
    vt = [P.sb(f"vt{i}", [128, 512], BF16, p1) for i in range(2)]
    gg = [P.sb(f"gg{i}", [128, 512], F32, p1) for i in range(2)]
    Pe = [P.sb(f"Pe{i}", [128, 4, 2, 128], BF16, p1) for i in range(2)]
    Pm = [P.sb(f"Pm{i}", [128, 4, 2, 128], BF16, p1) for i in range(2)]
    attn = [P.sb(f"attn{i}", [128, 512], F32, p1) for i in range(2)]
    Am = [P.sb(f"Am{i}", [128, 4, 128], BF16, p1) for i in range(2)]
    kdtok = [P.sb(f"kdtok{i}", [128, 4, 128], BF16, p1) for i in range(2)]
    tmpS = [P.sb(f"tmpS{i}", [128, 4, 128], F32, p1) for i in range(2)]
    sqo = [P.sb(f"sqo{i}", [128, 4, 128], F32, p1) for i in range(2)]
    rect = sqo
    cat = [P.sb(f"cat{i}", [128, D], BF16, p1) for i in range(2)]
    catT = [P.sb(f"catT{i}", [128, 8, 128], BF16, p1) for i in range(2)]
    Sst = P.sb("Sst", [128, 4, 128], F32, p1)
    Sb = P.sb("Sb", [128, 4, 128], BF16, p1)

    B_w1 = P.buf("w1")
    B_rope = P.bufs(2, "rope")
    B_qraw = P.bufs(2, "qraw")
    B_rt1 = P.bufs(2, "rt1")
    B_rt2 = P.bufs(2, "rt2")
    B_qT = P.bufs(2, "qT")
    B_kT = P.bufs(KRING, "kT")
    B_va = P.bufs(KRING, "va")
    B_hA = P.bufs(2, "hA")
    B_hB = P.bufs(2, "hB")
    B_hC = P.bufs(2, "hC")
    B_qd = P.bufs(2, "qd")
    B_kd = P.bufs(2, "kd")
    B_dec = P.bufs(2, "dec")
    B_vt = P.bufs(2, "vt")
    B_gg = P.bufs(2, "gg")
    B_Pe = P.bufs(2, "Pe")
    B_Pm = P.bufs(2, "Pm")
    B_attn = P.bufs(2, "attn")
    B_Am = P.bufs(2, "Am")
    B_kdtok = P.bufs(2, "kdtok")
    B_tmpS = P.bufs(2, "tmpS")
    B_sqo = P.bufs(2, "sqo")
    B_rect = B_sqo
    B_cat = P.bufs(2, "cat")
    B_catT = P.bufs(2, "catT")
    B_S = P.buf("S")
    B_Sb = P.buf("Sb")

    win_v = win_d.rearrange("(j p) c -> p j c", p=128)
    wout_v = wout_d.rearrange("(j p) c -> p j c", p=128)
    for j in range(8):
        P.op("pool", (lambda e, j=j: e.dma_start(out=win[:, j, :], in_=win_v[:, j, :])), writes=[B_w1], ndma=1)
    for kv in range(2):
        for r in range(2):
            c0 = (2 * kv + r) * 64
            P.op("pool", (lambda e, kv=kv, c0=c0: e.dma_start(out=wkd[:, :, c0:c0 + 64], in_=win_v[:, :, 512 + 64 * kv:576 + 64 * kv])),
                 writes=[B_w1], ndma=1)
    for j in range(8):
        P.op("pool", (lambda e, j=j: e.dma_start(out=wout[:, j, :], in_=wout_v[:, j, :])), writes=[B_w1], ndma=1)
    P.op("pool", lambda e: e.memset(vaug[:], 1.0), writes=B_va)

    pools = {"proj": [0, 1], "o": [4, 5, 6]}
    pcnt = {"proj": 0, "o": 0}

    def bank(pool):
        i = pools[pool][pcnt[pool] % len(pools[pool])]
        pcnt[pool] += 1
        return i

    state = {"av": -1, "s": -1}

    def rstd_ops(ss_ap, lnbuf_ap, out_ap, n, reads, writes):
        P.op("act", lambda e: e.activation(out=lnbuf_ap, in_=ss_ap, func=AF.Ln, scale=1.0 / n, bias=epsc[:, 0:1]),
             reads=reads + [B_const], writes=writes)
        P.op("act", lambda e: e.activation(out=out_ap, in_=lnbuf_ap, func=AF.Exp, scale=-0.5), reads=writes, writes=writes)

    def norm_and_transpose(g, a_mod, sh_lo, src_phase2):
        hs = g % 2
        b = (2 * g) // TPS
        ssx, rsx = (ss2, rs2) if src_phase2 else (ss1, rs1)
        for i in range(2):
            T = 2 * g + i
            xsl = T % 4
            P.op("act", (lambda e, T=T, xsl=xsl: e.activation(out=junk[:], in_=xs[xsl][:], func=AF.Square, accum_out=ssx[:, T:T + 1])),
                 reads=[B_xs[xsl]], writes=[B_stat[T]])
            rstd_ops(ssx[:, T:T + 1], rsx[:, T, 0:1], rsx[:, T, 1:2], D, [B_stat[T]], [B_stat[T]])
            P.op("dve", (lambda e, T=T, xsl=xsl, i=i: e.tensor_scalar(xn[i][:], xs[xsl][:], rsx[:, T, 1:2], None, ALU.mult)),
                 reads=[B_xs[xsl], B_stat[T]], writes=[B_xn[i]])
        yield
        for rnd in range(2):
            def tr(e, rnd=rnd):
                ins = []
                for jj in range(4):
                    j = rnd * 4 + jj
                    for i in range(2):
                        ins.append(e.transpose(ptr[:, jj * GW + i * 128: jj * GW + (i + 1) * 128], xn[i][:, j * 128:(j + 1) * 128], ident_bf[:]))
                return ins
            P.op("pe", tr, reads=[B_xn[0], B_xn[1], B_const], writes=[B_ptr])
            for jj in range(4):
                j = rnd * 4 + jj
                P.op("dve", (lambda e, j=j, jj=jj: e.tensor_scalar(hT[hs][:, j, :], ptr[:, jj * GW:(jj + 1) * GW],
                                                                  a_mod[:, j, b:b + 1], modT[:, sh_lo + j, b:b + 1], ALU.mult, ALU.add)),
                     reads=[B_ptr, B_mod], writes=[B_hT[hs]])
            yield

    def stage_A(g):
        hs = g % 2
        tp0 = (2 * g) % TPS
        for i in range(2):
            T = 2 * g + i
            xsl = T % 4
            P.op("sp", (lambda e, T=T, xsl=xsl: e.dma_start(out=xs[xsl][:], in_=x_d[T * 128:(T + 1) * 128, :])), writes=[B_xs[xsl]], ndma=1)
        P.op("sp", (lambda e: [e.dma_start(out=cosS[hs][:], in_=cosF_d[:, tp0 * 128: tp0 * 128 + GW]),
                               e.dma_start(out=sinS[hs][:], in_=sinF_d[:, tp0 * 128: tp0 * 128 + GW])]), writes=[B_rope[hs]], ndma=2)
        yield from norm_and_transpose(g, a1, 0, False)

        def fm_proj(w_ap_fn, reads_extra=()):
            bk = bank("proj")

            def f(e):
                return [e.matmul(pb[bk][:, 0:GW], w_ap_fn(k), hT[hs][:, k, :], start=(k == 0), stop=(k == 7)) for k in range(8)]
            P.op("pe", f, reads=[B_hT[hs], B_w1], writes=[B_pb[bk]])
            return bk

        for h in range(4):
            s = h % 2
            bk = fm_proj(lambda k, h=h: win[:, k, 1280 + h * 128: 1280 + (h + 1) * 128])
            P.op("act", (lambda e, bk=bk, s=s: e.activation(out=hA[s][:], in_=pb[bk][:, 0:GW], func=AF.Sigmoid)),
                 reads=[B_pb[bk]], writes=[B_hA[s]])
            P.op("act", (lambda e, s=s, h=h: e.activation(out=hB[s][:], in_=hA[s][:], func=AF.Ln, scale=oml[:, h:h + 1], bias=lbv[:, h:h + 1])),
                 reads=[B_hA[s], B_mod], writes=[B_hB[s]])
            P.op("dve", (lambda e, s=s, h=h: e.tensor_scalar(hA[s][:], hA[s][:], lbm1[:, h:h + 1], oml[:, h:h + 1], ALU.mult, ALU.add)),
                 reads=[B_hA[s], B_mod], writes=[B_hA[s]])
            P.op("dve", (lambda e, s=s: e.tensor_tensor_scan(hC[s][:], rmask[:], hB[s][:], 0.0, ALU.mult, ALU.add)),
                 reads=[B_hB[s], B_const], writes=[B_hC[s]])
            P.op("act", (lambda e, s=s: e.activation(out=hB[s][:], in_=hC[s][:], func=AF.Exp, scale=-1.0)),
                 reads=[B_hC[s]], writes=[B_hB[s]])
            P.op("pool", (lambda e, s=s, h=h: e.tensor_tensor(kd[hs][:, h, :], hA[s][:], hB[s][:], ALU.mult)),
                 reads=[B_hA[s], B_hB[s]], writes=[B_kd[hs]])
            P.op("act", (lambda e, s=s: e.activation(out=hB[s][:], in_=hC[s][:], func=AF.Exp)),
                 reads=[B_hC[s]], writes=[B_hB[s]])
            P.op("pool", (lambda e, s=s, h=h: e.tensor_copy(dec[hs][:, h, :], hB[s][:, 127::128])),
                 reads=[B_hB[s]], writes=[B_dec[hs]])
            bk2 = fm_proj(lambda k, h=h: win[:, k, 768 + h * 128: 768 + (h + 1) * 128])
            P.op("act", (lambda e, bk2=bk2, s=s: e.activation(out=hA[s][:], in_=pb[bk2][:, 0:GW], func=AF.Silu)),
                 reads=[B_pb[bk2]], writes=[B_hA[s]])
            P.op("pool", (lambda e, s=s, h=h: e.tensor_tensor(qd[hs][:, h, :], hA[s][:], hB[s][:], ALU.mult)),
                 reads=[B_hA[s], B_hB[s]], writes=[B_qd[hs]])
            yield

        for c in range(6):
            s = c % 2
            if c < 4:
                bk = fm_proj(lambda k, c=c: win[:, k, c * 128:(c + 1) * 128])
            else:
                bk = fm_proj(lambda k, c=c: wkd[:, k, (c - 4) * 128:(c - 3) * 128])
            P.op("act", (lambda e, bk=bk, s=s: e.copy(qraw[s][:], pb[bk][:, 0:GW])), reads=[B_pb[bk]], writes=[B_qraw[s]])
            bk2 = bank("proj")
            P.op("pe", (lambda e, bk2=bk2, s=s: e.matmul(pb[bk2][:, 0:GW], pswap_bf[:], qraw[s][:], start=True, stop=True)),
                 reads=[B_qraw[s], B_const], writes=[B_pb[bk2]])
            P.op("dve", (lambda e, bk2=bk2, s=s: e.tensor_tensor(rt1[s][:], pb[bk2][:, 0:GW], sinS[hs][:], ALU.mult)),
                 reads=[B_pb[bk2], B_rope[hs]], writes=[B_rt1[s]])
            P.op("pool", (lambda e, s=s: e.tensor_tensor(rt2[s][:], qraw[s][:], cosS[hs][:], ALU.mult)),
                 reads=[B_qraw[s], B_rope[hs]], writes=[B_rt2[s]])
            if c < 4:
                P.op("pool", (lambda e, s=s, c=c: e.tensor_tensor(qT[hs][:, c, :], rt1[s][:], rt2[s][:], ALU.add)),
                     reads=[B_rt1[s], B_rt2[s]], writes=[B_qT[hs]])
            else:
                kv = c - 4
                r0 = ((2 * g) % KRING)
                P.op("pool", (lambda e, s=s, kv=kv, r0=r0: e.tensor_tensor(kT[:, kv, r0 * 128: r0 * 128 + GW], rt1[s][:], rt2[s][:], ALU.add)),
                     reads=[B_rt1[s], B_rt2[s]], writes=[B_kT[r0], B_kT[r0 + 1]])
            yield

    def stage_B(T):
        g = T // 2
        i = T % 2
        hs = g % 2
        ts = T % 2
        b = T // TPS
        tp = T % TPS
        xsl = T % 4
        rk = T % KRING
        rkp = (T - 1) % KRING
        tc = slice(i * 128, (i + 1) * 128)

        def tm_proj(c0, n, eng_evac_fn):
            bk = bank("proj")

            def f(e):
                return [e.matmul(pb[bk][:, 0:n], hT[hs][:, k, tc], win[:, k, c0:c0 + n], start=(k == 0), stop=(k == 7)) for k in range(8)]
            P.op("pe", f, reads=[B_hT[hs], B_w1], writes=[B_pb[bk]])
            eng_evac_fn(bk)

        tm_proj(640, 128, lambda bk: P.op("dve", (lambda e: e.tensor_copy(vaug[:, rk, :, 0:64], pb[bk][:, 0:128].rearrange("p (a d) -> p a d", a=2))),
                                          reads=[B_pb[bk]], writes=[B_va[rk]]))
        state["av"] = T
        tm_proj(1792, 512, lambda bk: P.op("act", (lambda e: e.copy(vt[ts][:], pb[bk][:])), reads=[B_pb[bk]], writes=[B_vt[ts]]))

        def ev_g(bk):
            P.op("act", (lambda e: e.activation(out=gg[ts][:], in_=pb[bk][:], func=AF.Silu)), reads=[B_pb[bk]], writes=[B_gg[ts]])
            P.op("pool", (lambda e: e.tensor_tensor(gg[ts][:], gg[ts][:], hgw[:], ALU.mult)), reads=[B_gg[ts], B_const], writes=[B_gg[ts]])
        tm_proj(2304, 512, ev_g)
        yield
        if BSTOP <= 1:
            return

        bkA = bank("o")

        def f_At(e):
            return [e.matmul(pb[bkA][:, h * 128:(h + 1) * 128], kd[hs][:, h, tc], qd[hs][:, h, tc], start=True, stop=True) for h in range(4)]
        P.op("pe", f_At, reads=[B_kd[hs], B_qd[hs]], writes=[B_pb[bkA]])
        P.op("dve", (lambda e: e.tensor_tensor(Am[ts][:], pb[bkA][:].rearrange("p (h t) -> p h t", h=4),
                                               maskT_f[:].unsqueeze(1).broadcast_to([128, 4, 128]), ALU.mult)),
             reads=[B_pb[bkA], B_const], writes=[B_Am[ts]])

        def f_kt(e):
            return [e.transpose(ptr[:, h * 128:(h + 1) * 128], kd[hs][:, h, tc], ident_bf[:]) for h in range(4)]
        P.op("pe", f_kt, reads=[B_kd[hs], B_const], writes=[B_ptr])
        P.op("act", (lambda e: e.copy(kdtok[ts][:].rearrange("p h d -> p (h d)"), ptr[:, 0:512])), reads=[B_ptr], writes=[B_kdtok[ts]])
        yield
        if BSTOP <= 2:
            return

        while tp > 0 and state["av"] < T - 1:
            yield
        def do_half(half):
            kv = half
            kbs = (0, 1) if tp > 0 else (1,)

            def f_sc(e):
                ins = []
                for hh in range(4):
                    head = 4 * half + hh
                    c = head // 2
                    pr = slice((head % 2) * 64, (head % 2) * 64 + 64)
                    for kb in kbs:
                        rr = rkp if kb == 0 else rk
                        ins.append(e.matmul(pb[2 + hh % 2][:, ((hh // 2) * 2 + kb) * 128:((hh // 2) * 2 + kb + 1) * 128],
                                            kT[pr, kv, rr * 128:(rr + 1) * 128], qT[hs][pr, c, tc], start=True, stop=True))
                return ins
            rd = [B_qT[hs], B_kT[rk]] + ([B_kT[rkp]] if tp > 0 else [])
            P.op("pe", f_sc, reads=rd, writes=[B_pb[2], B_pb[3]])
            if BSTOP <= 2.1:
                return
            for bb in range(2):
                if tp > 0:
                    P.op("act", (lambda e, bb=bb: e.activation(out=Pe[ts][:, bb::2, :, :],
                                                               in_=pb[2 + bb][:].rearrange("p (a b q) -> p a b q", a=2, b=2), func=AF.Exp, scale=0.125)),
                         reads=[B_pb[2 + bb]], writes=[B_Pe[ts]])
                else:
                    P.op("act", (lambda e, bb=bb: e.activation(out=Pe[ts][:, bb::2, 1, :],
                                                               in_=pb[2 + bb][:].rearrange("p (a b q) -> p a b q", a=2, b=2)[:, :, 1, :],
                                                               func=AF.Exp, scale=0.125)),
                         reads=[B_pb[2 + bb]], writes=[B_Pe[ts]])
            if BSTOP <= 2.2:
                return
            if tp > 0:
                P.op("pool", (lambda e: e.tensor_tensor(Pm[ts][:], Pe[ts][:], mask2_bf[:].unsqueeze(1).broadcast_to([128, 4, 2, 128]), ALU.mult)),
                     reads=[B_Pe[ts], B_const], writes=[B_Pm[ts]])
            else:
                P.op("pool", (lambda e: e.tensor_tensor(Pm[ts][:, :, 1, :], Pe[ts][:, :, 1, :],
                                                        mask2_bf[:, 1, :].unsqueeze(1).broadcast_to([128, 4, 128]), ALU.mult)),
                     reads=[B_Pe[ts], B_const], writes=[B_Pm[ts]])
            if BSTOP <= 2.3:
                return
            bkO = bank("o")

            def f_pv(e):
                ins = []
                for hh in range(4):
                    for n, kb in enumerate(kbs):
                        rr = rkp if kb == 0 else rk
                        ins.append(e.matmul(pb[bkO][:, hh * 128:hh * 128 + 72], Pm[ts][:, hh, kb, :], vaug[:, rr, kv, :],
                                            start=(n == 0), stop=(n == len(kbs) - 1)))
                return ins
            rd = [B_Pm[ts], B_va[rk]] + ([B_va[rkp]] if tp > 0 else [])
            P.op("pe", f_pv, reads=rd, writes=[B_pb[bkO]])
            if BSTOP <= 2.4:
                return
            pO = pb[bkO][:].rearrange("p (h d) -> p h d", h=4)
            P.op("dve", (lambda e, pO=pO: e.tensor_tensor(den[:, T, half, :], pO[:, :, 64], esink[:, 4 * half:4 * half + 4], ALU.add)),
                 reads=[B_pb[bkO], B_const], writes=[B_stat[T]])
            P.op("dve", (lambda e: e.reciprocal(rden[:, T, half, :], den[:, T, half, :])), reads=[B_stat[T]], writes=[B_stat[T]])
            P.op("dve", (lambda e, pO=pO: e.tensor_tensor(attn[ts][:, half * 256:(half + 1) * 256].rearrange("p (h d) -> p h d", h=4),
                                                          pO[:, :, 0:64], rden[:, T, half, :].unsqueeze(2).broadcast_to([128, 4, 64]), ALU.mult)),
                 reads=[B_pb[bkO], B_stat[T]], writes=[B_attn[ts]])
        for half in range(2):
            do_half(half)
            yield
        if BSTOP <= 3:
            return
        P.op("act", (lambda e: e.activation(out=junk[:, 0:512], in_=attn[ts][:], func=AF.Square, accum_out=ssa[:, T:T + 1])),
             reads=[B_attn[ts]], writes=[B_stat[T]])
        rstd_ops(ssa[:, T:T + 1], rsa[:, T, 0:1], rsa[:, T, 1:2], 512, [B_stat[T]], [B_stat[T]])
        P.op("dve", (lambda e: e.scalar_tensor_tensor(cat[ts][:, 0:512], attn[ts][:], rsa[:, T, 1:2], aow[:], ALU.mult, ALU.mult)),
             reads=[B_attn[ts], B_stat[T], B_const], writes=[B_cat[ts]])
        yield
        if BSTOP <= 4:
            return

        while state["s"] < T - 1:
            yield
        if tp == 0:
            P.op("pool", lambda e: e.memset(Sst[:], 0.0), writes=[B_S])
            P.op("pool", lambda e: e.memset(Sb[:], 0.0), writes=[B_Sb])
        bko = bank("o")

        def f_o(e):
            ins = []
            for h in range(4):
                ins.append(e.matmul(pb[bko][:, h * 128:(h + 1) * 128], Am[ts][:, h, :], vt[ts][:, h * 128:(h + 1) * 128], start=True, stop=False))
                ins.append(e.matmul(pb[bko][:, h * 128:(h + 1) * 128], qd[hs][:, h, tc], Sb[:, h, :], start=False, stop=True))
            return ins
        P.op("pe", f_o, reads=[B_Am[ts], B_vt[ts], B_qd[hs], B_Sb], writes=[B_pb[bko]])
        bkK = bank("o")

        def f_kv(e):
            return [e.matmul(pb[bkK][:, h * 128:(h + 1) * 128], kdtok[ts][:, h, :], vt[ts][:, h * 128:(h + 1) * 128], start=True, stop=True)
                    for h in range(4)]
        P.op("pe", f_kv, reads=[B_kdtok[ts], B_vt[ts]], writes=[B_pb[bkK]])
        decb = dec[hs][:, :, i:i + 1].broadcast_to([128, 4, 128])
        P.op("dve", (lambda e: e.tensor_tensor(tmpS[ts][:], pb[bkK][:].rearrange("p (h d) -> p h d", h=4), Sst[:], ALU.add)),
             reads=[B_pb[bkK], B_S], writes=[B_tmpS[ts]])
        P.op("dve", (lambda e: e.tensor_tensor(Sb[:], tmpS[ts][:], decb, ALU.mult)), reads=[B_tmpS[ts], B_dec[hs]], writes=[B_Sb])
        P.op("pool", (lambda e: e.tensor_tensor(Sst[:], tmpS[ts][:], decb, ALU.mult)), reads=[B_tmpS[ts], B_dec[hs]], writes=[B_S])
        state["s"] = T
        P.op("act", (lambda e: e.activation(out=sqo[ts][:].rearrange("p h d -> p (h d)"), in_=pb[bko][:], func=AF.Square)),
             reads=[B_pb[bko]], writes=[B_sqo[ts]])
        P.op("dve", (lambda e: e.tensor_reduce(ssr[:, T, :], sqo[ts][:], AX.X, ALU.add)), reads=[B_sqo[ts]], writes=[B_stat[T]])
        rstd_ops(ssr[:, T, :], rsr[:, T, 0:4], rsr[:, T, 4:8], 128, [B_stat[T]], [B_stat[T]])
        P.op("dve", (lambda e: e.tensor_tensor(rect[ts][:], pb[bko][:].rearrange("p (h d) -> p h d", h=4),
                                               rsr[:, T, 4:8].unsqueeze(2).broadcast_to([128, 4, 128]), ALU.mult)),
             reads=[B_pb[bko], B_stat[T]], writes=[B_rect[ts]])
        P.op("pool", (lambda e: e.tensor_tensor(cat[ts][:, 512:1024], rect[ts][:].rearrange("p h d -> p (h d)"), gg[ts][:], ALU.mult)),
             reads=[B_rect[ts], B_gg[ts]], writes=[B_cat[ts]])
        yield
        if BSTOP <= 5:
            return

        def f_ct(e):
            return [e.transpose(ptr[:, j * 128:(j + 1) * 128], cat[ts][:, j * 128:(j + 1) * 128], ident_bf[:]) for j in range(8)]
        P.op("pe", f_ct, reads=[B_cat[ts], B_const], writes=[B_ptr])
        P.op("act", (lambda e: e.copy(catT[ts][:].rearrange("p j t -> p (j t)"), ptr[:, :])), reads=[B_ptr], writes=[B_catT[ts]])

        def f_mix(e):
            ins = []
            for nh in range(2):
                for k in range(8):
                    ins.append(e.matmul(pb[2 + nh][:], catT[ts][:, k, :], wout[:, k, nh * 512:(nh + 1) * 512], start=(k == 0), stop=(k == 7)))
            return ins
        P.op("pe", f_mix, reads=[B_catT[ts], B_w1], writes=[B_pb[2], B_pb[3]])
        for nh in range(2):
            P.op("act", (lambda e, nh=nh: e.activation(out=junk[:, 0:512], in_=pb[2 + nh][:], func=AF.Square, accum_out=ssm[:, T, nh:nh + 1])),
                 reads=[B_pb[2 + nh]], writes=[B_stat[T]])
        P.op("dve", (lambda e: e.tensor_tensor(ssm[:, T, 2:3], ssm[:, T, 0:1], ssm[:, T, 1:2], ALU.add)), reads=[B_stat[T]], writes=[B_stat[T]])
        rstd_ops(ssm[:, T, 2:3], ssm[:, T, 3:4], ssm[:, T, 0:1], D, [B_stat[T]], [B_stat[T]])
        for nh in range(2):
            P.op("dve", (lambda e, nh=nh: e.scalar_tensor_tensor(tmpx[ts][:, nh * 512:(nh + 1) * 512], pb[2 + nh][:], ssm[:, T, 0:1],
                                                                 gv1[b][:, nh * 512:(nh + 1) * 512], ALU.mult, ALU.mult)),
                 reads=[B_pb[2 + nh], B_stat[T], B_mod], writes=[B_tmpx[ts]])
        P.op("pool", (lambda e: e.tensor_tensor(xs[xsl][:], tmpx[ts][:], xs[xsl][:], ALU.add)), reads=[B_tmpx[ts], B_xs[xsl]], writes=[B_xs[xsl]])
        P.op("sp", (lambda e: e.dma_start(out=x1_d[T * 128:(T + 1) * 128, :], in_=xs[xsl][:])), reads=[B_xs[xsl]], writes=[B_x1d[T]], ndma=1)
        yield

    if stop >= 2:
        for _ in stage_A(0):
            pass
    for g in range(ngroups if stop >= 3 else 0):
        gens = [stage_B(2 * g), stage_B(2 * g + 1)]
        if g + 1 < ngroups:
            gens.append(stage_A(g + 1))
        run_interleaved(gens)

    P.barrier()
    p1.close()
    pp.close()

    p2 = ExitStack()
    wup = P.sb("wup", [128, 8, DFF], BF16, p2)
    wdn = P.sb("wdn", [128, 32, D], BF16, p2)
    rl = [P.sb(f"rl{i}", [128, 512], BF16, p2) for i in range(2)]
    uT = P.sb("uT", [128, 32, GW], BF16, p2)
    B_wup = P.buf("wup")
    B_wdn = P.buf("wdn")
    B_rl = P.bufs(2, "rl")
    B_uT = P.bufs(16, "uT")

    wup_v = wup_d.rearrange("(j p) c -> p j c", p=128)
    wdn_v = wdn_d.rearrange("(j p) c -> p j c", p=128)
    for j in range(8 if stop >= 4 else 0):
        P.op("pool", (lambda e, j=j: e.dma_start(out=wup[:, j, :], in_=wup_v[:, j, :])), writes=[B_wup], ndma=1)
    for j4 in range(8 if stop >= 4 else 0):
        P.op("pool", (lambda e, j4=j4: e.dma_start(out=wdn[:, 4 * j4:4 * j4 + 4, :], in_=wdn_v[:, 4 * j4:4 * j4 + 4, :])), writes=[B_wdn], ndma=1)

    up_banks = [4, 5, 6]
    upc = [0]
    down_units = [(0, 1), (2, 3)]

    def mlp_group(g):
        hs = g % 2
        for i in range(2):
            T = 2 * g + i
            xsl = T % 4
            P.op("sp", (lambda e, T=T, xsl=xsl: e.dma_start(out=xs[xsl][:], in_=x1_d[T * 128:(T + 1) * 128, :])),
                 reads=[B_x1d[T]], writes=[B_xs[xsl]], ndma=1)
        yield from norm_and_transpose(g, a2, 24, True)
        for cp in range(16):
            bk = up_banks[upc[0] % 3]
            upc[0] += 1
            s = cp % 2

            def f_up(e, cp=cp, bk=bk):
                ins = []
                for cc in range(2):
                    c = 2 * cp + cc
                    for k in range(8):
                        ins.append(e.matmul(pb[bk][:, cc * GW:(cc + 1) * GW], wup[:, k, c * 128:(c + 1) * 128], hT[hs][:, k, :],
                                            start=(k == 0), stop=(k == 7)))
                return ins
            P.op("pe", f_up, reads=[B_hT[hs], B_wup], writes=[B_pb[bk]])
            P.op("act", (lambda e, bk=bk, s=s: e.activation(out=rl[s][:], in_=pb[bk][:], func=AF.Relu)), reads=[B_pb[bk]], writes=[B_rl[s]])
            P.op("pool", (lambda e, s=s, cp=cp: e.tensor_tensor(uT[:, 2 * cp:2 * cp + 2, :].rearrange("p a t -> p (a t)"), rl[s][:], rl[s][:], ALU.mult)),
                 reads=[B_rl[s]], writes=[B_uT[cp]])
            if cp % 4 == 3:
                yield
        for i in range(2):
            T = 2 * g + i
            xsl = T % 4
            b = T // TPS
            ts = T % 2
            du = down_units[i]

            def f_dn(e, i=i, du=du):
                ins = []
                for nh in range(2):
                    for c in range(32):
                        ins.append(e.matmul(pb[du[nh]][:], uT[:, c, i * 128:(i + 1) * 128], wdn[:, c, nh * 512:(nh + 1) * 512],
                                            start=(c == 0), stop=(c == 31)))
                return ins
            P.op("pe", f_dn, reads=B_uT + [B_wdn], writes=[B_pb[du[0]], B_pb[du[1]]])
            for nh in range(2):
                P.op("act", (lambda e, nh=nh, du=du, T=T: e.activation(out=junk[:, 0:512], in_=pb[du[nh]][:], func=AF.Square, accum_out=ssy[:, T, nh:nh + 1])),
                     reads=[B_pb[du[nh]]], writes=[B_stat[T]])
            P.op("dve", (lambda e, T=T: e.tensor_tensor(ssy[:, T, 2:3], ssy[:, T, 0:1], ssy[:, T, 1:2], ALU.add)), reads=[B_stat[T]], writes=[B_stat[T]])
            rstd_ops(ssy[:, T, 2:3], ssy[:, T, 3:4], ssy[:, T, 0:1], D, [B_stat[T]], [B_stat[T]])
            for nh in range(2):
                P.op("dve", (lambda e, nh=nh, du=du, T=T, ts=ts, b=b: e.scalar_tensor_tensor(
                    tmpx[ts][:, nh * 512:(nh + 1) * 512], pb[du[nh]][:], ssy[:, T, 0:1], gv2[b][:, nh * 512:(nh + 1) * 512], ALU.mult, ALU.mult)),
                    reads=[B_pb[du[nh]], B_stat[T], B_mod], writes=[B_tmpx[ts]])
            P.op("pool", (lambda e, ts=ts, xsl=xsl: e.tensor_tensor(xs[xsl][:], tmpx[ts][:], xs[xsl][:], ALU.add)),
                 reads=[B_tmpx[ts], B_xs[xsl]], writes=[B_xs[xsl]])
            P.op("sp", (lambda e, T=T, xsl=xsl: e.dma_start(out=out_d[T * 128:(T + 1) * 128, :], in_=xs[xsl][:])),
                 reads=[B_xs[xsl]], writes=[B_out[T]], ndma=1)
            yield

    for g in range(ngroups if stop >= 5 else 0):
        for _ in mlp_group(g):
            pass

    P.op("sp", None, reads=B_out)
    p2.close()
    P.emit()
    return nc


_NC_CACHE = {}


def _layout_inputs(inp):
    f = lambda a: np.ascontiguousarray(np.asarray(a, dtype=np.float32))
    consts = _host_consts()
    colT = lambda v, n: f(np.asarray(v).reshape(n, 128).T)
    shared = {
        "w_ada": f(inp["w_ada"][0]),
        "b_adaT": colT(inp["b_ada"][0], 48),
        "pre_mixT": colT(inp["pre_w_mix"][0], 8),
        "pre_mlpT": colT(inp["pre_w_mlp"][0], 8),
        "post_mixT": colT(inp["post_w_mix"][0], 8),
        "post_mlpT": colT(inp["post_w_mlp"][0], 8),
        "w_in": f(inp["w_in"][0]),
        "w_out": f(inp["w_out"][0]),
        "w_up": f(inp["w_up"][0]),
        "w_down": f(inp["w_down"][0]),
        "sinks_rep": f(np.broadcast_to(np.asarray(inp["attn_sinks"][0])[None, :], (128, 8))),
        "aow_rep": f(np.broadcast_to(np.asarray(inp["attn_out_w"][0])[None, :], (128, 512))),
        "hgw_rep": f(np.broadcast_to(np.tile(np.asarray(inp["hg_norm_w"][0]), 4)[None, :], (128, 512))),
        "lbT": f(np.asarray(inp["lb_table"]).reshape(2, 4, 128).transpose(2, 0, 1)),
    }
    shared.update(consts)
    x = np.asarray(inp["x"], dtype=np.float32)
    c = np.asarray(inp["c"], dtype=np.float32)
    maps = []
    for i in range(NCORES):
        m = dict(shared)
        m["x"] = f(x[NSEQ * i:NSEQ * (i + 1)].reshape(TOK, D))
        cc = c[NSEQ * i:NSEQ * (i + 1)]
        m["cT"] = f(cc.reshape(2, 8, 128).transpose(2, 1, 0))
        maps.append(m)
    return maps


def kernel(**inputs):
    if "nc" not in _NC_CACHE:
        _NC_CACHE["nc"] = build_program()
    nc = _NC_CACHE["nc"]
    maps = _layout_inputs(inputs)
    res = run_bass_kernel_spmd(nc, maps, core_ids=list(range(NCORES)))
    outs = [np.asarray(r["out"], dtype=np.float32).reshape(NSEQ, SEQ, D) for r in res.results]
    return np.concatenate(outs, axis=0)
```

```python
import numpy as np
from contextlib import ExitStack
import concourse.bass as bass
import concourse.mybir as mybir
from concourse.bass_utils import run_bass_kernel_spmd

F32 = mybir.dt.float32
BF16 = mybir.dt.bfloat16
AF = mybir.ActivationFunctionType
ALU = mybir.AluOpType
AX = mybir.AxisListType

NCORES = 8
D = 1024
SEQ = 2048
NSEQ = 2
TOK = NSEQ * SEQ
NT = TOK // 128
TPS = SEQ // 128
NG = NT // 2
GW = 256
DFF = 4096
INC = 2816
EPS = 1e-6
KRING = 6
import os as _os
BSTOP = float(_os.environ.get('KBSTOP', '99'))


class Buf:
    __slots__ = ("name", "last_w", "readers")

    def __init__(self, name):
        self.name = name
        self.last_w = None
        self.readers = []


class Op:
    __slots__ = ("eng", "fn", "deps", "idx", "ndma", "signal", "semval", "sem", "name", "prevval")


class Prog:
    ENGS = ("pe", "act", "dve", "pool", "sp")
    NDMASEM = 24

    def __init__(self, nc):
        self.nc = nc
        self.streams = {e: [] for e in self.ENGS}
        self.es = ExitStack()
        self.nbuf = 0
        self.dma_since_barrier = []

    def sb(self, name, shape, dt, stack=None):
        return (stack or self.es).enter_context(self.nc.sbuf_tensor("s_" + name, list(shape), dt))

    def ps(self, name, shape, dt):
        return self.es.enter_context(self.nc.psum_tensor("p_" + name, list(shape), dt))

    def buf(self, name=None):
        self.nbuf += 1
        return Buf(name or f"b{self.nbuf}")

    def bufs(self, n, name="b"):
        return [self.buf(f"{name}{i}") for i in range(n)]

    def op(self, eng, fn, reads=(), writes=(), ndma=0, name=None):
        o = Op()
        o.eng = eng
        o.fn = fn
        o.ndma = ndma
        o.signal = False
        o.semval = None
        o.sem = None
        o.name = name
        o.prevval = 0
        deps = []
        for b in reads:
            if b.last_w is not None:
                deps.append(b.last_w)
        for b in writes:
            if b.last_w is not None:
                deps.append(b.last_w)
            deps.extend(b.readers)
        seen = set()
        dd = []
        for d in deps:
            if id(d) in seen:
                continue
            seen.add(id(d))
            if eng == "pe" and d.eng == "pe" and d.ndma == 0:
                continue
            dd.append(d)
        o.deps = dd
        o.idx = len(self.streams[eng])
        self.streams[eng].append(o)
        for b in reads:
            b.readers.append(o)
        for b in writes:
            b.last_w = o
            b.readers = []
        if ndma > 0:
            self.dma_since_barrier.append(o)
        return o

    def barrier(self):
        lasts = {}
        for e in self.ENGS:
            for o in reversed(self.streams[e]):
                if o.ndma == 0 and o.fn is not None:
                    lasts[e] = o
                    break
        dmas = list(self.dma_since_barrier)
        self.dma_since_barrier = []
        for e in self.ENGS:
            o = Op()
            o.eng = e
            o.fn = None
            o.ndma = 0
            o.signal = False
            o.semval = None
            o.sem = None
            o.name = "barrier"
            o.prevval = 0
            o.deps = [lasts[x] for x in lasts if x != e] + dmas
            o.idx = len(self.streams[e])
            self.streams[e].append(o)

    def emit(self):
        nc = self.nc
        for e in self.ENGS:
            for o in self.streams[e]:
                for d in o.deps:
                    d.signal = True
        for e in self.ENGS:
            cnt = 0
            for o in self.streams[e]:
                if o.ndma == 0 and o.signal:
                    cnt += 1
                    o.semval = cnt
        dcount = [0] * self.NDMASEM
        dpool = {"sp": list(range(0, 16)), "pool": list(range(16, 24))}
        for e in self.ENGS:
            rr = 0
            for o in self.streams[e]:
                if o.ndma > 0:
                    s = dpool[e][rr % len(dpool[e])]
                    rr += 1
                    o.sem = s
                    o.prevval = dcount[s]
                    dcount[s] += 16 * o.ndma
                    o.semval = dcount[s]
        es = self.es
        esem = {e: es.enter_context(nc.semaphore(f"tl_{e}")) for e in self.ENGS}
        dsem = [es.enter_context(nc.semaphore(f"dma_{i}")) for i in range(self.NDMASEM)]
        block = es.enter_context(nc.Block())
        streams = self.streams

        def run_stream(e, engobj):
            seen = {}
            for o in streams[e]:
                waits = []
                for d in o.deps:
                    if d.ndma > 0:
                        key = ("d", d.sem)
                        sem = dsem[d.sem]
                    else:
                        key = ("e", d.eng)
                        sem = esem[d.eng]
                    v = d.semval
                    if seen.get(key, 0) >= v:
                        continue
                    waits.append((key, sem, v))
                if o.ndma > 0:
                    key = ("d", o.sem)
                    if o.prevval > 0 and seen.get(key, 0) < o.prevval:
                        waits.append((key, dsem[o.sem], o.prevval))
                best = {}
                for key, sem, v in waits:
                    if key not in best or best[key][1] < v:
                        best[key] = (sem, v)
                wl = list(best.items())
                for key, (sem, v) in wl:
                    seen[key] = v
                if o.ndma > 0:
                    for key, (sem, v) in wl:
                        engobj.wait_ge(sem, v)
                    ins = o.fn(engobj)
                    if not isinstance(ins, (list, tuple)):
                        ins = [ins]
                    assert len(ins) == o.ndma, (o.name, len(ins), o.ndma)
                    for i in ins:
                        i.then_inc(dsem[o.sem], 16)
                elif o.fn is None:
                    for key, (sem, v) in wl:
                        engobj.wait_ge(sem, v)
                else:
                    for key, (sem, v) in wl[1:]:
                        engobj.wait_ge(sem, v)
                    ins = o.fn(engobj)
                    if not isinstance(ins, (list, tuple)):
                        ins = [ins]
                    if wl:
                        ins[0]._wait_ge(wl[0][1][0], wl[0][1][1])
                    if o.signal:
                        ins[-1].then_inc(esem[e], 1)

        @block.tensor
        def _(eng):
            run_stream("pe", eng)

        @block.scalar
        def _(eng):
            run_stream("act", eng)

        @block.vector
        def _(eng):
            run_stream("dve", eng)

        @block.gpsimd
        def _(eng):
            run_stream("pool", eng)

        @block.sync
        def _(eng):
            run_stream("sp", eng)

        es.close()


def run_interleaved(gens):
    gens = list(gens)
    while gens:
        for g in list(gens):
            try:
                next(g)
            except StopIteration:
                gens.remove(g)


def _host_consts():
    c = {}
    c["ident"] = np.eye(128, dtype=np.float32)
    c["ones"] = np.ones((128, 128), dtype=np.float32)
    k = np.arange(128)[:, None]
    q = np.arange(128)[None, :]
    cur = (k <= q).astype(np.float32)
    prev = (k > q).astype(np.float32)
    c["mask2"] = np.ascontiguousarray(np.stack([prev, cur], axis=1).reshape(128, 256))
    c["maskT"] = np.ascontiguousarray(cur)
    rm = np.ones((128, GW), dtype=np.float32)
    rm[:, 0::128] = 0.0
    c["rmask"] = rm
    ps = np.zeros((128, 128), dtype=np.float32)
    for m in range(128):
        d = m % 64
        base = m - d
        if d < 8:
            s = base + d + 8
        elif d < 16:
            s = base + d - 8
        else:
            s = m
        ps[s, m] = 1.0
    c["pswap"] = ps
    inv_freq = (np.float32(500000.0) ** (-np.arange(0, 16, 2, dtype=np.float32) / np.float32(16))).astype(np.float32)
    ang = (np.arange(SEQ, dtype=np.float32)[:, None] * inv_freq[None, :]).astype(np.float32)
    cs = np.cos(ang).astype(np.float32)
    sn = np.sin(ang).astype(np.float32)
    cosF = np.ones((128, SEQ), dtype=np.float32)
    sinF = np.zeros((128, SEQ), dtype=np.float32)
    for p in range(128):
        d = p % 64
        if d < 8:
            cosF[p] = cs[:, d]
            sinF[p] = -sn[:, d]
        elif d < 16:
            cosF[p] = cs[:, d - 8]
            sinF[p] = sn[:, d - 8]
    c["cosF"] = cosF
    c["sinF"] = sinF
    return c


def build_program(stop=99, ngroups=NG):
    nc = bass.Bass("TRN2", target_bir_lowering=False)
    P = Prog(nc)

    def din(name, shape):
        return nc.dram_tensor(name, list(shape), F32, kind="ExternalInput").ap()

    x_d = din("x", [TOK, D])
    cT_d = din("cT", [128, 8, 2])
    wada_d = din("w_ada", [D, 6 * D])
    bada_d = din("b_adaT", [128, 48])
    premix_d = din("pre_mixT", [128, 8])
    premlp_d = din("pre_mlpT", [128, 8])
    postmix_d = din("post_mixT", [128, 8])
    postmlp_d = din("post_mlpT", [128, 8])
    win_d = din("w_in", [D, INC])
    wout_d = din("w_out", [D, D])
    wup_d = din("w_up", [D, DFF])
    wdn_d = din("w_down", [DFF, D])
    sinks_d = din("sinks_rep", [128, 8])
    aow_d = din("aow_rep", [128, 512])
    hgw_d = din("hgw_rep", [128, 512])
    lbT_d = din("lbT", [128, 2, 4])
    ident_d = din("ident", [128, 128])
    ones_d = din("ones", [128, 128])
    mask2_d = din("mask2", [128, 256])
    maskT_d = din("maskT", [128, 128])
    rmask_d = din("rmask", [128, GW])
    pswap_d = din("pswap", [128, 128])
    cosF_d = din("cosF", [128, SEQ])
    sinF_d = din("sinF", [128, SEQ])
    out_d = nc.dram_tensor("out", [TOK, D], F32, kind="ExternalOutput").ap()
    x1_d = nc.dram_tensor("x1_scratch", [TOK, D], F32).ap()

    ident_bf = P.sb("ident_bf", [128, 128], BF16)
    ident_f = P.sb("ident_f", [128, 128], F32)
    ones_f = P.sb("ones_f", [128, 128], F32)
    pswap_bf = P.sb("pswap_bf", [128, 128], BF16)
    mask2_bf = P.sb("mask2_bf", [128, 2, 128], BF16)
    maskT_f = P.sb("maskT_f", [128, 128], F32)
    rmask = P.sb("rmask", [128, GW], F32)
    epsc = P.sb("epsc", [128, 1], F32)
    esink = P.sb("esink", [128, 8], F32)
    modT = P.sb("modT", [128, 48, 2], F32)
    a1 = P.sb("a1", [128, 8, 2], F32)
    a2 = P.sb("a2", [128, 8, 2], F32)
    gw1 = P.sb("gw1", [128, 8, 2], F32)
    gw2 = P.sb("gw2", [128, 8, 2], F32)
    lbv = P.sb("lbv", [128, 4], F32)
    oml = P.sb("oml", [128, 4], F32)
    lbm1 = P.sb("lbm1", [128, 4], F32)
    gv2 = [P.sb(f"gv2_{b}", [128, D], F32) for b in range(2)]
    ss1 = P.sb("ss1", [128, NT], F32)
    rs1 = P.sb("rs1", [128, NT, 2], F32)
    ssa = P.sb("ssa", [128, NT], F32)
    rsa = P.sb("rsa", [128, NT, 2], F32)
    ssm = P.sb("ssm", [128, NT, 4], F32)
    ssr = P.sb("ssr", [128, NT, 4], F32)
    rsr = P.sb("rsr", [128, NT, 8], F32)
    den = P.sb("den", [128, NT, 2, 4], F32)
    rden = P.sb("rden", [128, NT, 2, 4], F32)
    ss2 = P.sb("ss2", [128, NT], F32)
    rs2 = P.sb("rs2", [128, NT, 2], F32)
    ssy = P.sb("ssy", [128, NT, 4], F32)
    xs = [P.sb(f"xs{i}", [128, D], F32) for i in range(4)]
    xn = [P.sb(f"xn{i}", [128, D], BF16) for i in range(2)]
    junk = P.sb("junk", [128, D], BF16)
    hT = [P.sb(f"hT{i}", [128, 8, GW], BF16) for i in range(2)]
    tmpx = [P.sb(f"tmpx{i}", [128, D], F32) for i in range(2)]

    pp = ExitStack()
    aow = P.sb("aow", [128, 512], F32, pp)
    hgw = P.sb("hgw", [128, 512], F32, pp)
    gv1 = [P.sb(f"gv1_{b}", [128, D], F32, pp) for b in range(2)]

    pb = [P.ps(f"pb{i}", [128, 512], F32) for i in range(7)]
    ptr = P.ps("ptr", [128, 1024], BF16)
    B_pb = P.bufs(7, "pb")
    B_ptr = P.buf("ptr")

    B_const = P.buf("const")
    B_mod = P.buf("mod")
    B_xs = P.bufs(4, "xs")
    B_xn = P.bufs(2, "xn")
    B_junk = P.buf("junk")
    B_hT = P.bufs(2, "hT")
    B_tmpx = P.bufs(2, "tmpx")
    B_stat = [P.buf(f"stat{t}") for t in range(NT)]
    B_x1d = [P.buf(f"x1d{t}") for t in range(NT)]
    B_out = [P.buf(f"outd{t}") for t in range(NT)]

    def ld(eng, dst, src, b=B_const, n=1):
        P.op(eng, lambda e: e.dma_start(out=dst, in_=src), writes=[b], ndma=n)

    ld("pool", ident_bf[:], ident_d[:, :])
    ld("sp", ident_f[:], ident_d[:, :])
    ld("sp", ones_f[:], ones_d[:, :])
    ld("pool", pswap_bf[:], pswap_d[:, :])
    ld("pool", mask2_bf[:].rearrange("p a b -> p (a b)"), mask2_d[:, :])
    ld("sp", maskT_f[:], maskT_d[:, :])
    ld("sp", rmask[:], rmask_d[:, :])
    ld("sp", esink[:], sinks_d[:, :])
    ld("sp", aow[:], aow_d[:, :])
    ld("sp", hgw[:], hgw_d[:, :])
    P.op("pool", lambda e: e.memset(epsc[:], EPS), writes=[B_const])
    P.op("act", lambda e: e.activation(out=esink[:], in_=esink[:], func=AF.Exp), reads=[B_const], writes=[B_const])

    st = ExitStack()
    cTs = P.sb("cTs", [128, 8, 2], F32, st)
    ca = P.sb("ca", [128, 8, 2], F32, st)
    badaT = P.sb("badaT", [128, 48], F32, st)
    premix = P.sb("premix", [128, 8], F32, st)
    premlp = P.sb("premlp", [128, 8], F32, st)
    postmix = P.sb("postmix", [128, 8], F32, st)
    postmlp = P.sb("postmlp", [128, 8], F32, st)
    lbT = P.sb("lbT", [128, 2, 4], F32, st)
    lbd = P.sb("lbd", [128, 4], F32, st)
    wa = [P.sb(f"wa{i}", [128, 8, 512], F32, st) for i in range(2)]
    dg = [P.sb(f"dg{i}", [128, 128], F32, st) for i in range(2)]
    B_wa = P.bufs(2, "wa")
    B_dg = P.bufs(2, "dg")
    B_set = P.buf("setup")

    ld("sp", cTs[:], cT_d[:, :, :], B_set)
    ld("sp", badaT[:], bada_d[:, :], B_set)
    ld("sp", premix[:], premix_d[:, :], B_set)
    ld("sp", premlp[:], premlp_d[:, :], B_set)
    ld("sp", postmix[:], postmix_d[:, :], B_set)
    ld("sp", postmlp[:], postmlp_d[:, :], B_set)
    ld("sp", lbT[:], lbT_d[:, :, :], B_set)
    P.op("act", lambda e: e.activation(out=ca[:], in_=cTs[:], func=AF.Silu), reads=[B_set], writes=[B_set])
    P.op("dve", lambda e: e.tensor_tensor(lbd[:], lbT[:, 1, :], lbT[:, 0, :], ALU.subtract), reads=[B_set], writes=[B_set])
    P.op("act", lambda e: e.activation(out=lbv[:], in_=lbd[:], func=AF.Sigmoid), reads=[B_set], writes=[B_mod])
    P.op("dve", lambda e: e.tensor_scalar(oml[:], lbv[:], -1.0, 1.0, ALU.mult, ALU.add), reads=[B_mod], writes=[B_mod])
    P.op("dve", lambda e: e.tensor_scalar(lbm1[:], lbv[:], 1.0, -1.0, ALU.mult, ALU.add), reads=[B_mod], writes=[B_mod])

    wada_v = wada_d.rearrange("(j p) c -> p j c", p=128)
    pmod = pb[0][:, 0:96]
    for blk in range(12):
        s = blk % 2
        P.op("sp", (lambda e, s=s, blk=blk: e.dma_start(out=wa[s][:], in_=wada_v[:, :, blk * 512:(blk + 1) * 512])),
             writes=[B_wa[s]], ndma=1)

        def mm_mod(e, s=s, blk=blk):
            ins = []
            for mm in range(4):
                m = blk * 4 + mm
                for k in range(8):
                    ins.append(e.matmul(pmod[:, 2 * m:2 * m + 2], wa[s][:, k, mm * 128:(mm + 1) * 128], ca[:, k, :],
                                        start=(k == 0), stop=(k == 7)))
            return ins
        P.op("pe", mm_mod, reads=[B_wa[s], B_set], writes=[B_pb[0]])
    P.op("dve", lambda e: e.tensor_tensor(modT[:], pmod.rearrange("p (m b) -> p m b", b=2),
                                          badaT[:].unsqueeze(2).broadcast_to([128, 48, 2]), ALU.add),
         reads=[B_pb[0], B_set], writes=[B_mod])

    def bc2(v):
        return v[:].unsqueeze(2).broadcast_to([128, 8, 2])
    P.op("dve", lambda e: e.scalar_tensor_tensor(a1[:], modT[:, 8:16, :], 1.0, bc2(premix), ALU.add, ALU.mult),
         reads=[B_mod, B_set], writes=[B_mod])
    P.op("dve", lambda e: e.scalar_tensor_tensor(a2[:], modT[:, 32:40, :], 1.0, bc2(premlp), ALU.add, ALU.mult),
         reads=[B_mod, B_set], writes=[B_mod])
    P.op("dve", lambda e: e.tensor_tensor(gw1[:], modT[:, 16:24, :], bc2(postmix), ALU.mult), reads=[B_mod, B_set], writes=[B_mod])
    P.op("dve", lambda e: e.tensor_tensor(gw2[:], modT[:, 40:48, :], bc2(postmlp), ALU.mult), reads=[B_mod, B_set], writes=[B_mod])
    cnt = 0
    for gwt, gvt in ((gw1, gv1), (gw2, gv2)):
        for b in range(2):
            for jh in range(2):
                bank = 1 + (cnt % 2)
                for jj in range(4):
                    j = jh * 4 + jj
                    s = cnt % 2
                    cnt += 1
                    P.op("dve", (lambda e, s=s, gwt=gwt, j=j, b=b: e.tensor_scalar(dg[s][:], ident_f[:], gwt[:, j, b:b + 1], None, ALU.mult)),
                         reads=[B_mod, B_const], writes=[B_dg[s]])
                    P.op("pe", (lambda e, s=s, bank=bank, jj=jj: e.matmul(pb[bank][:, jj * 128:(jj + 1) * 128], ones_f[:], dg[s][:], start=True, stop=True)),
                         reads=[B_dg[s], B_const], writes=[B_pb[bank]])
                P.op("act", (lambda e, bank=bank, gvt=gvt, b=b, jh=jh: e.copy(gvt[b][:, jh * 512:(jh + 1) * 512], pb[bank][:])),
                     reads=[B_pb[bank]], writes=[B_mod])
    P.barrier()
    st.close()

    p1 = ExitStack()
    win = P.sb("win", [128, 8, INC], BF16, p1)
    wkd = P.sb("wkd", [128, 8, 256], BF16, p1)
    wout = P.sb("wout", [128, 8, D], BF16, p1)
    cosS = [P.sb(f"cosS{i}", [128, GW], F32, p1) for i in range(2)]
    sinS = [P.sb(f"sinS{i}", [128, GW], F32, p1) for i in range(2)]
    qraw = [P.sb(f"qraw{i}", [128, GW], BF16, p1) for i in range(2)]
    rt1 = [P.sb(f"rt1_{i}", [128, GW], F32, p1) for i in range(2)]
    rt2 = [P.sb(f"rt2_{i}", [128, GW], F32, p1) for i in range(2)]
    qT = [P.sb(f"qT{i}", [128, 4, GW], BF16, p1) for i in range(2)]
    kT = P.sb("kT", [128, 2, KRING * 128], BF16, p1)
    vaug = P.sb("vaug", [128, KRING, 2, 72], BF16, p1)
    hA = [P.sb(f"hA{i}", [128, GW], F32, p1) for i in range(2)]
    hB = [P.sb(f"hB{i}", [128, GW], F32, p1) for i in range(2)]
    hC = [P.sb(f"hC{i}", [128, GW], F32, p1) for i in range(2)]
    qd = [P.sb(f"qd{i}", [128, 4, GW], BF16, p1) for i in range(2)]
    kd = [P.sb(f"kd{i}", [128, 4, GW], BF16, p1) for i in range(2)]
    dec = [P.sb(f"dec{i}", [128, 4, 2], F32, p1) for i in range(2)]
    vt = [P.sb(f"vt{i}", [128, 512], BF16, p1) for i in range(2)]
    gg = [P.sb(f"gg{i}", [128, 512], F32, p1) for i in range(2)]
    Pe = [[P.sb(f"Pe{i}_{h}", [128, 4, 2, 128], BF16, p1) for h in range(2)] for i in range(2)]
    attn = [P.sb(f"attn{i}", [128, 512], F32, p1) for i in range(2)]
    Am = [P.sb(f"Am{i}", [128, 4, 128], BF16, p1) for i in range(2)]
    kdtok = [P.sb(f"kdtok{i}", [128, 4, 128], BF16, p1) for i in range(2)]
    tmpS = [P.sb(f"tmpS{i}", [128, 4, 128], F32, p1) for i in range(2)]
    sqo = [P.sb(f"sqo{i}", [128, 4, 128], F32, p1) for i in range(2)]
    rect = sqo
    cat = [P.sb(f"cat{i}", [128, D], BF16, p1) for i in range(2)]
    catT = [P.sb(f"catT{i}", [128, 8, 128], BF16, p1) for i in range(2)]
    Sst = P.sb("Sst", [128, 4, 128], F32, p1)
    Sb = P.sb("Sb", [128, 4, 128], BF16, p1)

    B_w1 = P.buf("w1")
    B_rope = P.bufs(2, "rope")
    B_qraw = P.bufs(2, "qraw")
    B_rt1 = P.bufs(2, "rt1")
    B_rt2 = P.bufs(2, "rt2")
    B_qT = P.bufs(2, "qT")
    B_kT = P.bufs(KRING, "kT")
    B_va = P.bufs(KRING, "va")
    B_hA = P.bufs(2, "hA")
    B_hB = P.bufs(2, "hB")
    B_hC = P.bufs(2, "hC")
    B_qd = P.bufs(2, "qd")
    B_kd = P.bufs(2, "kd")
    B_dec = P.bufs(2, "dec")
    B_vt = P.bufs(2, "vt")
    B_gg = P.bufs(2, "gg")
    B_Pe = [P.bufs(2, f"Pe{i}_") for i in range(2)]
    B_attn = P.bufs(2, "attn")
    B_Am = P.bufs(2, "Am")
    B_kdtok = P.bufs(2, "kdtok")
    B_tmpS = P.bufs(2, "tmpS")
    B_sqo = P.bufs(2, "sqo")
    B_rect = B_sqo
    B_cat = P.bufs(2, "cat")
    B_catT = P.bufs(2, "catT")
    B_S = P.buf("S")
    B_Sb = P.buf("Sb")

    win_v = win_d.rearrange("(j p) c -> p j c", p=128)
    wout_v = wout_d.rearrange("(j p) c -> p j c", p=128)
    for j in range(8):
        P.op("pool", (lambda e, j=j: e.dma_start(out=win[:, j, :], in_=win_v[:, j, :])), writes=[B_w1], ndma=1)
    for kv in range(2):
        for r in range(2):
            c0 = (2 * kv + r) * 64
            P.op("pool", (lambda e, kv=kv, c0=c0: e.dma_start(out=wkd[:, :, c0:c0 + 64], in_=win_v[:, :, 512 + 64 * kv:576 + 64 * kv])),
                 writes=[B_w1], ndma=1)
    for j in range(8):
        P.op("pool", (lambda e, j=j: e.dma_start(out=wout[:, j, :], in_=wout_v[:, j, :])), writes=[B_w1], ndma=1)
    P.op("pool", lambda e: e.memset(vaug[:], 1.0), writes=B_va)

    pools = {"proj": [0, 1], "o": [4, 5, 6]}
    pcnt = {"proj": 0, "o": 0}

    def bank(pool):
        i = pools[pool][pcnt[pool] % len(pools[pool])]
        pcnt[pool] += 1
        return i

    state = {"av": -1, "s": -1}

    def rstd_ops(ss_ap, lnbuf_ap, out_ap, n, reads, writes):
        P.op("act", lambda e: e.activation(out=lnbuf_ap, in_=ss_ap, func=AF.Ln, scale=1.0 / n, bias=epsc[:, 0:1]),
             reads=reads + [B_const], writes=writes)
        P.op("act", lambda e: e.activation(out=out_ap, in_=lnbuf_ap, func=AF.Exp, scale=-0.5), reads=writes, writes=writes)

    def norm_and_transpose(g, a_mod, sh_lo, src_phase2):
        hs = g % 2
        b = (2 * g) // TPS
        ssx, rsx = (ss2, rs2) if src_phase2 else (ss1, rs1)
        for i in range(2):
            T = 2 * g + i
            xsl = T % 4
            P.op("act", (lambda e, T=T, xsl=xsl: e.activation(out=junk[:], in_=xs[xsl][:], func=AF.Square, accum_out=ssx[:, T:T + 1])),
                 reads=[B_xs[xsl]], writes=[B_stat[T]])
            rstd_ops(ssx[:, T:T + 1], rsx[:, T, 0:1], rsx[:, T, 1:2], D, [B_stat[T]], [B_stat[T]])
            P.op("dve", (lambda e, T=T, xsl=xsl, i=i: e.tensor_scalar(xn[i][:], xs[xsl][:], rsx[:, T, 1:2], None, ALU.mult)),
                 reads=[B_xs[xsl], B_stat[T]], writes=[B_xn[i]])
        yield
        for rnd in range(2):
            def tr(e, rnd=rnd):
                ins = []
                for jj in range(4):
                    j = rnd * 4 + jj
                    for i in range(2):
                        ins.append(e.transpose(ptr[:, jj * GW + i * 128: jj * GW + (i + 1) * 128], xn[i][:, j * 128:(j + 1) * 128], ident_bf[:]))
                return ins
            P.op("pe", tr, reads=[B_xn[0], B_xn[1], B_const], writes=[B_ptr])
            for jj in range(4):
                j = rnd * 4 + jj
                P.op("dve", (lambda e, j=j, jj=jj: e.tensor_scalar(hT[hs][:, j, :], ptr[:, jj * GW:(jj + 1) * GW],
                                                                  a_mod[:, j, b:b + 1], modT[:, sh_lo + j, b:b + 1], ALU.mult, ALU.add)),
                     reads=[B_ptr, B_mod], writes=[B_hT[hs]])
            yield

    def stage_A(g):
        hs = g % 2
        tp0 = (2 * g) % TPS
        for i in range(2):
            T = 2 * g + i
            xsl = T % 4
            P.op("sp", (lambda e, T=T, xsl=xsl: e.dma_start(out=xs[xsl][:], in_=x_d[T * 128:(T + 1) * 128, :])), writes=[B_xs[xsl]], ndma=1)
        P.op("sp", (lambda e: [e.dma_start(out=cosS[hs][:], in_=cosF_d[:, tp0 * 128: tp0 * 128 + GW]),
                               e.dma_start(out=sinS[hs][:], in_=sinF_d[:, tp0 * 128: tp0 * 128 + GW])]), writes=[B_rope[hs]], ndma=2)
        yield from norm_and_transpose(g, a1, 0, False)

        def fm_proj(w_ap_fn, reads_extra=()):
            bk = bank("proj")

            def f(e):
                return [e.matmul(pb[bk][:, 0:GW], w_ap_fn(k), hT[hs][:, k, :], start=(k == 0), stop=(k == 7)) for k in range(8)]
            P.op("pe", f, reads=[B_hT[hs], B_w1], writes=[B_pb[bk]])
            return bk

        for h in range(4):
            s = h % 2
            bk = fm_proj(lambda k, h=h: win[:, k, 1280 + h * 128: 1280 + (h + 1) * 128])
            P.op("act", (lambda e, bk=bk, s=s: e.activation(out=hA[s][:], in_=pb[bk][:, 0:GW], func=AF.Sigmoid)),
                 reads=[B_pb[bk]], writes=[B_hA[s]])
            P.op("act", (lambda e, s=s, h=h: e.activation(out=hB[s][:], in_=hA[s][:], func=AF.Ln, scale=oml[:, h:h + 1], bias=lbv[:, h:h + 1])),
                 reads=[B_hA[s], B_mod], writes=[B_hB[s]])
            P.op("dve", (lambda e, s=s, h=h: e.tensor_scalar(hA[s][:], hA[s][:], lbm1[:, h:h + 1], oml[:, h:h + 1], ALU.mult, ALU.add)),
                 reads=[B_hA[s], B_mod], writes=[B_hA[s]])
            P.op("dve", (lambda e, s=s: e.tensor_tensor_scan(hC[s][:], rmask[:], hB[s][:], 0.0, ALU.mult, ALU.add)),
                 reads=[B_hB[s], B_const], writes=[B_hC[s]])
            P.op("act", (lambda e, s=s: e.activation(out=hB[s][:], in_=hC[s][:], func=AF.Exp, scale=-1.0)),
                 reads=[B_hC[s]], writes=[B_hB[s]])
            P.op("pool", (lambda e, s=s, h=h: e.tensor_tensor(kd[hs][:, h, :], hA[s][:], hB[s][:], ALU.mult)),
                 reads=[B_hA[s], B_hB[s]], writes=[B_kd[hs]])
            P.op("act", (lambda e, s=s: e.activation(out=hB[s][:], in_=hC[s][:], func=AF.Exp)),
                 reads=[B_hC[s]], writes=[B_hB[s]])
            P.op("pool", (lambda e, s=s, h=h: e.tensor_copy(dec[hs][:, h, :], hB[s][:, 127::128])),
                 reads=[B_hB[s]], writes=[B_dec[hs]])
            bk2 = fm_proj(lambda k, h=h: win[:, k, 768 + h * 128: 768 + (h + 1) * 128])
            P.op("act", (lambda e, bk2=bk2, s=s: e.activation(out=hA[s][:], in_=pb[bk2][:, 0:GW], func=AF.Silu)),
                 reads=[B_pb[bk2]], writes=[B_hA[s]])
            P.op("pool", (lambda e, s=s, h=h: e.tensor_tensor(qd[hs][:, h, :], hA[s][:], hB[s][:], ALU.mult)),
                 reads=[B_hA[s], B_hB[s]], writes=[B_qd[hs]])
            yield

        def rope_part(c):
            s = c % 2
            bk2 = bank("proj")
            P.op("pe", (lambda e: e.matmul(pb[bk2][:, 0:GW], pswap_bf[:], qraw[s][:], start=True, stop=True)),
                 reads=[B_qraw[s], B_const], writes=[B_pb[bk2]])
            P.op("dve", (lambda e: e.tensor_tensor(rt1[s][:], pb[bk2][:, 0:GW], sinS[hs][:], ALU.mult)),
                 reads=[B_pb[bk2], B_rope[hs]], writes=[B_rt1[s]])
            P.op("pool", (lambda e: e.tensor_tensor(rt2[s][:], qraw[s][:], cosS[hs][:], ALU.mult)),
                 reads=[B_qraw[s], B_rope[hs]], writes=[B_rt2[s]])
            if c < 4:
                P.op("pool", (lambda e: e.tensor_tensor(qT[hs][:, c, :], rt1[s][:], rt2[s][:], ALU.add)),
                     reads=[B_rt1[s], B_rt2[s]], writes=[B_qT[hs]])
            else:
                kv = c - 4
                r0 = ((2 * g) % KRING)
                P.op("pool", (lambda e: e.tensor_tensor(kT[:, kv, r0 * 128: r0 * 128 + GW], rt1[s][:], rt2[s][:], ALU.add)),
                     reads=[B_rt1[s], B_rt2[s]], writes=[B_kT[r0], B_kT[r0 + 1]])

        for c in range(7):
            if c < 6:
                s = c % 2
                if c < 4:
                    bk = fm_proj(lambda k, c=c: win[:, k, c * 128:(c + 1) * 128])
                else:
                    bk = fm_proj(lambda k, c=c: wkd[:, k, (c - 4) * 128:(c - 3) * 128])
                P.op("act", (lambda e, bk=bk, s=s: e.copy(qraw[s][:], pb[bk][:, 0:GW])), reads=[B_pb[bk]], writes=[B_qraw[s]])
            if c >= 1:
                rope_part(c - 1)
            yield

    def stage_B(T):
        g = T // 2
        i = T % 2
        hs = g % 2
        ts = T % 2
        b = T // TPS
        tp = T % TPS
        xsl = T % 4
        rk = T % KRING
        rkp = (T - 1) % KRING
        tc = slice(i * 128, (i + 1) * 128)
        kbs = (0, 1) if tp > 0 else (1,)

        def tm_proj(c0, n, eng_evac_fn):
            bk = bank("proj")

            def f(e):
                return [e.matmul(pb[bk][:, 0:n], hT[hs][:, k, tc], win[:, k, c0:c0 + n], start=(k == 0), stop=(k == 7)) for k in range(8)]
            P.op("pe", f, reads=[B_hT[hs], B_w1], writes=[B_pb[bk]])
            eng_evac_fn(bk)

        def sc_part(half):
            kv = half

            def f_sc(e):
                ins = []
                for hh in range(4):
                    head = 4 * half + hh
                    c = head // 2
                    pr = slice((head % 2) * 64, (head % 2) * 64 + 64)
                    for kb in kbs:
                        rr = rkp if kb == 0 else rk
                        ins.append(e.matmul(pb[2 + hh % 2][:, ((hh // 2) * 2 + kb) * 128:((hh // 2) * 2 + kb + 1) * 128],
                                            kT[pr, kv, rr * 128:(rr + 1) * 128], qT[hs][pr, c, tc], start=True, stop=True))
                return ins
            rd = [B_qT[hs], B_kT[rk]] + ([B_kT[rkp]] if tp > 0 else [])
            P.op("pe", f_sc, reads=rd, writes=[B_pb[2], B_pb[3]])
            for bb in range(2):
                if tp > 0:
                    P.op("act", (lambda e, bb=bb: e.activation(out=Pe[ts][half][:, bb::2, :, :],
                                                               in_=pb[2 + bb][:].rearrange("p (a b q) -> p a b q", a=2, b=2), func=AF.Exp, scale=0.125)),
                         reads=[B_pb[2 + bb]], writes=[B_Pe[ts][half]])
                else:
                    P.op("act", (lambda e, bb=bb: e.activation(out=Pe[ts][half][:, bb::2, 1, :],
                                                               in_=pb[2 + bb][:].rearrange("p (a b q) -> p a b q", a=2, b=2)[:, :, 1, :],
                                                               func=AF.Exp, scale=0.125)),
                         reads=[B_pb[2 + bb]], writes=[B_Pe[ts][half]])
            if tp > 0:
                P.op("dve", (lambda e: e.tensor_tensor(Pe[ts][half][:], Pe[ts][half][:], mask2_bf[:].unsqueeze(1).broadcast_to([128, 4, 2, 128]), ALU.mult)),
                     reads=[B_Pe[ts][half], B_const], writes=[B_Pe[ts][half]])
            else:
                P.op("dve", (lambda e: e.tensor_tensor(Pe[ts][half][:, :, 1, :], Pe[ts][half][:, :, 1, :],
                                                       mask2_bf[:, 1, :].unsqueeze(1).broadcast_to([128, 4, 128]), ALU.mult)),
                     reads=[B_Pe[ts][half], B_const], writes=[B_Pe[ts][half]])

        def pv_part(half):
            kv = half
            bkO = bank("o")

            def f_pv(e):
                ins = []
                for hh in range(4):
                    for n, kb in enumerate(kbs):
                        rr = rkp if kb == 0 else rk
                        ins.append(e.matmul(pb[bkO][:, hh * 128:hh * 128 + 72], Pe[ts][half][:, hh, kb, :], vaug[:, rr, kv, :],
                                            start=(n == 0), stop=(n == len(kbs) - 1)))
                return ins
            rd = [B_Pe[ts][half], B_va[rk]] + ([B_va[rkp]] if tp > 0 else [])
            P.op("pe", f_pv, reads=rd, writes=[B_pb[bkO]])
            pO = pb[bkO][:].rearrange("p (h d) -> p h d", h=4)
            P.op("dve", (lambda e: e.tensor_tensor(den[:, T, half, :], pO[:, :, 64], esink[:, 4 * half:4 * half + 4], ALU.add)),
                 reads=[B_pb[bkO], B_const], writes=[B_stat[T]])
            P.op("dve", (lambda e: e.reciprocal(rden[:, T, half, :], den[:, T, half, :])), reads=[B_stat[T]], writes=[B_stat[T]])
            P.op("dve", (lambda e: e.tensor_tensor(attn[ts][:, half * 256:(half + 1) * 256].rearrange("p (h d) -> p h d", h=4),
                                                   pO[:, :, 0:64], rden[:, T, half, :].unsqueeze(2).broadcast_to([128, 4, 64]), ALU.mult)),
                 reads=[B_pb[bkO], B_stat[T]], writes=[B_attn[ts]])

        tm_proj(640, 128, lambda bk: P.op("dve", (lambda e: e.tensor_copy(vaug[:, rk, :, 0:64], pb[bk][:, 0:128].rearrange("p (a d) -> p a d", a=2))),
                                          reads=[B_pb[bk]], writes=[B_va[rk]]))
        state["av"] = T
        tm_proj(1792, 512, lambda bk: P.op("act", (lambda e: e.copy(vt[ts][:], pb[bk][:])), reads=[B_pb[bk]], writes=[B_vt[ts]]))

        def ev_g(bk):
            P.op("act", (lambda e: e.activation(out=gg[ts][:], in_=pb[bk][:], func=AF.Silu)), reads=[B_pb[bk]], writes=[B_gg[ts]])
            P.op("pool", (lambda e: e.tensor_tensor(gg[ts][:], gg[ts][:], hgw[:], ALU.mult)), reads=[B_gg[ts], B_const], writes=[B_gg[ts]])
        tm_proj(2304, 512, ev_g)
        bkA = bank("o")

        def f_At(e):
            return [e.matmul(pb[bkA][:, h * 128:(h + 1) * 128], kd[hs][:, h, tc], qd[hs][:, h, tc], start=True, stop=True) for h in range(4)]
        P.op("pe", f_At, reads=[B_kd[hs], B_qd[hs]], writes=[B_pb[bkA]])
        P.op("dve", (lambda e: e.tensor_tensor(Am[ts][:], pb[bkA][:].rearrange("p (h t) -> p h t", h=4),
                                               maskT_f[:].unsqueeze(1).broadcast_to([128, 4, 128]), ALU.mult)),
             reads=[B_pb[bkA], B_const], writes=[B_Am[ts]])

        def f_kt(e):
            return [e.transpose(ptr[:, h * 128:(h + 1) * 128], kd[hs][:, h, tc], ident_bf[:]) for h in range(4)]
        P.op("pe", f_kt, reads=[B_kd[hs], B_const], writes=[B_ptr])
        P.op("act", (lambda e: e.copy(kdtok[ts][:].rearrange("p h d -> p (h d)"), ptr[:, 0:512])), reads=[B_ptr], writes=[B_kdtok[ts]])
        yield
        while tp > 0 and state["av"] < T - 1:
            yield
        sc_part(0)
        yield
        sc_part(1)
        pv_part(0)
        yield
        pv_part(1)
        P.op("act", (lambda e: e.activation(out=junk[:, 0:512], in_=attn[ts][:], func=AF.Square, accum_out=ssa[:, T:T + 1])),
             reads=[B_attn[ts]], writes=[B_stat[T]])
        rstd_ops(ssa[:, T:T + 1], rsa[:, T, 0:1], rsa[:, T, 1:2], 512, [B_stat[T]], [B_stat[T]])
        P.op("dve", (lambda e: e.scalar_tensor_tensor(cat[ts][:, 0:512], attn[ts][:], rsa[:, T, 1:2], aow[:], ALU.mult, ALU.mult)),
             reads=[B_attn[ts], B_stat[T], B_const], writes=[B_cat[ts]])
        yield

        while state["s"] < T - 1:
            yield
        if tp == 0:
            P.op("pool", lambda e: e.memset(Sst[:], 0.0), writes=[B_S])
            P.op("pool", lambda e: e.memset(Sb[:], 0.0), writes=[B_Sb])
        bko = bank("o")

        def f_o(e):
            ins = []
            for h in range(4):
                ins.append(e.matmul(pb[bko][:, h * 128:(h + 1) * 128], Am[ts][:, h, :], vt[ts][:, h * 128:(h + 1) * 128], start=True, stop=False))
                ins.append(e.matmul(pb[bko][:, h * 128:(h + 1) * 128], qd[hs][:, h, tc], Sb[:, h, :], start=False, stop=True))
            return ins
        P.op("pe", f_o, reads=[B_Am[ts], B_vt[ts], B_qd[hs], B_Sb], writes=[B_pb[bko]])
        bkK = bank("o")

        def f_kv(e):
            return [e.matmul(pb[bkK][:, h * 128:(h + 1) * 128], kdtok[ts][:, h, :], vt[ts][:, h * 128:(h + 1) * 128], start=True, stop=True)
                    for h in range(4)]
        P.op("pe", f_kv, reads=[B_kdtok[ts], B_vt[ts]], writes=[B_pb[bkK]])
        decb = dec[hs][:, :, i:i + 1].broadcast_to([128, 4, 128])
        P.op("dve", (lambda e: e.tensor_tensor(tmpS[ts][:], pb[bkK][:].rearrange("p (h d) -> p h d", h=4), Sst[:], ALU.add)),
             reads=[B_pb[bkK], B_S], writes=[B_tmpS[ts]])
        P.op("dve", (lambda e: e.tensor_tensor(Sb[:], tmpS[ts][:], decb, ALU.mult)), reads=[B_tmpS[ts], B_dec[hs]], writes=[B_Sb])
        P.op("pool", (lambda e: e.tensor_tensor(Sst[:], tmpS[ts][:], decb, ALU.mult)), reads=[B_tmpS[ts], B_dec[hs]], writes=[B_S])
        state["s"] = T
        P.op("act", (lambda e: e.activation(out=sqo[ts][:].rearrange("p h d -> p (h d)"), in_=pb[bko][:], func=AF.Square)),
             reads=[B_pb[bko]], writes=[B_sqo[ts]])
        P.op("dve", (lambda e: e.tensor_reduce(ssr[:, T, :], sqo[ts][:], AX.X, ALU.add)), reads=[B_sqo[ts]], writes=[B_stat[T]])
        rstd_ops(ssr[:, T, :], rsr[:, T, 0:4], rsr[:, T, 4:8], 128, [B_stat[T]], [B_stat[T]])
        P.op("dve", (lambda e: e.tensor_tensor(rect[ts][:], pb[bko][:].rearrange("p (h d) -> p h d", h=4),
                                               rsr[:, T, 4:8].unsqueeze(2).broadcast_to([128, 4, 128]), ALU.mult)),
             reads=[B_pb[bko], B_stat[T]], writes=[B_rect[ts]])
        P.op("pool", (lambda e: e.tensor_tensor(cat[ts][:, 512:1024], rect[ts][:].rearrange("p h d -> p (h d)"), gg[ts][:], ALU.mult)),
             reads=[B_rect[ts], B_gg[ts]], writes=[B_cat[ts]])
        yield

        def f_ct(e):
            return [e.transpose(ptr[:, j * 128:(j + 1) * 128], cat[ts][:, j * 128:(j + 1) * 128], ident_bf[:]) for j in range(8)]
        P.op("pe", f_ct, reads=[B_cat[ts], B_const], writes=[B_ptr])
        P.op("act", (lambda e: e.copy(catT[ts][:].rearrange("p j t -> p (j t)"), ptr[:, :])), reads=[B_ptr], writes=[B_catT[ts]])
        yield

        def f_mix(e):
            ins = []
            for nh in range(2):
                for k in range(8):
                    ins.append(e.matmul(pb[2 + nh][:], catT[ts][:, k, :], wout[:, k, nh * 512:(nh + 1) * 512], start=(k == 0), stop=(k == 7)))
            return ins
        P.op("pe", f_mix, reads=[B_catT[ts], B_w1], writes=[B_pb[2], B_pb[3]])
        for nh in range(2):
            P.op("act", (lambda e, nh=nh: e.activation(out=junk[:, 0:512], in_=pb[2 + nh][:], func=AF.Square, accum_out=ssm[:, T, nh:nh + 1])),
                 reads=[B_pb[2 + nh]], writes=[B_stat[T]])
        P.op("dve", (lambda e: e.tensor_tensor(ssm[:, T, 2:3], ssm[:, T, 0:1], ssm[:, T, 1:2], ALU.add)), reads=[B_stat[T]], writes=[B_stat[T]])
        rstd_ops(ssm[:, T, 2:3], ssm[:, T, 3:4], ssm[:, T, 0:1], D, [B_stat[T]], [B_stat[T]])
        for nh in range(2):
            P.op("dve", (lambda e, nh=nh: e.scalar_tensor_tensor(tmpx[ts][:, nh * 512:(nh + 1) * 512], pb[2 + nh][:], ssm[:, T, 0:1],
                                                                 gv1[b][:, nh * 512:(nh + 1) * 512], ALU.mult, ALU.mult)),
                 reads=[B_pb[2 + nh], B_stat[T], B_mod], writes=[B_tmpx[ts]])
        P.op("pool", (lambda e: e.tensor_tensor(xs[xsl][:], tmpx[ts][:], xs[xsl][:], ALU.add)), reads=[B_tmpx[ts], B_xs[xsl]], writes=[B_xs[xsl]])
        P.op("sp", (lambda e: e.dma_start(out=x1_d[T * 128:(T + 1) * 128, :], in_=xs[xsl][:])), reads=[B_xs[xsl]], writes=[B_x1d[T]], ndma=1)
        yield

    if stop >= 2:
        for _ in stage_A(0):
            pass
    for g in range(ngroups if stop >= 3 else 0):
        gens = [stage_B(2 * g), stage_B(2 * g + 1)]
        if g + 1 < ngroups:
            gens.append(stage_A(g + 1))
        run_interleaved(gens)

    P.barrier()
    p1.close()
    pp.close()

    p2 = ExitStack()
    wup = P.sb("wup", [128, 8, DFF], BF16, p2)
    wdn = P.sb("wdn", [128, 32, D], BF16, p2)
    rl = [P.sb(f"rl{i}", [128, 512], BF16, p2) for i in range(2)]
    uT = P.sb("uT", [128, 32, GW], BF16, p2)
    B_wup = P.bufs(4, "wup")
    B_wdn = P.bufs(8, "wdn")
    B_rl = P.bufs(2, "rl")
    B_uT = P.bufs(16, "uT")

    wup_v = wup_d.rearrange("(j p) c -> p j c", p=128)
    wdn_v = wdn_d.rearrange("(j p) c -> p j c", p=128)
    for q in range(4 if stop >= 4 else 0):
        P.op("pool", (lambda e, q=q: e.dma_start(out=wup[:, :, q * 1024:(q + 1) * 1024], in_=wup_v[:, :, q * 1024:(q + 1) * 1024])),
             writes=[B_wup[q]], ndma=1)
    for j4 in range(8 if stop >= 4 else 0):
        P.op("pool", (lambda e, j4=j4: e.dma_start(out=wdn[:, 4 * j4:4 * j4 + 4, :], in_=wdn_v[:, 4 * j4:4 * j4 + 4, :])), writes=[B_wdn[j4]], ndma=1)

    up_banks = [4, 5, 6]
    upc = [0]
    down_units = [(0, 1), (2, 3)]

    def mlp_prep(g):
        for i in range(2):
            T = 2 * g + i
            xsl = T % 4
            P.op("sp", (lambda e, T=T, xsl=xsl: e.dma_start(out=xs[xsl][:], in_=x1_d[T * 128:(T + 1) * 128, :])),
                 reads=[B_x1d[T]], writes=[B_xs[xsl]], ndma=1)
        yield from norm_and_transpose(g, a2, 24, True)

    def mlp_main(g):
        hs = g % 2
        for cp in range(16):
            bk = up_banks[upc[0] % 3]
            upc[0] += 1
            s = cp % 2

            def f_up(e, cp=cp, bk=bk):
                ins = []
                for cc in range(2):
                    c = 2 * cp + cc
                    for k in range(8):
                        ins.append(e.matmul(pb[bk][:, cc * GW:(cc + 1) * GW], wup[:, k, c * 128:(c + 1) * 128], hT[hs][:, k, :],
                                            start=(k == 0), stop=(k == 7)))
                return ins
            P.op("pe", f_up, reads=[B_hT[hs], B_wup[cp // 4]], writes=[B_pb[bk]])
            P.op("act", (lambda e, bk=bk, s=s: e.activation(out=rl[s][:], in_=pb[bk][:], func=AF.Relu)), reads=[B_pb[bk]], writes=[B_rl[s]])
            P.op("pool", (lambda e, s=s, cp=cp: e.tensor_tensor(uT[:, 2 * cp:2 * cp + 2, :].rearrange("p a t -> p (a t)"), rl[s][:], rl[s][:], ALU.mult)),
                 reads=[B_rl[s]], writes=[B_uT[cp]])
            if cp % 4 == 3:
                yield
        for i in range(2):
            T = 2 * g + i
            xsl = T % 4
            b = T // TPS
            ts = T % 2
            du = down_units[i]

            for cb in range(4):
                def f_dn(e, i=i, du=du, cb=cb):
                    ins = []
                    for nh in range(2):
                        for c in range(8 * cb, 8 * cb + 8):
                            ins.append(e.matmul(pb[du[nh]][:], uT[:, c, i * 128:(i + 1) * 128], wdn[:, c, nh * 512:(nh + 1) * 512],
                                                start=(c == 0), stop=(c == 31)))
                    return ins
                P.op("pe", f_dn, reads=B_uT[4 * cb:4 * cb + 4] + [B_wdn[2 * cb], B_wdn[2 * cb + 1]], writes=[B_pb[du[0]], B_pb[du[1]]])
            for nh in range(2):
                P.op("act", (lambda e, nh=nh, du=du, T=T: e.activation(out=junk[:, 0:512], in_=pb[du[nh]][:], func=AF.Square, accum_out=ssy[:, T, nh:nh + 1])),
                     reads=[B_pb[du[nh]]], writes=[B_stat[T]])
            P.op("dve", (lambda e, T=T: e.tensor_tensor(ssy[:, T, 2:3], ssy[:, T, 0:1], ssy[:, T, 1:2], ALU.add)), reads=[B_stat[T]], writes=[B_stat[T]])
            rstd_ops(ssy[:, T, 2:3], ssy[:, T, 3:4], ssy[:, T, 0:1], D, [B_stat[T]], [B_stat[T]])
            for nh in range(2):
                P.op("dve", (lambda e, nh=nh, du=du, T=T, ts=ts, b=b: e.scalar_tensor_tensor(
                    tmpx[ts][:, nh * 512:(nh + 1) * 512], pb[du[nh]][:], ssy[:, T, 0:1], gv2[b][:, nh * 512:(nh + 1) * 512], ALU.mult, ALU.mult)),
                    reads=[B_pb[du[nh]], B_stat[T], B_mod], writes=[B_tmpx[ts]])
            P.op("pool", (lambda e, ts=ts, xsl=xsl: e.tensor_tensor(xs[xsl][:], tmpx[ts][:], xs[xsl][:], ALU.add)),
                 reads=[B_tmpx[ts], B_xs[xsl]], writes=[B_xs[xsl]])
            P.op("sp", (lambda e, T=T, xsl=xsl: e.dma_start(out=out_d[T * 128:(T + 1) * 128, :], in_=xs[xsl][:])),
                 reads=[B_xs[xsl]], writes=[B_out[T]], ndma=1)
            yield

    n2 = ngroups if stop >= 5 else 0
    if n2:
        for _ in mlp_prep(0):
            pass
    for g in range(n2):
        gens = [mlp_main(g)]
        if g + 1 < n2:
            gens.append(mlp_prep(g + 1))
        run_interleaved(gens)

    P.op("sp", None, reads=B_out)
    p2.close()
    P.emit()
    return nc


_NC_CACHE = {}


def _layout_inputs(inp):
    f = lambda a: np.ascontiguousarray(np.asarray(a, dtype=np.float32))
    consts = _host_consts()
    colT = lambda v, n: f(np.asarray(v).reshape(n, 128).T)
    shared = {
        "w_ada": f(inp["w_ada"][0]),
        "b_adaT": colT(inp["b_ada"][0], 48),
        "pre_mixT": colT(inp["pre_w_mix"][0], 8),
        "pre_mlpT": colT(inp["pre_w_mlp"][0], 8),
        "post_mixT": colT(inp["post_w_mix"][0], 8),
        "post_mlpT": colT(inp["post_w_mlp"][0], 8),
        "w_in": f(inp["w_in"][0]),
        "w_out": f(inp["w_out"][0]),
        "w_up": f(inp["w_up"][0]),
        "w_down": f(inp["w_down"][0]),
        "sinks_rep": f(np.broadcast_to(np.asarray(inp["attn_sinks"][0])[None, :], (128, 8))),
        "aow_rep": f(np.broadcast_to(np.asarray(inp["attn_out_w"][0])[None, :], (128, 512))),
        "hgw_rep": f(np.broadcast_to(np.tile(np.asarray(inp["hg_norm_w"][0]), 4)[None, :], (128, 512))),
        "lbT": f(np.asarray(inp["lb_table"]).reshape(2, 4, 128).transpose(2, 0, 1)),
    }
    shared.update(consts)
    x = np.asarray(inp["x"], dtype=np.float32)
    c = np.asarray(inp["c"], dtype=np.float32)
    maps = []
    for i in range(NCORES):
        m = dict(shared)
        m["x"] = f(x[NSEQ * i:NSEQ * (i + 1)].reshape(TOK, D))
        cc = c[NSEQ * i:NSEQ * (i + 1)]
        m["cT"] = f(cc.reshape(2, 8, 128).transpose(2, 1, 0))
        maps.append(m)
    return maps


def kernel(**inputs):
    if "nc" not in _NC_CACHE:
        _NC_CACHE["nc"] = build_program()
    nc = _NC_CACHE["nc"]
    maps = _layout_inputs(inputs)
    res = run_bass_kernel_spmd(nc, maps, core_ids=list(range(NCORES)))
    outs = [np.asarray(r["out"], dtype=np.float32).reshape(NSEQ, SEQ, D) for r in res.results]
    return np.concatenate(outs, axis=0)
```

```python
import numpy as np
from contextlib import ExitStack
import concourse.bass as bass
import concourse.mybir as mybir
from concourse.bass_utils import run_bass_kernel_spmd

F32 = mybir.dt.float32
BF16 = mybir.dt.bfloat16
AF = mybir.ActivationFunctionType
ALU = mybir.AluOpType
AX = mybir.AxisListType

NCORES = 8
D = 1024
SEQ = 2048
NSEQ = 2
TOK = NSEQ * SEQ
NT = TOK // 128
TPS = SEQ // 128
NG = NT // 2
GW = 256
DFF = 4096
INC = 2816
EPS = 1e-6
KRING = 6
import os as _os
BSTOP = float(_os.environ.get('KBSTOP', '99'))


class Buf:
    __slots__ = ("name", "last_w", "readers")

    def __init__(self, name):
        self.name = name
        self.last_w = None
        self.readers = []


class Op:
    __slots__ = ("eng", "fn", "deps", "idx", "ndma", "signal", "semval", "sem", "name", "prevval")


class Prog:
    ENGS = ("pe", "act", "dve", "pool", "sp")
    NDMASEM = 24

    def __init__(self, nc):
        self.nc = nc
        self.streams = {e: [] for e in self.ENGS}
        self.es = ExitStack()
        self.nbuf = 0
        self.dma_since_barrier = []

    def sb(self, name, shape, dt, stack=None):
        return (stack or self.es).enter_context(self.nc.sbuf_tensor("s_" + name, list(shape), dt))

    def ps(self, name, shape, dt):
        return self.es.enter_context(self.nc.psum_tensor("p_" + name, list(shape), dt))

    def buf(self, name=None):
        self.nbuf += 1
        return Buf(name or f"b{self.nbuf}")

    def bufs(self, n, name="b"):
        return [self.buf(f"{name}{i}") for i in range(n)]

    def op(self, eng, fn, reads=(), writes=(), ndma=0, name=None):
        o = Op()
        o.eng = eng
        o.fn = fn
        o.ndma = ndma
        o.signal = False
        o.semval = None
        o.sem = None
        o.name = name
        o.prevval = 0
        deps = []
        for b in reads:
            if b.last_w is not None:
                deps.append(b.last_w)
        for b in writes:
            if b.last_w is not None:
                deps.append(b.last_w)
            deps.extend(b.readers)
        seen = set()
        dd = []
        for d in deps:
            if id(d) in seen:
                continue
            seen.add(id(d))
            if eng == "pe" and d.eng == "pe" and d.ndma == 0:
                continue
            dd.append(d)
        o.deps = dd
        o.idx = len(self.streams[eng])
        self.streams[eng].append(o)
        for b in reads:
            b.readers.append(o)
        for b in writes:
            b.last_w = o
            b.readers = []
        if ndma > 0:
            self.dma_since_barrier.append(o)
        return o

    def barrier(self):
        lasts = {}
        for e in self.ENGS:
            for o in reversed(self.streams[e]):
                if o.ndma == 0 and o.fn is not None:
                    lasts[e] = o
                    break
        dmas = list(self.dma_since_barrier)
        self.dma_since_barrier = []
        for e in self.ENGS:
            o = Op()
            o.eng = e
            o.fn = None
            o.ndma = 0
            o.signal = False
            o.semval = None
            o.sem = None
            o.name = "barrier"
            o.prevval = 0
            o.deps = [lasts[x] for x in lasts if x != e] + dmas
            o.idx = len(self.streams[e])
            self.streams[e].append(o)

    def emit(self):
        nc = self.nc
        for e in self.ENGS:
            for o in self.streams[e]:
                for d in o.deps:
                    d.signal = True
        for e in self.ENGS:
            cnt = 0
            for o in self.streams[e]:
                if o.ndma == 0 and o.signal:
                    cnt += 1
                    o.semval = cnt
        dcount = [0] * self.NDMASEM
        dpool = {"sp": list(range(0, 16)), "pool": list(range(16, 24))}
        for e in self.ENGS:
            rr = 0
            for o in self.streams[e]:
                if o.ndma > 0:
                    s = dpool[e][rr % len(dpool[e])]
                    rr += 1
                    o.sem = s
                    o.prevval = dcount[s]
                    dcount[s] += 16 * o.ndma
                    o.semval = dcount[s]
        es = self.es
        esem = {e: es.enter_context(nc.semaphore(f"tl_{e}")) for e in self.ENGS}
        dsem = [es.enter_context(nc.semaphore(f"dma_{i}")) for i in range(self.NDMASEM)]
        block = es.enter_context(nc.Block())
        streams = self.streams

        def run_stream(e, engobj):
            seen = {}
            for o in streams[e]:
                waits = []
                for d in o.deps:
                    if d.ndma > 0:
                        key = ("d", d.sem)
                        sem = dsem[d.sem]
                    else:
                        key = ("e", d.eng)
                        sem = esem[d.eng]
                    v = d.semval
                    if seen.get(key, 0) >= v:
                        continue
                    waits.append((key, sem, v))
                if o.ndma > 0:
                    key = ("d", o.sem)
                    if o.prevval > 0 and seen.get(key, 0) < o.prevval:
                        waits.append((key, dsem[o.sem], o.prevval))
                best = {}
                for key, sem, v in waits:
                    if key not in best or best[key][1] < v:
                        best[key] = (sem, v)
                wl = list(best.items())
                for key, (sem, v) in wl:
                    seen[key] = v
                if o.ndma > 0:
                    for key, (sem, v) in wl:
                        engobj.wait_ge(sem, v)
                    ins = o.fn(engobj)
                    if not isinstance(ins, (list, tuple)):
                        ins = [ins]
                    assert len(ins) == o.ndma, (o.name, len(ins), o.ndma)
                    for i in ins:
                        i.then_inc(dsem[o.sem], 16)
                elif o.fn is None:
                    for key, (sem, v) in wl:
                        engobj.wait_ge(sem, v)
                else:
                    for key, (sem, v) in wl[1:]:
                        engobj.wait_ge(sem, v)
                    ins = o.fn(engobj)
                    if not isinstance(ins, (list, tuple)):
                        ins = [ins]
                    if wl:
                        ins[0]._wait_ge(wl[0][1][0], wl[0][1][1])
                    if o.signal:
                        ins[-1].then_inc(esem[e], 1)

        @block.tensor
        def _(eng):
            run_stream("pe", eng)

        @block.scalar
        def _(eng):
            run_stream("act", eng)

        @block.vector
        def _(eng):
            run_stream("dve", eng)

        @block.gpsimd
        def _(eng):
            run_stream("pool", eng)

        @block.sync
        def _(eng):
            run_stream("sp", eng)

        es.close()


def run_interleaved(gens, weights=None):
    gens = list(gens)
    weights = list(weights) if weights else [1] * len(gens)
    live = list(range(len(gens)))
    while live:
        for gi in list(live):
            for _ in range(weights[gi]):
                try:
                    next(gens[gi])
                except StopIteration:
                    live.remove(gi)
                    break


def _host_consts():
    c = {}
    c["ident"] = np.eye(128, dtype=np.float32)
    c["ones"] = np.ones((128, 128), dtype=np.float32)
    sel = np.zeros((2, 2, 128), dtype=np.float32)
    sel[0, 0, :] = 1.0
    sel[1, 1, :] = 1.0
    c["sel"] = sel
    k = np.arange(128)[:, None]
    q = np.arange(128)[None, :]
    cur = (k <= q).astype(np.float32)
    prev = (k > q).astype(np.float32)
    c["mask2"] = np.ascontiguousarray(np.stack([prev, cur], axis=1).reshape(128, 256))
    c["maskT"] = np.ascontiguousarray(cur)
    rm = np.ones((128, GW), dtype=np.float32)
    rm[:, 0::128] = 0.0
    c["rmask"] = rm
    ps = np.zeros((128, 128), dtype=np.float32)
    for m in range(128):
        d = m % 64
        base = m - d
        if d < 8:
            s = base + d + 8
        elif d < 16:
            s = base + d - 8
        else:
            s = m
        ps[s, m] = 1.0
    c["pswap"] = ps
    inv_freq = (np.float32(500000.0) ** (-np.arange(0, 16, 2, dtype=np.float32) / np.float32(16))).astype(np.float32)
    ang = (np.arange(SEQ, dtype=np.float32)[:, None] * inv_freq[None, :]).astype(np.float32)
    cs = np.cos(ang).astype(np.float32)
    sn = np.sin(ang).astype(np.float32)
    cosF = np.ones((128, SEQ), dtype=np.float32)
    sinF = np.zeros((128, SEQ), dtype=np.float32)
    for p in range(128):
        d = p % 64
        if d < 8:
            cosF[p] = cs[:, d]
            sinF[p] = -sn[:, d]
        elif d < 16:
            cosF[p] = cs[:, d - 8]
            sinF[p] = sn[:, d - 8]
    c["cosF"] = cosF
    c["sinF"] = sinF
    return c


def build_program(stop=99, ngroups=NG):
    nc = bass.Bass("TRN2", target_bir_lowering=False)
    P = Prog(nc)

    def din(name, shape):
        return nc.dram_tensor(name, list(shape), F32, kind="ExternalInput").ap()

    x_d = din("x", [TOK, D])
    cT_d = din("cT", [128, 8, 2])
    wada_d = din("w_ada", [D, 6 * D])
    bada_d = din("b_adaT", [128, 48])
    premix_d = din("pre_mixT", [128, 8])
    premlp_d = din("pre_mlpT", [128, 8])
    win_d = din("w_in", [D, INC])
    wout_d = din("w_out", [D, D])
    wup_d = din("w_up", [D, DFF])
    wdn_d = din("w_down", [DFF, D])
    sinks_d = din("sinks_rep", [128, 8])
    aow_d = din("aow_rep", [128, 512])
    hgw_d = din("hgw_rep", [128, 512])
    lbT_d = din("lbT", [128, 2, 4])
    bgrow_d = din("bgrow", [2, 2, D])
    pwrow_d = din("pwrow", [2, 2, D])
    sel_d = din("sel", [2, 2, 128])
    ident_d = din("ident", [128, 128])
    ones_d = din("ones", [128, 128])
    mask2_d = din("mask2", [128, 256])
    maskT_d = din("maskT", [128, 128])
    rmask_d = din("rmask", [128, GW])
    pswap_d = din("pswap", [128, 128])
    cosF_d = din("cosF", [128, SEQ])
    sinF_d = din("sinF", [128, SEQ])
    out_d = nc.dram_tensor("out", [TOK, D], F32, kind="ExternalOutput").ap()
    x1_d = nc.dram_tensor("x1_scratch", [TOK, D], F32).ap()

    ident_bf = P.sb("ident_bf", [128, 128], BF16)
    ident_f = P.sb("ident_f", [128, 128], F32)
    ones_f = P.sb("ones_f", [128, 128], F32)
    pswap_bf = P.sb("pswap_bf", [128, 128], BF16)
    mask2_bf = P.sb("mask2_bf", [128, 2, 128], BF16)
    maskT_f = P.sb("maskT_f", [128, 128], F32)
    rmask = P.sb("rmask", [128, GW], F32)
    epsc = P.sb("epsc", [128, 1], F32)
    esink = P.sb("esink", [128, 8], F32)
    modT = P.sb("modT", [128, 48, 2], F32)
    a1 = P.sb("a1", [128, 8, 2], F32)
    a2 = P.sb("a2", [128, 8, 2], F32)
    lbv = P.sb("lbv", [128, 4], F32)
    oml = P.sb("oml", [128, 4], F32)
    lbm1 = P.sb("lbm1", [128, 4], F32)
    gv2 = [P.sb(f"gv2_{b}", [128, D], F32) for b in range(2)]
    ss1 = P.sb("ss1", [128, NT], F32)
    rs1 = P.sb("rs1", [128, NT, 2], F32)
    ssa = P.sb("ssa", [128, NT], F32)
    rsa = P.sb("rsa", [128, NT, 2], F32)
    ssm = P.sb("ssm", [128, NT, 4], F32)
    ssr = P.sb("ssr", [128, NT, 4], F32)
    rsr = P.sb("rsr", [128, NT, 8], F32)
    den = P.sb("den", [128, NT, 2, 4], F32)
    rden = P.sb("rden", [128, NT, 2, 4], F32)
    ss2 = P.sb("ss2", [128, NT], F32)
    rs2 = P.sb("rs2", [128, NT, 2], F32)
    ssy = P.sb("ssy", [128, NT, 4], F32)
    xs = [P.sb(f"xs{i}", [128, D], F32) for i in range(4)]
    xn = [P.sb(f"xn{i}", [128, D], BF16) for i in range(2)]
    junk = P.sb("junk", [128, D], BF16)
    hT = [P.sb(f"hT{i}", [128, 8, GW], BF16) for i in range(2)]
    tmpx = [P.sb(f"tmpx{i}", [128, D], F32) for i in range(2)]

    pp = ExitStack()
    aow = P.sb("aow", [128, 512], F32, pp)
    hgw = P.sb("hgw", [128, 512], F32, pp)
    gv1 = [P.sb(f"gv1_{b}", [128, D], F32, pp) for b in range(2)]

    pb = [P.ps(f"pb{i}", [128, 512], F32) for i in range(7)]
    ptr = P.ps("ptr", [128, 1024], BF16)
    B_pb = P.bufs(7, "pb")
    B_ptr = P.buf("ptr")

    B_const = P.buf("const")
    B_mod = P.buf("mod")
    B_xs = P.bufs(4, "xs")
    B_xn = P.bufs(2, "xn")
    B_junk = P.buf("junk")
    B_hT = P.bufs(2, "hT")
    B_tmpx = P.bufs(2, "tmpx")
    B_stat = [P.buf(f"stat{t}") for t in range(NT)]
    B_x1d = [P.buf(f"x1d{t}") for t in range(NT)]
    B_out = [P.buf(f"outd{t}") for t in range(NT)]

    def ld(eng, dst, src, b=B_const, n=1):
        P.op(eng, lambda e: e.dma_start(out=dst, in_=src), writes=[b], ndma=n)

    ld("pool", ident_bf[:], ident_d[:, :])
    ld("sp", ident_f[:], ident_d[:, :])
    ld("sp", ones_f[:], ones_d[:, :])
    ld("pool", pswap_bf[:], pswap_d[:, :])
    ld("pool", mask2_bf[:].rearrange("p a b -> p (a b)"), mask2_d[:, :])
    ld("sp", maskT_f[:], maskT_d[:, :])
    ld("sp", rmask[:], rmask_d[:, :])
    ld("sp", esink[:], sinks_d[:, :])
    ld("sp", aow[:], aow_d[:, :])
    ld("sp", hgw[:], hgw_d[:, :])
    P.op("pool", lambda e: e.memset(epsc[:], EPS), writes=[B_const])
    P.op("act", lambda e: e.activation(out=esink[:], in_=esink[:], func=AF.Exp), reads=[B_const], writes=[B_const])

    pw = ExitStack()
    win = P.sb("win", [128, 8, INC], BF16, pw)
    wkd = P.sb("wkd", [128, 8, 256], BF16, pw)
    wout = P.sb("wout", [128, 8, D], BF16, pw)
    B_w1 = P.buf("w1")
    win_v = win_d.rearrange("(j p) c -> p j c", p=128)
    wout_v = wout_d.rearrange("(j p) c -> p j c", p=128)
    for j in range(8):
        P.op("pool", (lambda e, j=j: e.dma_start(out=win[:, j, :], in_=win_v[:, j, :])), writes=[B_w1], ndma=1)
    for kv in range(2):
        for r in range(2):
            c0 = (2 * kv + r) * 64
            P.op("pool", (lambda e, kv=kv, c0=c0: e.dma_start(out=wkd[:, :, c0:c0 + 64], in_=win_v[:, :, 512 + 64 * kv:576 + 64 * kv])),
                 writes=[B_w1], ndma=1)
    for j in range(8):
        P.op("pool", (lambda e, j=j: e.dma_start(out=wout[:, j, :], in_=wout_v[:, j, :])), writes=[B_w1], ndma=1)

    st = ExitStack()
    cTs = P.sb("cTs", [128, 8, 2], F32, st)
    ca = P.sb("ca", [128, 8, 2], F32, st)
    badaT = P.sb("badaT", [128, 48], F32, st)
    premix = P.sb("premix", [128, 8], F32, st)
    premlp = P.sb("premlp", [128, 8], F32, st)
    lbT = P.sb("lbT", [128, 2, 4], F32, st)
    lbd = P.sb("lbd", [128, 4], F32, st)
    wa = [P.sb(f"wa{i}", [128, 8, 256], F32, st) for i in range(2)]
    modrow = P.sb("modrow", [2, 6 * D], F32, st)
    bgrow = P.sb("bgrow", [2, 2, D], F32, st)
    pwrow = P.sb("pwrow", [2, 2, D], F32, st)
    grow = P.sb("grow", [2, 2, D], F32, st)
    sel = P.sb("sel", [2, 2, 128], F32, st)
    B_wa = P.bufs(2, "wa")
    B_set = P.buf("setup")
    B_row = P.buf("modrow")

    ld("sp", cTs[:], cT_d[:, :, :], B_set)
    ld("sp", badaT[:], bada_d[:, :], B_set)
    ld("sp", premix[:], premix_d[:, :], B_set)
    ld("sp", premlp[:], premlp_d[:, :], B_set)
    ld("sp", lbT[:], lbT_d[:, :, :], B_set)
    ld("sp", bgrow[:], bgrow_d[:, :, :], B_set)
    ld("sp", pwrow[:], pwrow_d[:, :, :], B_set)
    ld("sp", sel[:], sel_d[:, :, :], B_set)
    P.op("act", lambda e: e.activation(out=ca[:], in_=cTs[:], func=AF.Silu), reads=[B_set], writes=[B_set])
    P.op("dve", lambda e: e.tensor_tensor(lbd[:], lbT[:, 1, :], lbT[:, 0, :], ALU.subtract), reads=[B_set], writes=[B_set])
    P.op("act", lambda e: e.activation(out=lbv[:], in_=lbd[:], func=AF.Sigmoid), reads=[B_set], writes=[B_mod])
    P.op("dve", lambda e: e.tensor_scalar(oml[:], lbv[:], -1.0, 1.0, ALU.mult, ALU.add), reads=[B_mod], writes=[B_mod])
    P.op("dve", lambda e: e.tensor_scalar(lbm1[:], lbv[:], 1.0, -1.0, ALU.mult, ALU.add), reads=[B_mod], writes=[B_mod])

    wada_v = wada_d.rearrange("(j p) c -> p j c", p=128)
    for blk in range(24):
        s = blk % 2
        bkm = blk % 2
        P.op("sp", (lambda e, s=s, blk=blk: e.dma_start(out=wa[s][:], in_=wada_v[:, :, blk * 256:(blk + 1) * 256])),
             writes=[B_wa[s]], ndma=1)

        def mm_mod(e, s=s, bkm=bkm):
            return [e.matmul(pb[bkm][0:2, 0:256], ca[:, k, :], wa[s][:, k, :], start=(k == 0), stop=(k == 7)) for k in range(8)]
        P.op("pe", mm_mod, reads=[B_wa[s], B_set], writes=[B_pb[bkm]])
        P.op("act", (lambda e, bkm=bkm, blk=blk: e.copy(modrow[:, blk * 256:(blk + 1) * 256], pb[bkm][0:2, 0:256])),
             reads=[B_pb[bkm]], writes=[B_row])
    pmod = pb[2][:, 0:96]

    def tr_mod(e):
        return [e.transpose(pmod[:, 2 * m:2 * m + 2], modrow[:, m * 128:(m + 1) * 128], ident_f[0:2, 0:2]) for m in range(48)]
    P.op("pe", tr_mod, reads=[B_row, B_const], writes=[B_pb[2]])
    P.op("dve", lambda e: e.tensor_tensor(modT[:], pmod.rearrange("p (m b) -> p m b", b=2),
                                          badaT[:].unsqueeze(2).broadcast_to([128, 48, 2]), ALU.add),
         reads=[B_pb[2], B_set], writes=[B_mod])

    def bc2(v):
        return v[:].unsqueeze(2).broadcast_to([128, 8, 2])
    P.op("dve", lambda e: e.scalar_tensor_tensor(a1[:], modT[:, 8:16, :], 1.0, bc2(premix), ALU.add, ALU.mult),
         reads=[B_mod, B_set], writes=[B_mod])
    P.op("dve", lambda e: e.scalar_tensor_tensor(a2[:], modT[:, 32:40, :], 1.0, bc2(premlp), ALU.add, ALU.mult),
         reads=[B_mod, B_set], writes=[B_mod])
    for gi, c0 in enumerate((2 * D, 5 * D)):
        P.op("dve", (lambda e, gi=gi, c0=c0: e.tensor_tensor(grow[:, gi, :], modrow[:, c0:c0 + D], bgrow[:, gi, :], ALU.add)),
             reads=[B_row, B_set], writes=[B_set])
        P.op("dve", (lambda e, gi=gi: e.tensor_tensor(grow[:, gi, :], grow[:, gi, :], pwrow[:, gi, :], ALU.mult)),
             reads=[B_set], writes=[B_set])
    cnt = 0
    for gi, gvt in enumerate((gv1, gv2)):
        for b in range(2):
            for nh in range(2):
                bkm = 3 + (cnt % 2)
                cnt += 1
                P.op("pe", (lambda e, bkm=bkm, gi=gi, b=b, nh=nh: e.matmul(pb[bkm][:], sel[:, b, :], grow[:, gi, nh * 512:(nh + 1) * 512],
                                                                            start=True, stop=True)),
                     reads=[B_set], writes=[B_pb[bkm]])
                P.op("act", (lambda e, bkm=bkm, gvt=gvt, b=b, nh=nh: e.copy(gvt[b][:, nh * 512:(nh + 1) * 512], pb[bkm][:])),
                     reads=[B_pb[bkm]], writes=[B_mod])
    P.barrier()
    st.close()

    p1 = ExitStack()
    cosS = [P.sb(f"cosS{i}", [128, GW], F32, p1) for i in range(2)]
    sinS = [P.sb(f"sinS{i}", [128, GW], F32, p1) for i in range(2)]
    qraw = [P.sb(f"qraw{i}", [128, GW], BF16, p1) for i in range(2)]
    rt1 = [P.sb(f"rt1_{i}", [128, GW], F32, p1) for i in range(2)]
    rt2 = [P.sb(f"rt2_{i}", [128, GW], F32, p1) for i in range(2)]
    qT = [P.sb(f"qT{i}", [128, 4, GW], BF16, p1) for i in range(2)]
    kT = P.sb("kT", [128, 2, KRING * 128], BF16, p1)
    vaug = P.sb("vaug", [128, KRING, 2, 72], BF16, p1)
    hA = [P.sb(f"hA{i}", [128, GW], F32, p1) for i in range(2)]
    hB = [P.sb(f"hB{i}", [128, GW], F32, p1) for i in range(2)]
    hC = [P.sb(f"hC{i}", [128, GW], F32, p1) for i in range(2)]
    qd = [P.sb(f"qd{i}", [128, 4, GW], BF16, p1) for i in range(2)]
    kd = [P.sb(f"kd{i}", [128, 4, GW], BF16, p1) for i in range(2)]
    dec = [P.sb(f"dec{i}", [128, 4, 2], F32, p1) for i in range(2)]
    vt = [P.sb(f"vt{i}", [128, 512], BF16, p1) for i in range(2)]
    gg = [P.sb(f"gg{i}", [128, 512], F32, p1) for i in range(2)]
    Pe = [[P.sb(f"Pe{i}_{h}", [128, 4, 2, 128], BF16, p1) for h in range(2)] for i in range(2)]
    attn = [P.sb(f"attn{i}", [128, 512], F32, p1) for i in range(2)]
    Am = [P.sb(f"Am{i}", [128, 4, 128], BF16, p1) for i in range(2)]
    kdtok = [P.sb(f"kdtok{i}", [128, 4, 128], BF16, p1) for i in range(2)]
    tmpS = [P.sb(f"tmpS{i}", [128, 4, 128], F32, p1) for i in range(2)]
    sqo = [P.sb(f"sqo{i}", [128, 4, 128], F32, p1) for i in range(2)]
    rect = sqo
    cat = [P.sb(f"cat{i}", [128, D], BF16, p1) for i in range(2)]
    catT = [P.sb(f"catT{i}", [128, 8, 128], BF16, p1) for i in range(2)]
    Sst = P.sb("Sst", [128, 4, 128], F32, p1)
    Sb = P.sb("Sb", [128, 4, 128], BF16, p1)

    B_rope = P.bufs(2, "rope")
    B_qraw = P.bufs(2, "qraw")
    B_rt1 = P.bufs(2, "rt1")
    B_rt2 = P.bufs(2, "rt2")
    B_qT = P.bufs(2, "qT")
    B_kT = P.bufs(KRING, "kT")
    B_va = P.bufs(KRING, "va")
    B_hA = P.bufs(2, "hA")
    B_hB = P.bufs(2, "hB")
    B_hC = P.bufs(2, "hC")
    B_qd = P.bufs(2, "qd")
    B_kd = P.bufs(2, "kd")
    B_dec = P.bufs(2, "dec")
    B_vt = P.bufs(2, "vt")
    B_gg = P.bufs(2, "gg")
    B_Pe = [P.bufs(2, f"Pe{i}_") for i in range(2)]
    B_attn = P.bufs(2, "attn")
    B_Am = P.bufs(2, "Am")
    B_kdtok = P.bufs(2, "kdtok")
    B_tmpS = P.bufs(2, "tmpS")
    B_sqo = P.bufs(2, "sqo")
    B_rect = B_sqo
    B_cat = P.bufs(2, "cat")
    B_catT = P.bufs(2, "catT")
    B_S = P.buf("S")
    B_Sb = P.buf("Sb")

    P.op("pool", lambda e: e.memset(vaug[:], 1.0), writes=B_va)

    pools = {"proj": [0, 1], "o": [4, 5, 6]}
    pcnt = {"proj": 0, "o": 0}

    def bank(pool):
        i = pools[pool][pcnt[pool] % len(pools[pool])]
        pcnt[pool] += 1
        return i

    state = {"av": -1, "s": -1}

    def rstd_ops(ss_ap, lnbuf_ap, out_ap, n, reads, writes):
        P.op("act", lambda e: e.activation(out=lnbuf_ap, in_=ss_ap, func=AF.Ln, scale=1.0 / n, bias=epsc[:, 0:1]),
             reads=reads + [B_const], writes=writes)
        P.op("act", lambda e: e.activation(out=out_ap, in_=lnbuf_ap, func=AF.Exp, scale=-0.5), reads=writes, writes=writes)

    def norm_and_transpose(g, a_mod, sh_lo, src_phase2):
        hs = g % 2
        b = (2 * g) // TPS
        ssx, rsx = (ss2, rs2) if src_phase2 else (ss1, rs1)
        for i in range(2):
            T = 2 * g + i
            xsl = T % 4
            P.op("act", (lambda e, T=T, xsl=xsl: e.activation(out=junk[:], in_=xs[xsl][:], func=AF.Square, accum_out=ssx[:, T:T + 1])),
                 reads=[B_xs[xsl]], writes=[B_stat[T]])
            rstd_ops(ssx[:, T:T + 1], rsx[:, T, 0:1], rsx[:, T, 1:2], D, [B_stat[T]], [B_stat[T]])
            P.op("dve", (lambda e, T=T, xsl=xsl, i=i: e.tensor_scalar(xn[i][:], xs[xsl][:], rsx[:, T, 1:2], None, ALU.mult)),
                 reads=[B_xs[xsl], B_stat[T]], writes=[B_xn[i]])
        yield
        for rnd in range(2):
            def tr(e, rnd=rnd):
                ins = []
                for jj in range(4):
                    j = rnd * 4 + jj
                    for i in range(2):
                        ins.append(e.transpose(ptr[:, jj * GW + i * 128: jj * GW + (i + 1) * 128], xn[i][:, j * 128:(j + 1) * 128], ident_bf[:]))
                return ins
            P.op("pe", tr, reads=[B_xn[0], B_xn[1], B_const], writes=[B_ptr])
            for jj in range(4):
                j = rnd * 4 + jj
                P.op("dve", (lambda e, j=j, jj=jj: e.tensor_scalar(hT[hs][:, j, :], ptr[:, jj * GW:(jj + 1) * GW],
                                                                  a_mod[:, j, b:b + 1], modT[:, sh_lo + j, b:b + 1], ALU.mult, ALU.add)),
                     reads=[B_ptr, B_mod], writes=[B_hT[hs]])
            yield

    def stage_A(g):
        hs = g % 2
        tp0 = (2 * g) % TPS
        for i in range(2):
            T = 2 * g + i
            xsl = T % 4
            P.op("sp", (lambda e, T=T, xsl=xsl: e.dma_start(out=xs[xsl][:], in_=x_d[T * 128:(T + 1) * 128, :])), writes=[B_xs[xsl]], ndma=1)
        P.op("sp", (lambda e: [e.dma_start(out=cosS[hs][:], in_=cosF_d[:, tp0 * 128: tp0 * 128 + GW]),
                               e.dma_start(out=sinS[hs][:], in_=sinF_d[:, tp0 * 128: tp0 * 128 + GW])]), writes=[B_rope[hs]], ndma=2)
        yield from norm_and_transpose(g, a1, 0, False)

        def fm_proj(w_ap_fn, reads_extra=()):
            bk = bank("proj")

            def f(e):
                return [e.matmul(pb[bk][:, 0:GW], w_ap_fn(k), hT[hs][:, k, :], start=(k == 0), stop=(k == 7)) for k in range(8)]
            P.op("pe", f, reads=[B_hT[hs], B_w1], writes=[B_pb[bk]])
            return bk

        for h in range(4):
            s = h % 2
            bk = fm_proj(lambda k, h=h: win[:, k, 1280 + h * 128: 1280 + (h + 1) * 128])
            P.op("act", (lambda e, bk=bk, s=s: e.activation(out=hA[s][:], in_=pb[bk][:, 0:GW], func=AF.Sigmoid)),
                 reads=[B_pb[bk]], writes=[B_hA[s]])
            P.op("act", (lambda e, s=s, h=h: e.activation(out=hB[s][:], in_=hA[s][:], func=AF.Ln, scale=oml[:, h:h + 1], bias=lbv[:, h:h + 1])),
                 reads=[B_hA[s], B_mod], writes=[B_hB[s]])
            P.op("dve", (lambda e, s=s, h=h: e.tensor_scalar(hA[s][:], hA[s][:], lbm1[:, h:h + 1], oml[:, h:h + 1], ALU.mult, ALU.add)),
                 reads=[B_hA[s], B_mod], writes=[B_hA[s]])
            P.op("dve", (lambda e, s=s: e.tensor_tensor_scan(hC[s][:], rmask[:], hB[s][:], 0.0, ALU.mult, ALU.add)),
                 reads=[B_hB[s], B_const], writes=[B_hC[s]])
            P.op("act", (lambda e, s=s: e.activation(out=hB[s][:], in_=hC[s][:], func=AF.Exp, scale=-1.0)),
                 reads=[B_hC[s]], writes=[B_hB[s]])
            P.op("pool", (lambda e, s=s, h=h: e.tensor_tensor(kd[hs][:, h, :], hA[s][:], hB[s][:], ALU.mult)),
                 reads=[B_hA[s], B_hB[s]], writes=[B_kd[hs]])
            P.op("act", (lambda e, s=s: e.activation(out=hB[s][:], in_=hC[s][:], func=AF.Exp)),
                 reads=[B_hC[s]], writes=[B_hB[s]])
            P.op("pool", (lambda e, s=s, h=h: e.tensor_copy(dec[hs][:, h, :], hB[s][:, 127::128])),
                 reads=[B_hB[s]], writes=[B_dec[hs]])
            bk2 = fm_proj(lambda k, h=h: win[:, k, 768 + h * 128: 768 + (h + 1) * 128])
            P.op("act", (lambda e, bk2=bk2, s=s: e.activation(out=hA[s][:], in_=pb[bk2][:, 0:GW], func=AF.Silu)),
                 reads=[B_pb[bk2]], writes=[B_hA[s]])
            P.op("pool", (lambda e, s=s, h=h: e.tensor_tensor(qd[hs][:, h, :], hA[s][:], hB[s][:], ALU.mult)),
                 reads=[B_hA[s], B_hB[s]], writes=[B_qd[hs]])
            yield

        def rope_part(c):
            s = c % 2
            bk2 = bank("proj")
            P.op("pe", (lambda e: e.matmul(pb[bk2][:, 0:GW], pswap_bf[:], qraw[s][:], start=True, stop=True)),
                 reads=[B_qraw[s], B_const], writes=[B_pb[bk2]])
            P.op("dve", (lambda e: e.tensor_tensor(rt1[s][:], pb[bk2][:, 0:GW], sinS[hs][:], ALU.mult)),
                 reads=[B_pb[bk2], B_rope[hs]], writes=[B_rt1[s]])
            P.op("pool", (lambda e: e.tensor_tensor(rt2[s][:], qraw[s][:], cosS[hs][:], ALU.mult)),
                 reads=[B_qraw[s], B_rope[hs]], writes=[B_rt2[s]])
            if c < 4:
                P.op("pool", (lambda e: e.tensor_tensor(qT[hs][:, c, :], rt1[s][:], rt2[s][:], ALU.add)),
                     reads=[B_rt1[s], B_rt2[s]], writes=[B_qT[hs]])
            else:
                kv = c - 4
                r0 = ((2 * g) % KRING)
                P.op("pool", (lambda e: e.tensor_tensor(kT[:, kv, r0 * 128: r0 * 128 + GW], rt1[s][:], rt2[s][:], ALU.add)),
                     reads=[B_rt1[s], B_rt2[s]], writes=[B_kT[r0], B_kT[r0 + 1]])

        for c in range(7):
            if c < 6:
                s = c % 2
                if c < 4:
                    bk = fm_proj(lambda k, c=c: win[:, k, c * 128:(c + 1) * 128])
                else:
                    bk = fm_proj(lambda k, c=c: wkd[:, k, (c - 4) * 128:(c - 3) * 128])
                P.op("act", (lambda e, bk=bk, s=s: e.copy(qraw[s][:], pb[bk][:, 0:GW])), reads=[B_pb[bk]], writes=[B_qraw[s]])
            if c >= 1:
                rope_part(c - 1)
            yield

    def stage_B(T):
        g = T // 2
        i = T % 2
        hs = g % 2
        ts = T % 2
        b = T // TPS
        tp = T % TPS
        xsl = T % 4
        rk = T % KRING
        rkp = (T - 1) % KRING
        tc = slice(i * 128, (i + 1) * 128)
        kbs = (0, 1) if tp > 0 else (1,)

        def tm_proj(c0, n, eng_evac_fn):
            bk = bank("proj")

            def f(e):
                return [e.matmul(pb[bk][:, 0:n], hT[hs][:, k, tc], win[:, k, c0:c0 + n], start=(k == 0), stop=(k == 7)) for k in range(8)]
            P.op("pe", f, reads=[B_hT[hs], B_w1], writes=[B_pb[bk]])
            eng_evac_fn(bk)

        def sc_part(half):
            kv = half

            def f_sc(e):
                ins = []
                for hh in range(4):
                    head = 4 * half + hh
                    c = head // 2
                    pr = slice((head % 2) * 64, (head % 2) * 64 + 64)
                    for kb in kbs:
                        rr = rkp if kb == 0 else rk
                        ins.append(e.matmul(pb[2 + hh % 2][:, ((hh // 2) * 2 + kb) * 128:((hh // 2) * 2 + kb + 1) * 128],
                                            kT[pr, kv, rr * 128:(rr + 1) * 128], qT[hs][pr, c, tc], start=True, stop=True))
                return ins
            rd = [B_qT[hs], B_kT[rk]] + ([B_kT[rkp]] if tp > 0 else [])
            P.op("pe", f_sc, reads=rd, writes=[B_pb[2], B_pb[3]])
            for bb in range(2):
                if tp > 0:
                    P.op("act", (lambda e, bb=bb: e.activation(out=Pe[ts][half][:, bb::2, :, :],
                                                               in_=pb[2 + bb][:].rearrange("p (a b q) -> p a b q", a=2, b=2), func=AF.Exp, scale=0.125)),
                         reads=[B_pb[2 + bb]], writes=[B_Pe[ts][half]])
                else:
                    P.op("act", (lambda e, bb=bb: e.activation(out=Pe[ts][half][:, bb::2, 1, :],
                                                               in_=pb[2 + bb][:].rearrange("p (a b q) -> p a b q", a=2, b=2)[:, :, 1, :],
                                                               func=AF.Exp, scale=0.125)),
                         reads=[B_pb[2 + bb]], writes=[B_Pe[ts][half]])
            if tp > 0:
                P.op("dve", (lambda e: e.tensor_tensor(Pe[ts][half][:], Pe[ts][half][:], mask2_bf[:].unsqueeze(1).broadcast_to([128, 4, 2, 128]), ALU.mult)),
                     reads=[B_Pe[ts][half], B_const], writes=[B_Pe[ts][half]])
            else:
                P.op("dve", (lambda e: e.tensor_tensor(Pe[ts][half][:, :, 1, :], Pe[ts][half][:, :, 1, :],
                                                       mask2_bf[:, 1, :].unsqueeze(1).broadcast_to([128, 4, 128]), ALU.mult)),
                     reads=[B_Pe[ts][half], B_const], writes=[B_Pe[ts][half]])

        def pv_part(half):
            kv = half
            bkO = bank("o")

            def f_pv(e):
                ins = []
                for hh in range(4):
                    for n, kb in enumerate(kbs):
                        rr = rkp if kb == 0 else rk
                        ins.append(e.matmul(pb[bkO][:, hh * 128:hh * 128 + 72], Pe[ts][half][:, hh, kb, :], vaug[:, rr, kv, :],
                                            start=(n == 0), stop=(n == len(kbs) - 1)))
                return ins
            rd = [B_Pe[ts][half], B_va[rk]] + ([B_va[rkp]] if tp > 0 else [])
            P.op("pe", f_pv, reads=rd, writes=[B_pb[bkO]])
            pO = pb[bkO][:].rearrange("p (h d) -> p h d", h=4)
            P.op("dve", (lambda e: e.tensor_tensor(den[:, T, half, :], pO[:, :, 64], esink[:, 4 * half:4 * half + 4], ALU.add)),
                 reads=[B_pb[bkO], B_const], writes=[B_stat[T]])
            P.op("dve", (lambda e: e.reciprocal(rden[:, T, half, :], den[:, T, half, :])), reads=[B_stat[T]], writes=[B_stat[T]])
            P.op("dve", (lambda e: e.tensor_tensor(attn[ts][:, half * 256:(half + 1) * 256].rearrange("p (h d) -> p h d", h=4),
                                                   pO[:, :, 0:64], rden[:, T, half, :].unsqueeze(2).broadcast_to([128, 4, 64]), ALU.mult)),
                 reads=[B_pb[bkO], B_stat[T]], writes=[B_attn[ts]])

        tm_proj(640, 128, lambda bk: P.op("dve", (lambda e: e.tensor_copy(vaug[:, rk, :, 0:64], pb[bk][:, 0:128].rearrange("p (a d) -> p a d", a=2))),
                                          reads=[B_pb[bk]], writes=[B_va[rk]]))
        state["av"] = T
        tm_proj(1792, 512, lambda bk: P.op("act", (lambda e: e.copy(vt[ts][:], pb[bk][:])), reads=[B_pb[bk]], writes=[B_vt[ts]]))

        def ev_g(bk):
            P.op("act", (lambda e: e.activation(out=gg[ts][:], in_=pb[bk][:], func=AF.Silu)), reads=[B_pb[bk]], writes=[B_gg[ts]])
            P.op("pool", (lambda e: e.tensor_tensor(gg[ts][:], gg[ts][:], hgw[:], ALU.mult)), reads=[B_gg[ts], B_const], writes=[B_gg[ts]])
        tm_proj(2304, 512, ev_g)
        bkA = bank("o")

        def f_At(e):
            return [e.matmul(pb[bkA][:, h * 128:(h + 1) * 128], kd[hs][:, h, tc], qd[hs][:, h, tc], start=True, stop=True) for h in range(4)]
        P.op("pe", f_At, reads=[B_kd[hs], B_qd[hs]], writes=[B_pb[bkA]])
        P.op("dve", (lambda e: e.tensor_tensor(Am[ts][:], pb[bkA][:].rearrange("p (h t) -> p h t", h=4),
                                               maskT_f[:].unsqueeze(1).broadcast_to([128, 4, 128]), ALU.mult)),
             reads=[B_pb[bkA], B_const], writes=[B_Am[ts]])

        def f_kt(e):
            return [e.transpose(ptr[:, h * 128:(h + 1) * 128], kd[hs][:, h, tc], ident_bf[:]) for h in range(4)]
        P.op("pe", f_kt, reads=[B_kd[hs], B_const], writes=[B_ptr])
        P.op("act", (lambda e: e.copy(kdtok[ts][:].rearrange("p h d -> p (h d)"), ptr[:, 0:512])), reads=[B_ptr], writes=[B_kdtok[ts]])
        yield
        while tp > 0 and state["av"] < T - 1:
            yield
        sc_part(0)
        yield
        sc_part(1)
        pv_part(0)
        yield
        pv_part(1)
        P.op("act", (lambda e: e.activation(out=junk[:, 0:512], in_=attn[ts][:], func=AF.Square, accum_out=ssa[:, T:T + 1])),
             reads=[B_attn[ts]], writes=[B_stat[T]])
        rstd_ops(ssa[:, T:T + 1], rsa[:, T, 0:1], rsa[:, T, 1:2], 512, [B_stat[T]], [B_stat[T]])
        P.op("dve", (lambda e: e.scalar_tensor_tensor(cat[ts][:, 0:512], attn[ts][:], rsa[:, T, 1:2], aow[:], ALU.mult, ALU.mult)),
             reads=[B_attn[ts], B_stat[T], B_const], writes=[B_cat[ts]])
        yield

        while state["s"] < T - 1:
            yield
        if tp == 0:
            P.op("pool", lambda e: e.memset(Sst[:], 0.0), writes=[B_S])
            P.op("pool", lambda e: e.memset(Sb[:], 0.0), writes=[B_Sb])
        bko = bank("o")

        def f_o(e):
            ins = []
            for h in range(4):
                ins.append(e.matmul(pb[bko][:, h * 128:(h + 1) * 128], Am[ts][:, h, :], vt[ts][:, h * 128:(h + 1) * 128], start=True, stop=False))
                ins.append(e.matmul(pb[bko][:, h * 128:(h + 1) * 128], qd[hs][:, h, tc], Sb[:, h, :], start=False, stop=True))
            return ins
        P.op("pe", f_o, reads=[B_Am[ts], B_vt[ts], B_qd[hs], B_Sb], writes=[B_pb[bko]])
        bkK = bank("o")

        def f_kv(e):
            return [e.matmul(pb[bkK][:, h * 128:(h + 1) * 128], kdtok[ts][:, h, :], vt[ts][:, h * 128:(h + 1) * 128], start=True, stop=True)
                    for h in range(4)]
        P.op("pe", f_kv, reads=[B_kdtok[ts], B_vt[ts]], writes=[B_pb[bkK]])
        decb = dec[hs][:, :, i:i + 1].broadcast_to([128, 4, 128])
        P.op("dve", (lambda e: e.tensor_tensor(tmpS[ts][:], pb[bkK][:].rearrange("p (h d) -> p h d", h=4), Sst[:], ALU.add)),
             reads=[B_pb[bkK], B_S], writes=[B_tmpS[ts]])
        P.op("dve", (lambda e: e.tensor_tensor(Sb[:], tmpS[ts][:], decb, ALU.mult)), reads=[B_tmpS[ts], B_dec[hs]], writes=[B_Sb])
        P.op("pool", (lambda e: e.tensor_tensor(Sst[:], tmpS[ts][:], decb, ALU.mult)), reads=[B_tmpS[ts], B_dec[hs]], writes=[B_S])
        state["s"] = T
        P.op("act", (lambda e: e.activation(out=sqo[ts][:].rearrange("p h d -> p (h d)"), in_=pb[bko][:], func=AF.Square)),
             reads=[B_pb[bko]], writes=[B_sqo[ts]])
        P.op("dve", (lambda e: e.tensor_reduce(ssr[:, T, :], sqo[ts][:], AX.X, ALU.add)), reads=[B_sqo[ts]], writes=[B_stat[T]])
        rstd_ops(ssr[:, T, :], rsr[:, T, 0:4], rsr[:, T, 4:8], 128, [B_stat[T]], [B_stat[T]])
        P.op("dve", (lambda e: e.tensor_tensor(rect[ts][:], pb[bko][:].rearrange("p (h d) -> p h d", h=4),
                                               rsr[:, T, 4:8].unsqueeze(2).broadcast_to([128, 4, 128]), ALU.mult)),
             reads=[B_pb[bko], B_stat[T]], writes=[B_rect[ts]])
        P.op("pool", (lambda e: e.tensor_tensor(cat[ts][:, 512:1024], rect[ts][:].rearrange("p h d -> p (h d)"), gg[ts][:], ALU.mult)),
             reads=[B_rect[ts], B_gg[ts]], writes=[B_cat[ts]])
        yield

        def f_ct(e):
            return [e.transpose(ptr[:, j * 128:(j + 1) * 128], cat[ts][:, j * 128:(j + 1) * 128], ident_bf[:]) for j in range(8)]
        P.op("pe", f_ct, reads=[B_cat[ts], B_const], writes=[B_ptr])
        P.op("act", (lambda e: e.copy(catT[ts][:].rearrange("p j t -> p (j t)"), ptr[:, :])), reads=[B_ptr], writes=[B_catT[ts]])
        yield

        def f_mix(e):
            ins = []
            for nh in range(2):
                for k in range(8):
                    ins.append(e.matmul(pb[2 + nh][:], catT[ts][:, k, :], wout[:, k, nh * 512:(nh + 1) * 512], start=(k == 0), stop=(k == 7)))
            return ins
        P.op("pe", f_mix, reads=[B_catT[ts], B_w1], writes=[B_pb[2], B_pb[3]])
        for nh in range(2):
            P.op("act", (lambda e, nh=nh: e.activation(out=junk[:, 0:512], in_=pb[2 + nh][:], func=AF.Square, accum_out=ssm[:, T, nh:nh + 1])),
                 reads=[B_pb[2 + nh]], writes=[B_stat[T]])
        P.op("dve", (lambda e: e.tensor_tensor(ssm[:, T, 2:3], ssm[:, T, 0:1], ssm[:, T, 1:2], ALU.add)), reads=[B_stat[T]], writes=[B_stat[T]])
        rstd_ops(ssm[:, T, 2:3], ssm[:, T, 3:4], ssm[:, T, 0:1], D, [B_stat[T]], [B_stat[T]])
        for nh in range(2):
            P.op("dve", (lambda e, nh=nh: e.scalar_tensor_tensor(tmpx[ts][:, nh * 512:(nh + 1) * 512], pb[2 + nh][:], ssm[:, T, 0:1],
                                                                 gv1[b][:, nh * 512:(nh + 1) * 512], ALU.mult, ALU.mult)),
                 reads=[B_pb[2 + nh], B_stat[T], B_mod], writes=[B_tmpx[ts]])
        P.op("pool", (lambda e: e.tensor_tensor(xs[xsl][:], tmpx[ts][:], xs[xsl][:], ALU.add)), reads=[B_tmpx[ts], B_xs[xsl]], writes=[B_xs[xsl]])
        P.op("sp", (lambda e: e.dma_start(out=x1_d[T * 128:(T + 1) * 128, :], in_=xs[xsl][:])), reads=[B_xs[xsl]], writes=[B_x1d[T]], ndma=1)
        yield

    if stop >= 2:
        for _ in stage_A(0):
            pass
    for g in range(ngroups if stop >= 3 else 0):
        gens = [stage_B(2 * g), stage_B(2 * g + 1)]
        if g + 1 < ngroups:
            gens.append(stage_A(g + 1))
        run_interleaved(gens)

    P.barrier()
    p1.close()
    pw.close()
    pp.close()

    p2 = ExitStack()
    wup = P.sb("wup", [128, 8, DFF], BF16, p2)
    wdn = P.sb("wdn", [128, 32, D], BF16, p2)
    rl = [P.sb(f"rl{i}", [128, 512], BF16, p2) for i in range(2)]
    uT = P.sb("uT", [128, 32, GW], BF16, p2)
    B_wup = P.bufs(4, "wup")
    B_wdn = P.bufs(8, "wdn")
    B_rl = P.bufs(2, "rl")
    B_uT = P.bufs(16, "uT")

    wup_v = wup_d.rearrange("(j p) c -> p j c", p=128)
    wdn_v = wdn_d.rearrange("(j p) c -> p j c", p=128)
    for q in range(4 if stop >= 4 else 0):
        P.op("pool", (lambda e, q=q: e.dma_start(out=wup[:, :, q * 1024:(q + 1) * 1024], in_=wup_v[:, :, q * 1024:(q + 1) * 1024])),
             writes=[B_wup[q]], ndma=1)
    for j4 in range(8 if stop >= 4 else 0):
        P.op("pool", (lambda e, j4=j4: e.dma_start(out=wdn[:, 4 * j4:4 * j4 + 4, :], in_=wdn_v[:, 4 * j4:4 * j4 + 4, :])), writes=[B_wdn[j4]], ndma=1)

    up_banks = [4, 5, 6]
    upc = [0]
    down_units = [(0, 1), (2, 3)]

    def mlp_prep(g):
        for i in range(2):
            T = 2 * g + i
            xsl = T % 4
            P.op("sp", (lambda e, T=T, xsl=xsl: e.dma_start(out=xs[xsl][:], in_=x1_d[T * 128:(T + 1) * 128, :])),
                 reads=[B_x1d[T]], writes=[B_xs[xsl]], ndma=1)
        yield from norm_and_transpose(g, a2, 24, True)

    def mlp_main(g):
        hs = g % 2
        for cp in range(16):
            bk = up_banks[upc[0] % 3]
            upc[0] += 1
            s = cp % 2

            def f_up(e, cp=cp, bk=bk):
                ins = []
                for cc in range(2):
                    c = 2 * cp + cc
                    for k in range(8):
                        ins.append(e.matmul(pb[bk][:, cc * GW:(cc + 1) * GW], wup[:, k, c * 128:(c + 1) * 128], hT[hs][:, k, :],
                                            start=(k == 0), stop=(k == 7)))
                return ins
            P.op("pe", f_up, reads=[B_hT[hs], B_wup[cp // 4]], writes=[B_pb[bk]])
            P.op("act", (lambda e, bk=bk, s=s: e.activation(out=rl[s][:], in_=pb[bk][:], func=AF.Relu)), reads=[B_pb[bk]], writes=[B_rl[s]])
            P.op("pool", (lambda e, s=s, cp=cp: e.tensor_tensor(uT[:, 2 * cp:2 * cp + 2, :].rearrange("p a t -> p (a t)"), rl[s][:], rl[s][:], ALU.mult)),
                 reads=[B_rl[s]], writes=[B_uT[cp]])
            if cp % 4 == 3:
                yield
        for i in range(2):
            T = 2 * g + i
            xsl = T % 4
            b = T // TPS
            ts = T % 2
            du = down_units[i]

            for cb in range(4):
                def f_dn(e, i=i, du=du, cb=cb):
                    ins = []
                    for nh in range(2):
                        for c in range(8 * cb, 8 * cb + 8):
                            ins.append(e.matmul(pb[du[nh]][:], uT[:, c, i * 128:(i + 1) * 128], wdn[:, c, nh * 512:(nh + 1) * 512],
                                                start=(c == 0), stop=(c == 31)))
                    return ins
                P.op("pe", f_dn, reads=B_uT[4 * cb:4 * cb + 4] + [B_wdn[2 * cb], B_wdn[2 * cb + 1]], writes=[B_pb[du[0]], B_pb[du[1]]])
            for nh in range(2):
                P.op("act", (lambda e, nh=nh, du=du, T=T: e.activation(out=junk[:, 0:512], in_=pb[du[nh]][:], func=AF.Square, accum_out=ssy[:, T, nh:nh + 1])),
                     reads=[B_pb[du[nh]]], writes=[B_stat[T]])
            P.op("dve", (lambda e, T=T: e.tensor_tensor(ssy[:, T, 2:3], ssy[:, T, 0:1], ssy[:, T, 1:2], ALU.add)), reads=[B_stat[T]], writes=[B_stat[T]])
            rstd_ops(ssy[:, T, 2:3], ssy[:, T, 3:4], ssy[:, T, 0:1], D, [B_stat[T]], [B_stat[T]])
            for nh in range(2):
                P.op("dve", (lambda e, nh=nh, du=du, T=T, ts=ts, b=b: e.scalar_tensor_tensor(
                    tmpx[ts][:, nh * 512:(nh + 1) * 512], pb[du[nh]][:], ssy[:, T, 0:1], gv2[b][:, nh * 512:(nh + 1) * 512], ALU.mult, ALU.mult)),
                    reads=[B_pb[du[nh]], B_stat[T], B_mod], writes=[B_tmpx[ts]])
            P.op("pool", (lambda e, ts=ts, xsl=xsl: e.tensor_tensor(xs[xsl][:], tmpx[ts][:], xs[xsl][:], ALU.add)),
                 reads=[B_tmpx[ts], B_xs[xsl]], writes=[B_xs[xsl]])
            P.op("sp", (lambda e, T=T, xsl=xsl: e.dma_start(out=out_d[T * 128:(T + 1) * 128, :], in_=xs[xsl][:])),
                 reads=[B_xs[xsl]], writes=[B_out[T]], ndma=1)
            yield

    n2 = ngroups if stop >= 5 else 0
    if n2:
        for _ in mlp_prep(0):
            pass
    for g in range(n2):
        gens = [mlp_main(g)]
        if g + 1 < n2:
            gens.append(mlp_prep(g + 1))
        run_interleaved(gens)

    P.op("sp", None, reads=B_out)
    p2.close()
    P.emit()
    return nc


_NC_CACHE = {}


def _layout_inputs(inp):
    f = lambda a: np.ascontiguousarray(np.asarray(a, dtype=np.float32))
    consts = _host_consts()
    colT = lambda v, n: f(np.asarray(v).reshape(n, 128).T)
    shared = {
        "w_ada": f(inp["w_ada"][0]),
        "b_adaT": colT(inp["b_ada"][0], 48),
        "pre_mixT": colT(inp["pre_w_mix"][0], 8),
        "pre_mlpT": colT(inp["pre_w_mlp"][0], 8),
        "bgrow": f(np.broadcast_to(np.stack([np.asarray(inp["b_ada"][0])[2 * D:3 * D], np.asarray(inp["b_ada"][0])[5 * D:6 * D]])[None], (2, 2, D))),
        "pwrow": f(np.broadcast_to(np.stack([np.asarray(inp["post_w_mix"][0]), np.asarray(inp["post_w_mlp"][0])])[None], (2, 2, D))),
        "w_in": f(inp["w_in"][0]),
        "w_out": f(inp["w_out"][0]),
        "w_up": f(inp["w_up"][0]),
        "w_down": f(inp["w_down"][0]),
        "sinks_rep": f(np.broadcast_to(np.asarray(inp["attn_sinks"][0])[None, :], (128, 8))),
        "aow_rep": f(np.broadcast_to(np.asarray(inp["attn_out_w"][0])[None, :], (128, 512))),
        "hgw_rep": f(np.broadcast_to(np.tile(np.asarray(inp["hg_norm_w"][0]), 4)[None, :], (128, 512))),
        "lbT": f(np.asarray(inp["lb_table"]).reshape(2, 4, 128).transpose(2, 0, 1)),
    }
    shared.update(consts)
    x = np.asarray(inp["x"], dtype=np.float32)
    c = np.asarray(inp["c"], dtype=np.float32)
    maps = []
    for i in range(NCORES):
        m = dict(shared)
        m["x"] = f(x[NSEQ * i:NSEQ * (i + 1)].reshape(TOK, D))
        cc = c[NSEQ * i:NSEQ * (i + 1)]
        m["cT"] = f(cc.reshape(2, 8, 128).transpose(2, 1, 0))
        maps.append(m)
    return maps


def kernel(**inputs):
    if "nc" not in _NC_CACHE:
        _NC_CACHE["nc"] = build_program()
    nc = _NC_CACHE["nc"]
    maps = _layout_inputs(inputs)
    res = run_bass_kernel_spmd(nc, maps, core_ids=list(range(NCORES)))
    outs = [np.asarray(r["out"], dtype=np.float32).reshape(NSEQ, SEQ, D) for r in res.results]
    return np.concatenate(outs, axis=0)
```

```python
import numpy as np
from contextlib import ExitStack
import concourse.bass as bass
import concourse.mybir as mybir
from concourse.bass_utils import run_bass_kernel_spmd

F32 = mybir.dt.float32
BF16 = mybir.dt.bfloat16
AF = mybir.ActivationFunctionType
ALU = mybir.AluOpType
AX = mybir.AxisListType

NCORES = 8
D = 1024
SEQ = 2048
NSEQ = 2
TOK = NSEQ * SEQ
NT = TOK // 128
TPS = SEQ // 128
NG = NT // 2
GW = 256
DFF = 4096
INC = 2816
EPS = 1e-6
KRING = 6
import os as _os
BSTOP = float(_os.environ.get('KBSTOP', '99'))
KDELAY = int(_os.environ.get('KDELAY', '3'))
KWA = int(_os.environ.get('KWA', '1'))


class Buf:
    __slots__ = ("name", "last_w", "readers")

    def __init__(self, name):
        self.name = name
        self.last_w = None
        self.readers = []


class Op:
    __slots__ = ("eng", "fn", "deps", "idx", "ndma", "signal", "semval", "sem", "name", "prevval")


class Prog:
    ENGS = ("pe", "act", "dve", "pool", "sp")
    NDMASEM = 24

    def __init__(self, nc):
        self.nc = nc
        self.streams = {e: [] for e in self.ENGS}
        self.es = ExitStack()
        self.nbuf = 0
        self.dma_since_barrier = []

    def sb(self, name, shape, dt, stack=None):
        return (stack or self.es).enter_context(self.nc.sbuf_tensor("s_" + name, list(shape), dt))

    def ps(self, name, shape, dt):
        return self.es.enter_context(self.nc.psum_tensor("p_" + name, list(shape), dt))

    def buf(self, name=None):
        self.nbuf += 1
        return Buf(name or f"b{self.nbuf}")

    def bufs(self, n, name="b"):
        return [self.buf(f"{name}{i}") for i in range(n)]

    def op(self, eng, fn, reads=(), writes=(), ndma=0, name=None):
        o = Op()
        o.eng = eng
        o.fn = fn
        o.ndma = ndma
        o.signal = False
        o.semval = None
        o.sem = None
        o.name = name
        o.prevval = 0
        deps = []
        for b in reads:
            if b.last_w is not None:
                deps.append(b.last_w)
        for b in writes:
            if b.last_w is not None:
                deps.append(b.last_w)
            deps.extend(b.readers)
        seen = set()
        dd = []
        for d in deps:
            if id(d) in seen:
                continue
            seen.add(id(d))
            if eng == "pe" and d.eng == "pe" and d.ndma == 0:
                continue
            dd.append(d)
        o.deps = dd
        o.idx = len(self.streams[eng])
        self.streams[eng].append(o)
        for b in reads:
            b.readers.append(o)
        for b in writes:
            b.last_w = o
            b.readers = []
        if ndma > 0:
            self.dma_since_barrier.append(o)
        return o

    def barrier(self):
        lasts = {}
        for e in self.ENGS:
            for o in reversed(self.streams[e]):
                if o.ndma == 0 and o.fn is not None:
                    lasts[e] = o
                    break
        dmas = list(self.dma_since_barrier)
        self.dma_since_barrier = []
        for e in self.ENGS:
            o = Op()
            o.eng = e
            o.fn = None
            o.ndma = 0
            o.signal = False
            o.semval = None
            o.sem = None
            o.name = "barrier"
            o.prevval = 0
            o.deps = [lasts[x] for x in lasts if x != e] + dmas
            o.idx = len(self.streams[e])
            self.streams[e].append(o)

    def emit(self):
        nc = self.nc
        for e in self.ENGS:
            for o in self.streams[e]:
                for d in o.deps:
                    d.signal = True
        for e in self.ENGS:
            cnt = 0
            for o in self.streams[e]:
                if o.ndma == 0 and o.signal:
                    cnt += 1
                    o.semval = cnt
        dcount = [0] * self.NDMASEM
        dpool = {"sp": list(range(0, 16)), "pool": list(range(16, 24))}
        for e in self.ENGS:
            rr = 0
            for o in self.streams[e]:
                if o.ndma > 0:
                    s = dpool[e][rr % len(dpool[e])]
                    rr += 1
                    o.sem = s
                    o.prevval = dcount[s]
                    dcount[s] += 16 * o.ndma
                    o.semval = dcount[s]
        es = self.es
        esem = {e: es.enter_context(nc.semaphore(f"tl_{e}")) for e in self.ENGS}
        dsem = [es.enter_context(nc.semaphore(f"dma_{i}")) for i in range(self.NDMASEM)]
        block = es.enter_context(nc.Block())
        streams = self.streams

        def run_stream(e, engobj):
            seen = {}
            for o in streams[e]:
                waits = []
                for d in o.deps:
                    if d.ndma > 0:
                        key = ("d", d.sem)
                        sem = dsem[d.sem]
                    else:
                        key = ("e", d.eng)
                        sem = esem[d.eng]
                    v = d.semval
                    if seen.get(key, 0) >= v:
                        continue
                    waits.append((key, sem, v))
                if o.ndma > 0:
                    key = ("d", o.sem)
                    if o.prevval > 0 and seen.get(key, 0) < o.prevval:
                        waits.append((key, dsem[o.sem], o.prevval))
                best = {}
                for key, sem, v in waits:
                    if key not in best or best[key][1] < v:
                        best[key] = (sem, v)
                wl = list(best.items())
                for key, (sem, v) in wl:
                    seen[key] = v
                if o.ndma > 0:
                    for key, (sem, v) in wl:
                        engobj.wait_ge(sem, v)
                    ins = o.fn(engobj)
                    if not isinstance(ins, (list, tuple)):
                        ins = [ins]
                    assert len(ins) == o.ndma, (o.name, len(ins), o.ndma)
                    for i in ins:
                        i.then_inc(dsem[o.sem], 16)
                elif o.fn is None:
                    for key, (sem, v) in wl:
                        engobj.wait_ge(sem, v)
                else:
                    for key, (sem, v) in wl[1:]:
                        engobj.wait_ge(sem, v)
                    ins = o.fn(engobj)
                    if not isinstance(ins, (list, tuple)):
                        ins = [ins]
                    if wl:
                        ins[0]._wait_ge(wl[0][1][0], wl[0][1][1])
                    if o.signal:
                        ins[-1].then_inc(esem[e], 1)

        @block.tensor
        def _(eng):
            run_stream("pe", eng)

        @block.scalar
        def _(eng):
            run_stream("act", eng)

        @block.vector
        def _(eng):
            run_stream("dve", eng)

        @block.gpsimd
        def _(eng):
            run_stream("pool", eng)

        @block.sync
        def _(eng):
            run_stream("sp", eng)

        es.close()


def run_interleaved(gens, weights=None):
    gens = list(gens)
    weights = list(weights) if weights else [1] * len(gens)
    live = list(range(len(gens)))
    while live:
        for gi in list(live):
            for _ in range(weights[gi]):
                try:
                    next(gens[gi])
                except StopIteration:
                    live.remove(gi)
                    break


def _host_consts():
    c = {}
    c["ident"] = np.eye(128, dtype=np.float32)
    c["ones"] = np.ones((128, 128), dtype=np.float32)
    sel = np.zeros((2, 2, 128), dtype=np.float32)
    sel[0, 0, :] = 1.0
    sel[1, 1, :] = 1.0
    c["sel"] = sel
    k = np.arange(128)[:, None]
    q = np.arange(128)[None, :]
    cur = (k <= q).astype(np.float32)
    prev = (k > q).astype(np.float32)
    c["mask2"] = np.ascontiguousarray(np.stack([prev, cur], axis=1).reshape(128, 256))
    c["maskT"] = np.ascontiguousarray(cur)
    rm = np.ones((128, 2 * GW), dtype=np.float32)
    rm[:, 0::128] = 0.0
    c["rmask"] = rm
    ps = np.zeros((128, 128), dtype=np.float32)
    for m in range(128):
        d = m % 64
        base = m - d
        if d < 8:
            s = base + d + 8
        elif d < 16:
            s = base + d - 8
        else:
            s = m
        ps[s, m] = 1.0
    c["pswap"] = ps
    inv_freq = (np.float32(500000.0) ** (-np.arange(0, 16, 2, dtype=np.float32) / np.float32(16))).astype(np.float32)
    ang = (np.arange(SEQ, dtype=np.float32)[:, None] * inv_freq[None, :]).astype(np.float32)
    cs = np.cos(ang).astype(np.float32)
    sn = np.sin(ang).astype(np.float32)
    cosF = np.ones((128, SEQ), dtype=np.float32)
    sinF = np.zeros((128, SEQ), dtype=np.float32)
    for p in range(128):
        d = p % 64
        if d < 8:
            cosF[p] = cs[:, d]
            sinF[p] = -sn[:, d]
        elif d < 16:
            cosF[p] = cs[:, d - 8]
            sinF[p] = sn[:, d - 8]
    c["cosF"] = cosF
    c["sinF"] = sinF
    return c


def build_program(stop=99, ngroups=NG):
    nc = bass.Bass("TRN2", target_bir_lowering=False)
    P = Prog(nc)

    def din(name, shape):
        return nc.dram_tensor(name, list(shape), F32, kind="ExternalInput").ap()

    x_d = din("x", [TOK, D])
    cT_d = din("cT", [128, 8, 2])
    wada_d = din("w_ada", [D, 6 * D])
    bada_d = din("b_adaT", [128, 48])
    premix_d = din("pre_mixT", [128, 8])
    premlp_d = din("pre_mlpT", [128, 8])
    win_d = din("w_in", [D, INC])
    wout_d = din("w_out", [D, D])
    wup_d = din("w_up", [D, DFF])
    wdn_d = din("w_down", [DFF, D])
    sinks_d = din("sinks_rep", [128, 8])
    aow_d = din("aow_rep", [128, 512])
    hgw_d = din("hgw_rep", [128, 512])
    lbT_d = din("lbT", [128, 2, 4])
    bgrow_d = din("bgrow", [2, 2, D])
    pwrow_d = din("pwrow", [2, 2, D])
    sel_d = din("sel", [2, 2, 128])
    ident_d = din("ident", [128, 128])
    ones_d = din("ones", [128, 128])
    mask2_d = din("mask2", [128, 256])
    maskT_d = din("maskT", [128, 128])
    rmask_d = din("rmask", [128, 2 * GW])
    pswap_d = din("pswap", [128, 128])
    cosF_d = din("cosF", [128, SEQ])
    sinF_d = din("sinF", [128, SEQ])
    out_d = nc.dram_tensor("out", [TOK, D], F32, kind="ExternalOutput").ap()
    x1_d = nc.dram_tensor("x1_scratch", [TOK, D], F32).ap()

    ident_bf = P.sb("ident_bf", [128, 128], BF16)
    ident_f = P.sb("ident_f", [128, 2], F32)
    ones_f = P.sb("ones_f", [128, 1], F32)
    pswap_bf = P.sb("pswap_bf", [128, 128], BF16)
    mask2_bf = P.sb("mask2_bf", [128, 2, 128], BF16)
    maskT_f = P.sb("maskT_f", [128, 128], F32)
    rmask = P.sb("rmask", [128, 2 * GW], F32)
    epsc = P.sb("epsc", [128, 1], F32)
    esink = P.sb("esink", [128, 8], F32)
    modT = P.sb("modT", [128, 48, 2], F32)
    a1 = P.sb("a1", [128, 8, 2], F32)
    a2 = P.sb("a2", [128, 8, 2], F32)
    lbv = P.sb("lbv", [128, 4], F32)
    oml = P.sb("oml", [128, 4], F32)
    lbm1 = P.sb("lbm1", [128, 4], F32)
    gv2 = [P.sb(f"gv2_{b}", [128, D], F32) for b in range(2)]
    ss1 = P.sb("ss1", [128, NT], F32)
    rs1 = P.sb("rs1", [128, NT, 2], F32)
    ssa = P.sb("ssa", [128, NT], F32)
    rsa = P.sb("rsa", [128, NT, 2], F32)
    ssm = P.sb("ssm", [128, NT, 4], F32)
    ssr = P.sb("ssr", [128, NT, 4], F32)
    rsr = P.sb("rsr", [128, NT, 8], F32)
    den = P.sb("den", [128, NT, 2, 4], F32)
    rden = P.sb("rden", [128, NT, 2, 4], F32)
    ss2 = P.sb("ss2", [128, NT], F32)
    rs2 = P.sb("rs2", [128, NT, 2], F32)
    ssy = P.sb("ssy", [128, NT, 4], F32)
    xs = [P.sb(f"xs{i}", [128, D], F32) for i in range(4)]
    xn = [P.sb(f"xn{i}", [128, D], BF16) for i in range(2)]
    junk = P.sb("junk", [128, D], BF16)
    hT = [P.sb(f"hT{i}", [128, 8, GW], BF16) for i in range(2)]
    _tmpx = P.sb("tmpx", [128, D], F32)
    tmpx = [_tmpx, _tmpx]

    pp = ExitStack()
    aow = P.sb("aow", [128, 512], F32, pp)
    hgw = P.sb("hgw", [128, 512], F32, pp)
    gv1 = [P.sb(f"gv1_{b}", [128, D], F32, pp) for b in range(2)]

    pb = [P.ps(f"pb{i}", [128, 512], F32) for i in range(7)]
    ptr = P.ps("ptr", [128, 1024], BF16)
    B_pb = P.bufs(7, "pb")
    B_ptr = P.buf("ptr")

    B_const = P.buf("const")
    B_mod = P.buf("mod")
    B_xs = P.bufs(4, "xs")
    B_xn = P.bufs(2, "xn")
    B_junk = P.buf("junk")
    B_hT = P.bufs(2, "hT")
    _B_tmpx = P.buf("tmpx")
    B_tmpx = [_B_tmpx, _B_tmpx]
    B_stat = [P.buf(f"stat{t}") for t in range(NT)]
    B_x1d = [P.buf(f"x1d{t}") for t in range(NT)]
    B_out = [P.buf(f"outd{t}") for t in range(NT)]

    def ld(eng, dst, src, b=B_const, n=1):
        P.op(eng, lambda e: e.dma_start(out=dst, in_=src), writes=[b], ndma=n)

    ld("pool", ident_bf[:], ident_d[:, :])
    P.op("pool", lambda e: e.memset(ident_f[:], 0.0), writes=[B_const])
    P.op("pool", lambda e: e.memset(ident_f[0:1, 0:1], 1.0), writes=[B_const])
    P.op("sp", lambda e: e.dma_start(out=ident_f[1:2, 0:2], in_=ident_d[1:2, 0:2]), writes=[B_const], ndma=1)
    P.op("pool", lambda e: e.memset(ones_f[:], 1.0), writes=[B_const])
    ld("pool", pswap_bf[:], pswap_d[:, :])
    ld("pool", mask2_bf[:].rearrange("p a b -> p (a b)"), mask2_d[:, :])
    ld("sp", maskT_f[:], maskT_d[:, :])
    ld("sp", rmask[:], rmask_d[:, :])
    ld("sp", esink[:], sinks_d[:, :])
    ld("sp", aow[:], aow_d[:, :])
    ld("sp", hgw[:], hgw_d[:, :])
    P.op("pool", lambda e: e.memset(epsc[:], EPS), writes=[B_const])
    P.op("act", lambda e: e.activation(out=esink[:], in_=esink[:], func=AF.Exp), reads=[B_const], writes=[B_const])

    pw = ExitStack()
    win = P.sb("win", [128, 8, INC], BF16, pw)
    wkd = P.sb("wkd", [128, 8, 256], BF16, pw)
    wout = P.sb("wout", [128, 8, D], BF16, pw)
    B_w1 = P.buf("w1")
    win_v = win_d.rearrange("(j p) c -> p j c", p=128)
    wout_v = wout_d.rearrange("(j p) c -> p j c", p=128)
    for j in range(8):
        P.op("pool", (lambda e, j=j: e.dma_start(out=win[:, j, :], in_=win_v[:, j, :])), writes=[B_w1], ndma=1)
    for kv in range(2):
        for r in range(2):
            c0 = (2 * kv + r) * 64
            P.op("pool", (lambda e, kv=kv, c0=c0: e.dma_start(out=wkd[:, :, c0:c0 + 64], in_=win_v[:, :, 512 + 64 * kv:576 + 64 * kv])),
                 writes=[B_w1], ndma=1)
    for j in range(8):
        P.op("pool", (lambda e, j=j: e.dma_start(out=wout[:, j, :], in_=wout_v[:, j, :])), writes=[B_w1], ndma=1)

    st = ExitStack()
    cTs = P.sb("cTs", [128, 8, 2], F32, st)
    ca = P.sb("ca", [128, 8, 2], F32, st)
    badaT = P.sb("badaT", [128, 48], F32, st)
    premix = P.sb("premix", [128, 8], F32, st)
    premlp = P.sb("premlp", [128, 8], F32, st)
    lbT = P.sb("lbT", [128, 2, 4], F32, st)
    lbd = P.sb("lbd", [128, 4], F32, st)
    wa = [P.sb(f"wa{i}", [128, 8, 256], F32, st) for i in range(2)]
    modrow = P.sb("modrow", [2, 6 * D], F32, st)
    bgrow = P.sb("bgrow", [2, 2, D], F32, st)
    pwrow = P.sb("pwrow", [2, 2, D], F32, st)
    grow = P.sb("grow", [2, 2, D], F32, st)
    sel = P.sb("sel", [2, 2, 128], F32, st)
    B_wa = P.bufs(2, "wa")
    B_set = P.buf("setup")
    B_row = P.buf("modrow")

    ld("sp", cTs[:], cT_d[:, :, :], B_set)
    ld("sp", badaT[:], bada_d[:, :], B_set)
    ld("sp", premix[:], premix_d[:, :], B_set)
    ld("sp", premlp[:], premlp_d[:, :], B_set)
    ld("sp", lbT[:], lbT_d[:, :, :], B_set)
    ld("sp", bgrow[:], bgrow_d[:, :, :], B_set)
    ld("sp", pwrow[:], pwrow_d[:, :, :], B_set)
    ld("sp", sel[:], sel_d[:, :, :], B_set)
    P.op("act", lambda e: e.activation(out=ca[:], in_=cTs[:], func=AF.Silu), reads=[B_set], writes=[B_set])
    P.op("dve", lambda e: e.tensor_tensor(lbd[:], lbT[:, 1, :], lbT[:, 0, :], ALU.subtract), reads=[B_set], writes=[B_set])
    P.op("act", lambda e: e.activation(out=lbv[:], in_=lbd[:], func=AF.Sigmoid), reads=[B_set], writes=[B_mod])
    P.op("dve", lambda e: e.tensor_scalar(oml[:], lbv[:], -1.0, 1.0, ALU.mult, ALU.add), reads=[B_mod], writes=[B_mod])
    P.op("dve", lambda e: e.tensor_scalar(lbm1[:], lbv[:], 1.0, -1.0, ALU.mult, ALU.add), reads=[B_mod], writes=[B_mod])

    wada_v = wada_d.rearrange("(j p) c -> p j c", p=128)
    for blk in range(24):
        s = blk % 2
        bkm = blk % 2
        P.op("sp", (lambda e, s=s, blk=blk: e.dma_start(out=wa[s][:], in_=wada_v[:, :, blk * 256:(blk + 1) * 256])),
             writes=[B_wa[s]], ndma=1)

        def mm_mod(e, s=s, bkm=bkm):
            return [e.matmul(pb[bkm][0:2, 0:256], ca[:, k, :], wa[s][:, k, :], start=(k == 0), stop=(k == 7)) for k in range(8)]
        P.op("pe", mm_mod, reads=[B_wa[s], B_set], writes=[B_pb[bkm]])
        P.op("act", (lambda e, bkm=bkm, blk=blk: e.copy(modrow[:, blk * 256:(blk + 1) * 256], pb[bkm][0:2, 0:256])),
             reads=[B_pb[bkm]], writes=[B_row])
    pmod = pb[2][:, 0:96]

    def tr_mod(e):
        return [e.transpose(pmod[:, 2 * m:2 * m + 2], modrow[:, m * 128:(m + 1) * 128], ident_f[0:2, 0:2]) for m in range(48)]
    P.op("pe", tr_mod, reads=[B_row, B_const], writes=[B_pb[2]])
    P.op("dve", lambda e: e.tensor_tensor(modT[:], pmod.rearrange("p (m b) -> p m b", b=2),
                                          badaT[:].unsqueeze(2).broadcast_to([128, 48, 2]), ALU.add),
         reads=[B_pb[2], B_set], writes=[B_mod])

    def bc2(v):
        return v[:].unsqueeze(2).broadcast_to([128, 8, 2])
    P.op("dve", lambda e: e.scalar_tensor_tensor(a1[:], modT[:, 8:16, :], 1.0, bc2(premix), ALU.add, ALU.mult),
         reads=[B_mod, B_set], writes=[B_mod])
    P.op("dve", lambda e: e.scalar_tensor_tensor(a2[:], modT[:, 32:40, :], 1.0, bc2(premlp), ALU.add, ALU.mult),
         reads=[B_mod, B_set], writes=[B_mod])
    for gi, c0 in enumerate((2 * D, 5 * D)):
        P.op("dve", (lambda e, gi=gi, c0=c0: e.tensor_tensor(grow[:, gi, :], modrow[:, c0:c0 + D], bgrow[:, gi, :], ALU.add)),
             reads=[B_row, B_set], writes=[B_set])
        P.op("dve", (lambda e, gi=gi: e.tensor_tensor(grow[:, gi, :], grow[:, gi, :], pwrow[:, gi, :], ALU.mult)),
             reads=[B_set], writes=[B_set])
    cnt = 0
    for gi, gvt in enumerate((gv1, gv2)):
        for b in range(2):
            for nh in range(2):
                bkm = 3 + (cnt % 2)
                cnt += 1
                P.op("pe", (lambda e, bkm=bkm, gi=gi, b=b, nh=nh: e.matmul(pb[bkm][:], sel[:, b, :], grow[:, gi, nh * 512:(nh + 1) * 512],
                                                                            start=True, stop=True)),
                     reads=[B_set], writes=[B_pb[bkm]])
                P.op("act", (lambda e, bkm=bkm, gvt=gvt, b=b, nh=nh: e.copy(gvt[b][:, nh * 512:(nh + 1) * 512], pb[bkm][:])),
                     reads=[B_pb[bkm]], writes=[B_mod])
    P.barrier()
    st.close()

    p1 = ExitStack()
    cosS = [P.sb(f"cosS{i}", [128, GW], F32, p1) for i in range(2)]
    sinS = [P.sb(f"sinS{i}", [128, GW], F32, p1) for i in range(2)]
    qraw = [P.sb(f"qraw{i}", [128, 2, GW], BF16, p1) for i in range(2)]
    _rt1 = P.sb("rt1", [128, 2, GW], F32, p1)
    _rt2 = P.sb("rt2", [128, 2, GW], F32, p1)
    rt1 = [_rt1, _rt1]
    rt2 = [_rt2, _rt2]
    qT = [P.sb(f"qT{i}", [128, 4, GW], BF16, p1) for i in range(2)]
    kT = P.sb("kT", [128, 2, KRING * 128], BF16, p1)
    vaug = P.sb("vaug", [128, KRING, 2, 72], BF16, p1)
    hA = [P.sb(f"hA{i}", [128, 2, GW], F32, p1) for i in range(2)]
    hB = [P.sb(f"hB{i}", [128, 2, GW], F32, p1) for i in range(2)]
    hC = [P.sb(f"hC{i}", [128, 2, GW], F32, p1) for i in range(2)]
    qd = [P.sb(f"qd{i}", [128, 4, GW], BF16, p1) for i in range(2)]
    kd = [P.sb(f"kd{i}", [128, 4, GW], BF16, p1) for i in range(2)]
    dec = [P.sb(f"dec{i}", [128, 4, 2], F32, p1) for i in range(2)]
    vt = [P.sb(f"vt{i}", [128, 512], BF16, p1) for i in range(2)]
    gg = [P.sb(f"gg{i}", [128, 512], F32, p1) for i in range(2)]
    Pe = [[P.sb(f"Pe{i}_{h}", [128, 4, 2, 128], BF16, p1) for h in range(2)] for i in range(2)]
    attn = [P.sb(f"attn{i}", [128, 512], F32, p1) for i in range(2)]
    Am = [P.sb(f"Am{i}", [128, 4, 128], BF16, p1) for i in range(2)]
    kdtok = [P.sb(f"kdtok{i}", [128, 4, 128], BF16, p1) for i in range(2)]
    tmpS = [P.sb(f"tmpS{i}", [128, 4, 128], F32, p1) for i in range(2)]
    sqo = [P.sb(f"sqo{i}", [128, 4, 128], F32, p1) for i in range(2)]
    rect = sqo
    cat = [P.sb(f"cat{i}", [128, D], BF16, p1) for i in range(2)]
    catT = [P.sb(f"catT{i}", [128, 8, 128], BF16, p1) for i in range(2)]
    Sst = P.sb("Sst", [128, 4, 128], F32, p1)
    Sb = P.sb("Sb", [128, 4, 128], BF16, p1)

    B_rope = P.bufs(2, "rope")
    B_qraw = P.bufs(2, "qraw")
    _b1 = P.buf("rt1")
    _b2 = P.buf("rt2")
    B_rt1 = [_b1, _b1]
    B_rt2 = [_b2, _b2]
    B_qT = P.bufs(2, "qT")
    B_kT = P.bufs(KRING, "kT")
    B_va = P.bufs(KRING, "va")
    B_hA = P.bufs(2, "hA")
    B_hB = P.bufs(2, "hB")
    B_hC = P.bufs(2, "hC")
    B_qd = P.bufs(2, "qd")
    B_kd = P.bufs(2, "kd")
    B_dec = P.bufs(2, "dec")
    B_vt = P.bufs(2, "vt")
    B_gg = P.bufs(2, "gg")
    B_Pe = [P.bufs(2, f"Pe{i}_") for i in range(2)]
    B_attn = P.bufs(2, "attn")
    B_Am = P.bufs(2, "Am")
    B_kdtok = P.bufs(2, "kdtok")
    B_tmpS = P.bufs(2, "tmpS")
    B_sqo = P.bufs(2, "sqo")
    B_rect = B_sqo
    B_cat = P.bufs(2, "cat")
    B_catT = P.bufs(2, "catT")
    B_S = P.buf("S")
    B_Sb = P.buf("Sb")

    P.op("pool", lambda e: e.memset(vaug[:], 1.0), writes=B_va)

    pools = {"proj": [0, 1], "o": [4, 5, 6]}
    pcnt = {"proj": 0, "o": 0}

    def bank(pool):
        i = pools[pool][pcnt[pool] % len(pools[pool])]
        pcnt[pool] += 1
        return i

    state = {"av": -1, "s": -1}

    def rstd_ops(ss_ap, lnbuf_ap, out_ap, n, reads, writes):
        P.op("act", lambda e: e.activation(out=lnbuf_ap, in_=ss_ap, func=AF.Ln, scale=1.0 / n, bias=epsc[:, 0:1]),
             reads=reads + [B_const], writes=writes)
        P.op("act", lambda e: e.activation(out=out_ap, in_=lnbuf_ap, func=AF.Exp, scale=-0.5), reads=writes, writes=writes)

    def norm_and_transpose(g, a_mod, sh_lo, src_phase2):
        hs = g % 2
        b = (2 * g) // TPS
        ssx, rsx = (ss2, rs2) if src_phase2 else (ss1, rs1)
        for i in range(2):
            T = 2 * g + i
            xsl = T % 4
            P.op("dve", (lambda e, T=T, xsl=xsl: e.scalar_tensor_tensor(junk[:], xs[xsl][:], 1.0, xs[xsl][:], ALU.mult, ALU.mult,
                                                                         accum_out=ssx[:, T:T + 1])),
                 reads=[B_xs[xsl]], writes=[B_stat[T]])
            rstd_ops(ssx[:, T:T + 1], rsx[:, T, 0:1], rsx[:, T, 1:2], D, [B_stat[T]], [B_stat[T]])
            P.op("dve", (lambda e, T=T, xsl=xsl, i=i: e.tensor_scalar(xn[i][:], xs[xsl][:], rsx[:, T, 1:2], None, ALU.mult)),
                 reads=[B_xs[xsl], B_stat[T]], writes=[B_xn[i]])
        yield
        for rnd in range(2):
            def tr(e, rnd=rnd):
                ins = []
                for jj in range(4):
                    j = rnd * 4 + jj
                    for i in range(2):
                        ins.append(e.transpose(ptr[:, jj * GW + i * 128: jj * GW + (i + 1) * 128], xn[i][:, j * 128:(j + 1) * 128], ident_bf[:]))
                return ins
            P.op("pe", tr, reads=[B_xn[0], B_xn[1], B_const], writes=[B_ptr])
            for jj in range(4):
                j = rnd * 4 + jj
                P.op("dve", (lambda e, j=j, jj=jj: e.tensor_scalar(hT[hs][:, j, :], ptr[:, jj * GW:(jj + 1) * GW],
                                                                  a_mod[:, j, b:b + 1], modT[:, sh_lo + j, b:b + 1], ALU.mult, ALU.add)),
                     reads=[B_ptr, B_mod], writes=[B_hT[hs]])
            yield

    def stage_A(g):
        hs = g % 2
        tp0 = (2 * g) % TPS
        for i in range(2):
            T = 2 * g + i
            xsl = T % 4
            P.op("sp", (lambda e, T=T, xsl=xsl: e.dma_start(out=xs[xsl][:], in_=x_d[T * 128:(T + 1) * 128, :])), writes=[B_xs[xsl]], ndma=1)
        P.op("sp", (lambda e: [e.dma_start(out=cosS[hs][:], in_=cosF_d[:, tp0 * 128: tp0 * 128 + GW]),
                               e.dma_start(out=sinS[hs][:], in_=sinF_d[:, tp0 * 128: tp0 * 128 + GW])]), writes=[B_rope[hs]], ndma=2)
        yield from norm_and_transpose(g, a1, 0, False)

        def fm_proj2(w_ap_fn):
            bk = bank("proj")

            def f(e):
                ins = []
                for cc in range(2):
                    for k in range(8):
                        ins.append(e.matmul(pb[bk][:, cc * GW:(cc + 1) * GW], w_ap_fn(cc, k), hT[hs][:, k, :], start=(k == 0), stop=(k == 7)))
                return ins
            P.op("pe", f, reads=[B_hT[hs], B_w1], writes=[B_pb[bk]])
            return bk

        def fl(t):
            return t[:].rearrange("p a t -> p (a t)")

        def sigmoid_act(dst, bk, bdst):
            P.op("act", (lambda e: e.activation(out=fl(dst), in_=pb[bk][:], func=AF.Exp, scale=-1.0)), reads=[B_pb[bk]], writes=[bdst])
            P.op("act", (lambda e: e.activation(out=fl(dst), in_=fl(dst), func=AF.Ln, bias=ones_f[:, 0:1])), reads=[bdst, B_const], writes=[bdst])
            P.op("act", (lambda e: e.activation(out=fl(dst), in_=fl(dst), func=AF.Exp, scale=-1.0)), reads=[bdst], writes=[bdst])

        def hgrn_pair(p):
            s = p % 2
            bk = fm_proj2(lambda cc, k: win[:, k, 1280 + (2 * p + cc) * 128: 1280 + (2 * p + cc + 1) * 128])
            sigmoid_act(hA[s], bk, B_hA[s])
            for cc in range(2):
                h = 2 * p + cc
                P.op("act", (lambda e, cc=cc, h=h: e.activation(out=hB[s][:, cc, :], in_=hA[s][:, cc, :], func=AF.Ln, scale=oml[:, h:h + 1], bias=lbv[:, h:h + 1])),
                     reads=[B_hA[s], B_mod], writes=[B_hB[s]])
            for cc in range(2):
                h = 2 * p + cc
                P.op("dve", (lambda e, cc=cc, h=h: e.tensor_scalar(hA[s][:, cc, :], hA[s][:, cc, :], lbm1[:, h:h + 1], oml[:, h:h + 1], ALU.mult, ALU.add)),
                     reads=[B_hA[s], B_mod], writes=[B_hA[s]])
            P.op("dve", (lambda e: e.tensor_tensor_scan(fl(hC[s]), rmask[:], fl(hB[s]), 0.0, ALU.mult, ALU.add)),
                 reads=[B_hB[s], B_const], writes=[B_hC[s]])
            P.op("act", (lambda e: e.activation(out=fl(hB[s]), in_=fl(hC[s]), func=AF.Exp, scale=-1.0)), reads=[B_hC[s]], writes=[B_hB[s]])
            P.op("pool", (lambda e: e.tensor_tensor(kd[hs][:, 2 * p:2 * p + 2, :], hA[s][:], hB[s][:], ALU.mult)),
                 reads=[B_hA[s], B_hB[s]], writes=[B_kd[hs]])
            P.op("act", (lambda e: e.activation(out=fl(hB[s]), in_=fl(hC[s]), func=AF.Exp)), reads=[B_hC[s]], writes=[B_hB[s]])
            P.op("pool", (lambda e: e.tensor_copy(dec[hs][:, 2 * p:2 * p + 2, :], hB[s][:, :, 127::128])),
                 reads=[B_hB[s]], writes=[B_dec[hs]])
            bk2 = fm_proj2(lambda cc, k: win[:, k, 768 + (2 * p + cc) * 128: 768 + (2 * p + cc + 1) * 128])
            sigmoid_act(hA[s], bk2, B_hA[s])
            P.op("dve", (lambda e: e.tensor_tensor(fl(hA[s]), pb[bk2][:], fl(hA[s]), ALU.mult)), reads=[B_pb[bk2], B_hA[s]], writes=[B_hA[s]])
            P.op("pool", (lambda e: e.tensor_tensor(qd[hs][:, 2 * p:2 * p + 2, :], hA[s][:], hB[s][:], ALU.mult)),
                 reads=[B_hA[s], B_hB[s]], writes=[B_qd[hs]])

        for p in range(2):
            hgrn_pair(p)
            yield

        def qk_proj(cp):
            s = cp % 2
            if cp < 2:
                bk = fm_proj2(lambda cc, k: win[:, k, (2 * cp + cc) * 128:(2 * cp + cc + 1) * 128])
            else:
                bk = fm_proj2(lambda cc, k: wkd[:, k, cc * 128:(cc + 1) * 128])
            P.op("dve", (lambda e: e.tensor_copy(fl(qraw[s]), pb[bk][:])), reads=[B_pb[bk]], writes=[B_qraw[s]])

        def rope_part(cp):
            s = cp % 2
            bk2 = bank("proj")
            P.op("pe", (lambda e: e.matmul(pb[bk2][:], pswap_bf[:], fl(qraw[s]), start=True, stop=True)),
                 reads=[B_qraw[s], B_const], writes=[B_pb[bk2]])
            sinb = sinS[hs][:].unsqueeze(1).broadcast_to([128, 2, GW])
            cosb = cosS[hs][:].unsqueeze(1).broadcast_to([128, 2, GW])
            P.op("dve", (lambda e: e.tensor_tensor(rt1[s][:], pb[bk2][:].rearrange("p (a t) -> p a t", a=2), sinb, ALU.mult)),
                 reads=[B_pb[bk2], B_rope[hs]], writes=[B_rt1[s]])
            P.op("pool", (lambda e: e.tensor_tensor(rt2[s][:], qraw[s][:], cosb, ALU.mult)),
                 reads=[B_qraw[s], B_rope[hs]], writes=[B_rt2[s]])
            if cp < 2:
                P.op("pool", (lambda e: e.tensor_tensor(qT[hs][:, 2 * cp:2 * cp + 2, :], rt1[s][:], rt2[s][:], ALU.add)),
                     reads=[B_rt1[s], B_rt2[s]], writes=[B_qT[hs]])
            else:
                r0 = ((2 * g) % KRING)
                P.op("pool", (lambda e: e.tensor_tensor(kT[:, :, r0 * 128: r0 * 128 + GW], rt1[s][:], rt2[s][:], ALU.add)),
                     reads=[B_rt1[s], B_rt2[s]], writes=[B_kT[r0], B_kT[r0 + 1]])

        for cp in range(4):
            if cp < 3:
                qk_proj(cp)
            if cp >= 1:
                rope_part(cp - 1)
            yield

    def stage_B(T):
        g = T // 2
        i = T % 2
        hs = g % 2
        ts = T % 2
        b = T // TPS
        tp = T % TPS
        xsl = T % 4
        rk = T % KRING
        rkp = (T - 1) % KRING
        tc = slice(i * 128, (i + 1) * 128)
        kbs = (0, 1) if tp > 0 else (1,)
        if i == 1:
            for _ in range(KDELAY):
                yield

        def tm_proj(c0, n, eng_evac_fn):
            bk = bank("proj")

            def f(e):
                return [e.matmul(pb[bk][:, 0:n], hT[hs][:, k, tc], win[:, k, c0:c0 + n], start=(k == 0), stop=(k == 7)) for k in range(8)]
            P.op("pe", f, reads=[B_hT[hs], B_w1], writes=[B_pb[bk]])
            eng_evac_fn(bk)

        def sc_part(half):
            kv = half

            def f_sc(e):
                ins = []
                for hh in range(4):
                    head = 4 * half + hh
                    c = head // 2
                    pr = slice((head % 2) * 64, (head % 2) * 64 + 64)
                    for kb in kbs:
                        rr = rkp if kb == 0 else rk
                        ins.append(e.matmul(pb[2 + hh % 2][:, ((hh // 2) * 2 + kb) * 128:((hh // 2) * 2 + kb + 1) * 128],
                                            kT[pr, kv, rr * 128:(rr + 1) * 128], qT[hs][pr, c, tc], start=True, stop=True))
                return ins
            rd = [B_qT[hs], B_kT[rk]] + ([B_kT[rkp]] if tp > 0 else [])
            P.op("pe", f_sc, reads=rd, writes=[B_pb[2], B_pb[3]])
            for bb in range(2):
                if tp > 0:
                    P.op("act", (lambda e, bb=bb: e.activation(out=Pe[ts][half][:, bb::2, :, :],
                                                               in_=pb[2 + bb][:].rearrange("p (a b q) -> p a b q", a=2, b=2), func=AF.Exp, scale=0.125)),
                         reads=[B_pb[2 + bb]], writes=[B_Pe[ts][half]])
                else:
                    P.op("act", (lambda e, bb=bb: e.activation(out=Pe[ts][half][:, bb::2, 1, :],
                                                               in_=pb[2 + bb][:].rearrange("p (a b q) -> p a b q", a=2, b=2)[:, :, 1, :],
                                                               func=AF.Exp, scale=0.125)),
                         reads=[B_pb[2 + bb]], writes=[B_Pe[ts][half]])
            if tp > 0:
                P.op("dve", (lambda e: e.tensor_tensor(Pe[ts][half][:], Pe[ts][half][:], mask2_bf[:].unsqueeze(1).broadcast_to([128, 4, 2, 128]), ALU.mult)),
                     reads=[B_Pe[ts][half], B_const], writes=[B_Pe[ts][half]])
            else:
                P.op("dve", (lambda e: e.tensor_tensor(Pe[ts][half][:, :, 1, :], Pe[ts][half][:, :, 1, :],
                                                       mask2_bf[:, 1, :].unsqueeze(1).broadcast_to([128, 4, 128]), ALU.mult)),
                     reads=[B_Pe[ts][half], B_const], writes=[B_Pe[ts][half]])

        def pv_part(half):
            kv = half
            bkO = bank("o")

            def f_pv(e):
                ins = []
                for hh in range(4):
                    for n, kb in enumerate(kbs):
                        rr = rkp if kb == 0 else rk
                        ins.append(e.matmul(pb[bkO][:, hh * 128:hh * 128 + 72], Pe[ts][half][:, hh, kb, :], vaug[:, rr, kv, :],
                                            start=(n == 0), stop=(n == len(kbs) - 1)))
                return ins
            rd = [B_Pe[ts][half], B_va[rk]] + ([B_va[rkp]] if tp > 0 else [])
            P.op("pe", f_pv, reads=rd, writes=[B_pb[bkO]])
            pO = pb[bkO][:].rearrange("p (h d) -> p h d", h=4)
            P.op("dve", (lambda e: e.tensor_tensor(den[:, T, half, :], pO[:, :, 64], esink[:, 4 * half:4 * half + 4], ALU.add)),
                 reads=[B_pb[bkO], B_const], writes=[B_stat[T]])
            P.op("dve", (lambda e: e.reciprocal(rden[:, T, half, :], den[:, T, half, :])), reads=[B_stat[T]], writes=[B_stat[T]])
            P.op("dve", (lambda e: e.tensor_tensor(attn[ts][:, half * 256:(half + 1) * 256].rearrange("p (h d) -> p h d", h=4),
                                                   pO[:, :, 0:64], rden[:, T, half, :].unsqueeze(2).broadcast_to([128, 4, 64]), ALU.mult)),
                 reads=[B_pb[bkO], B_stat[T]], writes=[B_attn[ts]])

        tm_proj(640, 128, lambda bk: P.op("dve", (lambda e: e.tensor_copy(vaug[:, rk, :, 0:64], pb[bk][:, 0:128].rearrange("p (a d) -> p a d", a=2))),
                                          reads=[B_pb[bk]], writes=[B_va[rk]]))
        state["av"] = T
        tm_proj(1792, 512, lambda bk: P.op("act", (lambda e: e.copy(vt[ts][:], pb[bk][:])), reads=[B_pb[bk]], writes=[B_vt[ts]]))

        def ev_g(bk):
            P.op("act", (lambda e: e.activation(out=gg[ts][:], in_=pb[bk][:], func=AF.Exp, scale=-1.0)), reads=[B_pb[bk]], writes=[B_gg[ts]])
            P.op("act", (lambda e: e.activation(out=gg[ts][:], in_=gg[ts][:], func=AF.Ln, bias=ones_f[:, 0:1])), reads=[B_gg[ts], B_const], writes=[B_gg[ts]])
            P.op("act", (lambda e: e.activation(out=gg[ts][:], in_=gg[ts][:], func=AF.Exp, scale=-1.0)), reads=[B_gg[ts]], writes=[B_gg[ts]])
            P.op("dve", (lambda e: e.tensor_tensor(gg[ts][:], pb[bk][:], gg[ts][:], ALU.mult)), reads=[B_pb[bk], B_gg[ts]], writes=[B_gg[ts]])
            P.op("pool", (lambda e: e.tensor_tensor(gg[ts][:], gg[ts][:], hgw[:], ALU.mult)), reads=[B_gg[ts], B_const], writes=[B_gg[ts]])
        tm_proj(2304, 512, ev_g)
        bkA = bank("o")

        def f_At(e):
            return [e.matmul(pb[bkA][:, h * 128:(h + 1) * 128], kd[hs][:, h, tc], qd[hs][:, h, tc], start=True, stop=True) for h in range(4)]
        P.op("pe", f_At, reads=[B_kd[hs], B_qd[hs]], writes=[B_pb[bkA]])
        P.op("dve", (lambda e: e.tensor_tensor(Am[ts][:], pb[bkA][:].rearrange("p (h t) -> p h t", h=4),
                                               maskT_f[:].unsqueeze(1).broadcast_to([128, 4, 128]), ALU.mult)),
             reads=[B_pb[bkA], B_const], writes=[B_Am[ts]])

        def f_kt(e):
            return [e.transpose(ptr[:, h * 128:(h + 1) * 128], kd[hs][:, h, tc], ident_bf[:]) for h in range(4)]
        P.op("pe", f_kt, reads=[B_kd[hs], B_const], writes=[B_ptr])
        P.op("dve", (lambda e: e.tensor_copy(kdtok[ts][:].rearrange("p h d -> p (h d)"), ptr[:, 0:512])), reads=[B_ptr], writes=[B_kdtok[ts]])
        yield
        while tp > 0 and state["av"] < T - 1:
            yield
        sc_part(0)
        yield
        sc_part(1)
        pv_part(0)
        yield
        pv_part(1)
        P.op("act", (lambda e: e.activation(out=junk[:, 0:512], in_=attn[ts][:], func=AF.Square, accum_out=ssa[:, T:T + 1])),
             reads=[B_attn[ts]], writes=[B_stat[T]])
        rstd_ops(ssa[:, T:T + 1], rsa[:, T, 0:1], rsa[:, T, 1:2], 512, [B_stat[T]], [B_stat[T]])
        P.op("dve", (lambda e: e.scalar_tensor_tensor(cat[ts][:, 0:512], attn[ts][:], rsa[:, T, 1:2], aow[:], ALU.mult, ALU.mult)),
             reads=[B_attn[ts], B_stat[T], B_const], writes=[B_cat[ts]])
        yield

        while state["s"] < T - 1:
            yield
        if tp == 0:
            P.op("pool", lambda e: e.memset(Sst[:], 0.0), writes=[B_S])
            P.op("pool", lambda e: e.memset(Sb[:], 0.0), writes=[B_Sb])
        bko = bank("o")

        def f_o(e):
            ins = []
            for h in range(4):
                ins.append(e.matmul(pb[bko][:, h * 128:(h + 1) * 128], Am[ts][:, h, :], vt[ts][:, h * 128:(h + 1) * 128], start=True, stop=False))
                ins.append(e.matmul(pb[bko][:, h * 128:(h + 1) * 128], qd[hs][:, h, tc], Sb[:, h, :], start=False, stop=True))
            return ins
        P.op("pe", f_o, reads=[B_Am[ts], B_vt[ts], B_qd[hs], B_Sb], writes=[B_pb[bko]])
        bkK = bank("o")

        def f_kv(e):
            return [e.matmul(pb[bkK][:, h * 128:(h + 1) * 128], kdtok[ts][:, h, :], vt[ts][:, h * 128:(h + 1) * 128], start=True, stop=True)
                    for h in range(4)]
        P.op("pe", f_kv, reads=[B_kdtok[ts], B_vt[ts]], writes=[B_pb[bkK]])
        decb = dec[hs][:, :, i:i + 1].broadcast_to([128, 4, 128])
        P.op("dve", (lambda e: e.tensor_tensor(tmpS[ts][:], pb[bkK][:].rearrange("p (h d) -> p h d", h=4), Sst[:], ALU.add)),
             reads=[B_pb[bkK], B_S], writes=[B_tmpS[ts]])
        P.op("dve", (lambda e: e.tensor_tensor(Sb[:], tmpS[ts][:], decb, ALU.mult)), reads=[B_tmpS[ts], B_dec[hs]], writes=[B_Sb])
        P.op("pool", (lambda e: e.tensor_tensor(Sst[:], tmpS[ts][:], decb, ALU.mult)), reads=[B_tmpS[ts], B_dec[hs]], writes=[B_S])
        state["s"] = T
        P.op("act", (lambda e: e.activation(out=sqo[ts][:].rearrange("p h d -> p (h d)"), in_=pb[bko][:], func=AF.Square)),
             reads=[B_pb[bko]], writes=[B_sqo[ts]])
        P.op("dve", (lambda e: e.tensor_reduce(ssr[:, T, :], sqo[ts][:], AX.X, ALU.add)), reads=[B_sqo[ts]], writes=[B_stat[T]])
        rstd_ops(ssr[:, T, :], rsr[:, T, 0:4], rsr[:, T, 4:8], 128, [B_stat[T]], [B_stat[T]])
        P.op("dve", (lambda e: e.tensor_tensor(rect[ts][:], pb[bko][:].rearrange("p (h d) -> p h d", h=4),
                                               rsr[:, T, 4:8].unsqueeze(2).broadcast_to([128, 4, 128]), ALU.mult)),
             reads=[B_pb[bko], B_stat[T]], writes=[B_rect[ts]])
        P.op("pool", (lambda e: e.tensor_tensor(cat[ts][:, 512:1024], rect[ts][:].rearrange("p h d -> p (h d)"), gg[ts][:], ALU.mult)),
             reads=[B_rect[ts], B_gg[ts]], writes=[B_cat[ts]])
        yield

        def f_ct(e):
            return [e.transpose(ptr[:, j * 128:(j + 1) * 128], cat[ts][:, j * 128:(j + 1) * 128], ident_bf[:]) for j in range(8)]
        P.op("pe", f_ct, reads=[B_cat[ts], B_const], writes=[B_ptr])
        P.op("dve", (lambda e: e.tensor_copy(catT[ts][:].rearrange("p j t -> p (j t)"), ptr[:, :])), reads=[B_ptr], writes=[B_catT[ts]])
        yield

        def f_mix(e):
            ins = []
            for nh in range(2):
                for k in range(8):
                    ins.append(e.matmul(pb[2 + nh][:], catT[ts][:, k, :], wout[:, k, nh * 512:(nh + 1) * 512], start=(k == 0), stop=(k == 7)))
            return ins
        P.op("pe", f_mix, reads=[B_catT[ts], B_w1], writes=[B_pb[2], B_pb[3]])
        for nh in range(2):
            P.op("act", (lambda e, nh=nh: e.activation(out=junk[:, 0:512], in_=pb[2 + nh][:], func=AF.Square, accum_out=ssm[:, T, nh:nh + 1])),
                 reads=[B_pb[2 + nh]], writes=[B_stat[T]])
        P.op("dve", (lambda e: e.tensor_tensor(ssm[:, T, 2:3], ssm[:, T, 0:1], ssm[:, T, 1:2], ALU.add)), reads=[B_stat[T]], writes=[B_stat[T]])
        rstd_ops(ssm[:, T, 2:3], ssm[:, T, 3:4], ssm[:, T, 0:1], D, [B_stat[T]], [B_stat[T]])
        for nh in range(2):
            P.op("dve", (lambda e, nh=nh: e.scalar_tensor_tensor(tmpx[ts][:, nh * 512:(nh + 1) * 512], pb[2 + nh][:], ssm[:, T, 0:1],
                                                                 gv1[b][:, nh * 512:(nh + 1) * 512], ALU.mult, ALU.mult)),
                 reads=[B_pb[2 + nh], B_stat[T], B_mod], writes=[B_tmpx[ts]])
        P.op("pool", (lambda e: e.tensor_tensor(xs[xsl][:], tmpx[ts][:], xs[xsl][:], ALU.add)), reads=[B_tmpx[ts], B_xs[xsl]], writes=[B_xs[xsl]])
        P.op("sp", (lambda e: e.dma_start(out=x1_d[T * 128:(T + 1) * 128, :], in_=xs[xsl][:])), reads=[B_xs[xsl]], writes=[B_x1d[T]], ndma=1)
        yield

    if stop >= 2:
        for _ in stage_A(0):
            pass
    for g in range(ngroups if stop >= 3 else 0):
        gens = [stage_B(2 * g), stage_B(2 * g + 1)]
        if g + 1 < ngroups:
            gens.append(stage_A(g + 1))
        run_interleaved(gens, [1, 1, KWA])

    P.barrier()
    p1.close()
    pw.close()
    pp.close()

    p2 = ExitStack()
    wup = P.sb("wup", [128, 8, DFF], BF16, p2)
    wdn = P.sb("wdn", [128, 32, D], BF16, p2)
    rl = [P.sb(f"rl{i}", [128, 512], BF16, p2) for i in range(2)]
    uT = P.sb("uT", [128, 32, GW], BF16, p2)
    B_wup = P.bufs(4, "wup")
    B_wdn = P.bufs(8, "wdn")
    B_rl = P.bufs(2, "rl")
    B_uT = P.bufs(16, "uT")

    wup_v = wup_d.rearrange("(j p) c -> p j c", p=128)
    wdn_v = wdn_d.rearrange("(j p) c -> p j c", p=128)
    for q in range(4 if stop >= 4 else 0):
        P.op("pool", (lambda e, q=q: e.dma_start(out=wup[:, :, q * 1024:(q + 1) * 1024], in_=wup_v[:, :, q * 1024:(q + 1) * 1024])),
             writes=[B_wup[q]], ndma=1)
    for j4 in range(8 if stop >= 4 else 0):
        P.op("pool", (lambda e, j4=j4: e.dma_start(out=wdn[:, 4 * j4:4 * j4 + 4, :], in_=wdn_v[:, 4 * j4:4 * j4 + 4, :])), writes=[B_wdn[j4]], ndma=1)

    up_banks = [4, 5, 6]
    upc = [0]
    down_units = [(0, 1), (2, 3)]

    def mlp_prep(g):
        for i in range(2):
            T = 2 * g + i
            xsl = T % 4
            P.op("sp", (lambda e, T=T, xsl=xsl: e.dma_start(out=xs[xsl][:], in_=x1_d[T * 128:(T + 1) * 128, :])),
                 reads=[B_x1d[T]], writes=[B_xs[xsl]], ndma=1)
        yield from norm_and_transpose(g, a2, 24, True)

    def mlp_main(g):
        hs = g % 2
        for cp in range(16):
            bk = up_banks[upc[0] % 3]
            upc[0] += 1
            s = cp % 2

            def f_up(e, cp=cp, bk=bk):
                ins = []
                for cc in range(2):
                    c = 2 * cp + cc
                    for k in range(8):
                        ins.append(e.matmul(pb[bk][:, cc * GW:(cc + 1) * GW], wup[:, k, c * 128:(c + 1) * 128], hT[hs][:, k, :],
                                            start=(k == 0), stop=(k == 7)))
                return ins
            P.op("pe", f_up, reads=[B_hT[hs], B_wup[cp // 4]], writes=[B_pb[bk]])
            P.op("act", (lambda e, bk=bk, s=s: e.activation(out=rl[s][:], in_=pb[bk][:], func=AF.Relu)), reads=[B_pb[bk]], writes=[B_rl[s]])
            P.op("pool", (lambda e, s=s, cp=cp: e.tensor_tensor(uT[:, 2 * cp:2 * cp + 2, :].rearrange("p a t -> p (a t)"), rl[s][:], rl[s][:], ALU.mult)),
                 reads=[B_rl[s]], writes=[B_uT[cp]])
            if cp % 4 == 3:
                yield
        for i in range(2):
            T = 2 * g + i
            xsl = T % 4
            b = T // TPS
            ts = T % 2
            du = down_units[i]

            for cb in range(4):
                def f_dn(e, i=i, du=du, cb=cb):
                    ins = []
                    for nh in range(2):
                        for c in range(8 * cb, 8 * cb + 8):
                            ins.append(e.matmul(pb[du[nh]][:], uT[:, c, i * 128:(i + 1) * 128], wdn[:, c, nh * 512:(nh + 1) * 512],
                                                start=(c == 0), stop=(c == 31)))
                    return ins
                P.op("pe", f_dn, reads=B_uT[4 * cb:4 * cb + 4] + [B_wdn[2 * cb], B_wdn[2 * cb + 1]], writes=[B_pb[du[0]], B_pb[du[1]]])
            for nh in range(2):
                P.op("act", (lambda e, nh=nh, du=du, T=T: e.activation(out=junk[:, 0:512], in_=pb[du[nh]][:], func=AF.Square, accum_out=ssy[:, T, nh:nh + 1])),
                     reads=[B_pb[du[nh]]], writes=[B_stat[T]])
            P.op("dve", (lambda e, T=T: e.tensor_tensor(ssy[:, T, 2:3], ssy[:, T, 0:1], ssy[:, T, 1:2], ALU.add)), reads=[B_stat[T]], writes=[B_stat[T]])
            rstd_ops(ssy[:, T, 2:3], ssy[:, T, 3:4], ssy[:, T, 0:1], D, [B_stat[T]], [B_stat[T]])
            for nh in range(2):
                P.op("dve", (lambda e, nh=nh, du=du, T=T, ts=ts, b=b: e.scalar_tensor_tensor(
                    tmpx[ts][:, nh * 512:(nh + 1) * 512], pb[du[nh]][:], ssy[:, T, 0:1], gv2[b][:, nh * 512:(nh + 1) * 512], ALU.mult, ALU.mult)),
                    reads=[B_pb[du[nh]], B_stat[T], B_mod], writes=[B_tmpx[ts]])
            P.op("pool", (lambda e, ts=ts, xsl=xsl: e.tensor_tensor(xs[xsl][:], tmpx[ts][:], xs[xsl][:], ALU.add)),
                 reads=[B_tmpx[ts], B_xs[xsl]], writes=[B_xs[xsl]])
            P.op("sp", (lambda e, T=T, xsl=xsl: e.dma_start(out=out_d[T * 128:(T + 1) * 128, :], in_=xs[xsl][:])),
                 reads=[B_xs[xsl]], writes=[B_out[T]], ndma=1)
            yield

    n2 = ngroups if stop >= 5 else 0
    if n2:
        for _ in mlp_prep(0):
            pass
    for g in range(n2):
        gens = [mlp_main(g)]
        if g + 1 < n2:
            gens.append(mlp_prep(g + 1))
        run_interleaved(gens)

    P.op("sp", None, reads=B_out)
    p2.close()
    P.emit()
    return nc


_NC_CACHE = {}


def _layout_inputs(inp):
    f = lambda a: np.ascontiguousarray(np.asarray(a, dtype=np.float32))
    consts = _host_consts()
    colT = lambda v, n: f(np.asarray(v).reshape(n, 128).T)
    shared = {
        "w_ada": f(inp["w_ada"][0]),
        "b_adaT": colT(inp["b_ada"][0], 48),
        "pre_mixT": colT(inp["pre_w_mix"][0], 8),
        "pre_mlpT": colT(inp["pre_w_mlp"][0], 8),
        "bgrow": f(np.broadcast_to(np.stack([np.asarray(inp["b_ada"][0])[2 * D:3 * D], np.asarray(inp["b_ada"][0])[5 * D:6 * D]])[None], (2, 2, D))),
        "pwrow": f(np.broadcast_to(np.stack([np.asarray(inp["post_w_mix"][0]), np.asarray(inp["post_w_mlp"][0])])[None], (2, 2, D))),
        "w_in": f(inp["w_in"][0]),
        "w_out": f(inp["w_out"][0]),
        "w_up": f(inp["w_up"][0]),
        "w_down": f(inp["w_down"][0]),
        "sinks_rep": f(np.broadcast_to(np.asarray(inp["attn_sinks"][0])[None, :], (128, 8))),
        "aow_rep": f(np.broadcast_to(np.asarray(inp["attn_out_w"][0])[None, :], (128, 512))),
        "hgw_rep": f(np.broadcast_to(np.tile(np.asarray(inp["hg_norm_w"][0]), 4)[None, :], (128, 512))),
        "lbT": f(np.asarray(inp["lb_table"]).reshape(2, 4, 128).transpose(2, 0, 1)),
    }
    shared.update(consts)
    x = np.asarray(inp["x"], dtype=np.float32)
    c = np.asarray(inp["c"], dtype=np.float32)
    maps = []
    for i in range(NCORES):
        m = dict(shared)
        m["x"] = f(x[NSEQ * i:NSEQ * (i + 1)].reshape(TOK, D))
        cc = c[NSEQ * i:NSEQ * (i + 1)]
        m["cT"] = f(cc.reshape(2, 8, 128).transpose(2, 1, 0))
        maps.append(m)
    return maps


def kernel(**inputs):
    if "nc" not in _NC_CACHE:
        _NC_CACHE["nc"] = build_program()
    nc = _NC_CACHE["nc"]
    maps = _layout_inputs(inputs)
    res = run_bass_kernel_spmd(nc, maps, core_ids=list(range(NCORES)))
    outs = [np.asarray(r["out"], dtype=np.float32).reshape(NSEQ, SEQ, D) for r in res.results]
    return np.concatenate(outs, axis=0)
```

```python
import numpy as np
from contextlib import ExitStack
import concourse.bass as bass
import concourse.mybir as mybir
from concourse.bass_utils import run_bass_kernel_spmd

F32 = mybir.dt.float32
BF16 = mybir.dt.bfloat16
AF = mybir.ActivationFunctionType
ALU = mybir.AluOpType
AX = mybir.AxisListType

NCORES = 8
D = 1024
SEQ = 2048
NSEQ = 2
TOK = NSEQ * SEQ
NT = TOK // 128
TPS = SEQ // 128
NG = NT // 2
GW = 256
DFF = 4096
INC = 2816
EPS = 1e-6
KRING = 6
import os as _os
BSTOP = float(_os.environ.get('KBSTOP', '99'))
KDELAY = int(_os.environ.get('KDELAY', '2'))
KWA = int(_os.environ.get('KWA', '1'))
KADELAY = int(_os.environ.get('KADELAY', '0'))


class Buf:
    __slots__ = ("name", "last_w", "readers")

    def __init__(self, name):
        self.name = name
        self.last_w = None
        self.readers = []


class Op:
    __slots__ = ("eng", "fn", "deps", "idx", "ndma", "signal", "semval", "sem", "name", "prevval")


class Prog:
    ENGS = ("pe", "act", "dve", "pool", "sp")
    NDMASEM = 24

    def __init__(self, nc):
        self.nc = nc
        self.streams = {e: [] for e in self.ENGS}
        self.es = ExitStack()
        self.nbuf = 0
        self.dma_since_barrier = []

    def sb(self, name, shape, dt, stack=None):
        return (stack or self.es).enter_context(self.nc.sbuf_tensor("s_" + name, list(shape), dt))

    def ps(self, name, shape, dt):
        return self.es.enter_context(self.nc.psum_tensor("p_" + name, list(shape), dt))

    def buf(self, name=None):
        self.nbuf += 1
        return Buf(name or f"b{self.nbuf}")

    def bufs(self, n, name="b"):
        return [self.buf(f"{name}{i}") for i in range(n)]

    def op(self, eng, fn, reads=(), writes=(), ndma=0, name=None):
        o = Op()
        o.eng = eng
        o.fn = fn
        o.ndma = ndma
        o.signal = False
        o.semval = None
        o.sem = None
        o.name = name
        o.prevval = 0
        deps = []
        for b in reads:
            if b.last_w is not None:
                deps.append(b.last_w)
        for b in writes:
            if b.last_w is not None:
                deps.append(b.last_w)
            deps.extend(b.readers)
        seen = set()
        dd = []
        for d in deps:
            if id(d) in seen:
                continue
            seen.add(id(d))
            if eng == "pe" and d.eng == "pe" and d.ndma == 0:
                continue
            dd.append(d)
        o.deps = dd
        o.idx = len(self.streams[eng])
        self.streams[eng].append(o)
        for b in reads:
            b.readers.append(o)
        for b in writes:
            b.last_w = o
            b.readers = []
        if ndma > 0:
            self.dma_since_barrier.append(o)
        return o

    def barrier(self):
        lasts = {}
        for e in self.ENGS:
            for o in reversed(self.streams[e]):
                if o.ndma == 0 and o.fn is not None:
                    lasts[e] = o
                    break
        dmas = list(self.dma_since_barrier)
        self.dma_since_barrier = []
        for e in self.ENGS:
            o = Op()
            o.eng = e
            o.fn = None
            o.ndma = 0
            o.signal = False
            o.semval = None
            o.sem = None
            o.name = "barrier"
            o.prevval = 0
            o.deps = [lasts[x] for x in lasts if x != e] + dmas
            o.idx = len(self.streams[e])
            self.streams[e].append(o)

    def emit(self):
        nc = self.nc
        for e in self.ENGS:
            for o in self.streams[e]:
                for d in o.deps:
                    d.signal = True
        for e in self.ENGS:
            cnt = 0
            for o in self.streams[e]:
                if o.ndma == 0 and o.signal:
                    cnt += 1
                    o.semval = cnt
        dcount = [0] * self.NDMASEM
        dpool = {"sp": list(range(0, 16)), "pool": list(range(16, 24))}
        for e in self.ENGS:
            rr = 0
            for o in self.streams[e]:
                if o.ndma > 0:
                    s = dpool[e][rr % len(dpool[e])]
                    rr += 1
                    o.sem = s
                    o.prevval = dcount[s]
                    dcount[s] += 16 * o.ndma
                    o.semval = dcount[s]
        es = self.es
        esem = {e: es.enter_context(nc.semaphore(f"tl_{e}")) for e in self.ENGS}
        dsem = [es.enter_context(nc.semaphore(f"dma_{i}")) for i in range(self.NDMASEM)]
        block = es.enter_context(nc.Block())
        streams = self.streams

        def run_stream(e, engobj):
            seen = {}
            for o in streams[e]:
                waits = []
                for d in o.deps:
                    if d.ndma > 0:
                        key = ("d", d.sem)
                        sem = dsem[d.sem]
                    else:
                        key = ("e", d.eng)
                        sem = esem[d.eng]
                    v = d.semval
                    if seen.get(key, 0) >= v:
                        continue
                    waits.append((key, sem, v))
                if o.ndma > 0:
                    key = ("d", o.sem)
                    if o.prevval > 0 and seen.get(key, 0) < o.prevval:
                        waits.append((key, dsem[o.sem], o.prevval))
                best = {}
                for key, sem, v in waits:
                    if key not in best or best[key][1] < v:
                        best[key] = (sem, v)
                wl = list(best.items())
                for key, (sem, v) in wl:
                    seen[key] = v
                if o.ndma > 0:
                    for key, (sem, v) in wl:
                        engobj.wait_ge(sem, v)
                    ins = o.fn(engobj)
                    if not isinstance(ins, (list, tuple)):
                        ins = [ins]
                    assert len(ins) == o.ndma, (o.name, len(ins), o.ndma)
                    for i in ins:
                        i.then_inc(dsem[o.sem], 16)
                elif o.fn is None:
                    for key, (sem, v) in wl:
                        engobj.wait_ge(sem, v)
                else:
                    for key, (sem, v) in wl[1:]:
                        engobj.wait_ge(sem, v)
                    ins = o.fn(engobj)
                    if not isinstance(ins, (list, tuple)):
                        ins = [ins]
                    if wl:
                        ins[0]._wait_ge(wl[0][1][0], wl[0][1][1])
                    if o.signal:
                        ins[-1].then_inc(esem[e], 1)

        @block.tensor
        def _(eng):
            run_stream("pe", eng)

        @block.scalar
        def _(eng):
            run_stream("act", eng)

        @block.vector
        def _(eng):
            run_stream("dve", eng)

        @block.gpsimd
        def _(eng):
            run_stream("pool", eng)

        @block.sync
        def _(eng):
            run_stream("sp", eng)

        es.close()


def run_interleaved(gens, weights=None):
    gens = list(gens)
    weights = list(weights) if weights else [1] * len(gens)
    live = list(range(len(gens)))
    while live:
        for gi in list(live):
            for _ in range(weights[gi]):
                try:
                    next(gens[gi])
                except StopIteration:
                    live.remove(gi)
                    break


def _host_consts():
    c = {}
    c["ident"] = np.eye(128, dtype=np.float32)
    c["ones"] = np.ones((128, 128), dtype=np.float32)
    sel = np.zeros((2, 2, 128), dtype=np.float32)
    sel[0, 0, :] = 1.0
    sel[1, 1, :] = 1.0
    c["sel"] = sel
    k = np.arange(128)[:, None]
    q = np.arange(128)[None, :]
    cur = (k <= q).astype(np.float32)
    prev = (k > q).astype(np.float32)
    c["mask2"] = np.ascontiguousarray(np.stack([prev, cur], axis=1).reshape(128, 256))
    c["maskT"] = np.ascontiguousarray(cur)
    rm = np.ones((128, 2 * GW), dtype=np.float32)
    rm[:, 0::128] = 0.0
    c["rmask"] = rm
    ps = np.zeros((128, 128), dtype=np.float32)
    for m in range(128):
        d = m % 64
        base = m - d
        if d < 8:
            s = base + d + 8
        elif d < 16:
            s = base + d - 8
        else:
            s = m
        ps[s, m] = 1.0
    c["pswap"] = ps
    inv_freq = (np.float32(500000.0) ** (-np.arange(0, 16, 2, dtype=np.float32) / np.float32(16))).astype(np.float32)
    ang = (np.arange(SEQ, dtype=np.float32)[:, None] * inv_freq[None, :]).astype(np.float32)
    cs = np.cos(ang).astype(np.float32)
    sn = np.sin(ang).astype(np.float32)
    cosF = np.ones((128, SEQ), dtype=np.float32)
    sinF = np.zeros((128, SEQ), dtype=np.float32)
    for p in range(128):
        d = p % 64
        if d < 8:
            cosF[p] = cs[:, d]
            sinF[p] = -sn[:, d]
        elif d < 16:
            cosF[p] = cs[:, d - 8]
            sinF[p] = sn[:, d - 8]
    c["cosF"] = cosF
    c["sinF"] = sinF
    return c


def build_program(stop=99, ngroups=NG):
    nc = bass.Bass("TRN2", target_bir_lowering=False)
    P = Prog(nc)

    def din(name, shape):
        return nc.dram_tensor(name, list(shape), F32, kind="ExternalInput").ap()

    x_d = din("x", [TOK, D])
    cT_d = din("cT", [128, 8, 2])
    wada_d = din("w_ada", [D, 6 * D])
    bada_d = din("b_adaT", [128, 48])
    premix_d = din("pre_mixT", [128, 8])
    premlp_d = din("pre_mlpT", [128, 8])
    win_d = din("w_in", [D, INC])
    wout_d = din("w_out", [D, D])
    wup_d = din("w_up", [D, DFF])
    wdn_d = din("w_down", [DFF, D])
    sinks_d = din("sinks_rep", [128, 8])
    aow_d = din("aow_rep", [128, 512])
    hgw_d = din("hgw_rep", [128, 512])
    lbT_d = din("lbT", [128, 2, 4])
    bgrow_d = din("bgrow", [2, 2, D])
    pwrow_d = din("pwrow", [2, 2, D])
    sel_d = din("sel", [2, 2, 128])
    ident_d = din("ident", [128, 128])
    ones_d = din("ones", [128, 128])
    mask2_d = din("mask2", [128, 256])
    maskT_d = din("maskT", [128, 128])
    rmask_d = din("rmask", [128, 2 * GW])
    pswap_d = din("pswap", [128, 128])
    cosF_d = din("cosF", [128, SEQ])
    sinF_d = din("sinF", [128, SEQ])
    out_d = nc.dram_tensor("out", [TOK, D], F32, kind="ExternalOutput").ap()
    x1_d = nc.dram_tensor("x1_scratch", [TOK, D], F32).ap()

    ident_bf = P.sb("ident_bf", [128, 128], BF16)
    ident_f = P.sb("ident_f", [128, 2], F32)
    ones_f = P.sb("ones_f", [128, 1], F32)
    pswap_bf = P.sb("pswap_bf", [128, 128], BF16)
    mask2_bf = P.sb("mask2_bf", [128, 2, 128], BF16)
    maskT_f = P.sb("maskT_f", [128, 128], F32)
    rmask = P.sb("rmask", [128, 2 * GW], F32)
    epsc = P.sb("epsc", [128, 1], F32)
    esink = P.sb("esink", [128, 8], F32)
    modT = P.sb("modT", [128, 48, 2], F32)
    a1 = P.sb("a1", [128, 8, 2], F32)
    a2 = P.sb("a2", [128, 8, 2], F32)
    lbv = P.sb("lbv", [128, 4], F32)
    oml = P.sb("oml", [128, 4], F32)
    lbm1 = P.sb("lbm1", [128, 4], F32)
    gv2 = [P.sb(f"gv2_{b}", [128, D], F32) for b in range(2)]
    ss1 = P.sb("ss1", [128, NT], F32)
    rs1 = P.sb("rs1", [128, NT, 2], F32)
    ssa = P.sb("ssa", [128, NT], F32)
    rsa = P.sb("rsa", [128, NT, 2], F32)
    ssm = P.sb("ssm", [128, NT, 4], F32)
    ssr = P.sb("ssr", [128, NT, 4], F32)
    rsr = P.sb("rsr", [128, NT, 8], F32)
    den = P.sb("den", [128, NT, 2, 4], F32)
    rden = P.sb("rden", [128, NT, 2, 4], F32)
    ss2 = P.sb("ss2", [128, NT], F32)
    rs2 = P.sb("rs2", [128, NT, 2], F32)
    ssy = P.sb("ssy", [128, NT, 4], F32)
    xs = [P.sb(f"xs{i}", [128, D], F32) for i in range(4)]
    xn = [P.sb(f"xn{i}", [128, D], BF16) for i in range(2)]
    junk = P.sb("junk", [128, D], BF16)
    hT = [P.sb(f"hT{i}", [128, 8, GW], BF16) for i in range(2)]
    _tmpx = P.sb("tmpx", [128, D], F32)
    tmpx = [_tmpx, _tmpx]

    pp = ExitStack()
    aow = P.sb("aow", [128, 512], F32, pp)
    hgw = P.sb("hgw", [128, 512], F32, pp)
    gv1 = [P.sb(f"gv1_{b}", [128, D], F32, pp) for b in range(2)]

    pb = [P.ps(f"pb{i}", [128, 512], F32) for i in range(7)]
    ptr = P.ps("ptr", [128, 1024], BF16)
    B_pb = P.bufs(7, "pb")
    B_ptr = P.buf("ptr")

    B_const = P.buf("const")
    B_mod = P.buf("mod")
    B_xs = P.bufs(4, "xs")
    B_xn = P.bufs(2, "xn")
    B_junk = P.buf("junk")
    B_hT = P.bufs(2, "hT")
    _B_tmpx = P.buf("tmpx")
    B_tmpx = [_B_tmpx, _B_tmpx]
    B_stat = [P.buf(f"stat{t}") for t in range(NT)]
    B_x1d = [P.buf(f"x1d{t}") for t in range(NT)]
    B_out = [P.buf(f"outd{t}") for t in range(NT)]

    def ld(eng, dst, src, b=B_const, n=1):
        P.op(eng, lambda e: e.dma_start(out=dst, in_=src), writes=[b], ndma=n)

    ld("pool", ident_bf[:], ident_d[:, :])
    P.op("pool", lambda e: e.memset(ident_f[:], 0.0), writes=[B_const])
    P.op("pool", lambda e: e.memset(ident_f[0:1, 0:1], 1.0), writes=[B_const])
    P.op("sp", lambda e: e.dma_start(out=ident_f[1:2, 0:2], in_=ident_d[1:2, 0:2]), writes=[B_const], ndma=1)
    P.op("pool", lambda e: e.memset(ones_f[:], 1.0), writes=[B_const])
    ld("pool", pswap_bf[:], pswap_d[:, :])
    ld("pool", mask2_bf[:].rearrange("p a b -> p (a b)"), mask2_d[:, :])
    ld("sp", maskT_f[:], maskT_d[:, :])
    ld("sp", rmask[:], rmask_d[:, :])
    ld("sp", esink[:], sinks_d[:, :])
    ld("sp", aow[:], aow_d[:, :])
    ld("sp", hgw[:], hgw_d[:, :])
    P.op("pool", lambda e: e.memset(epsc[:], EPS), writes=[B_const])
    P.op("act", lambda e: e.activation(out=esink[:], in_=esink[:], func=AF.Exp), reads=[B_const], writes=[B_const])

    pw = ExitStack()
    win = P.sb("win", [128, 8, INC], BF16, pw)
    wkd = P.sb("wkd", [128, 8, 256], BF16, pw)
    wout = P.sb("wout", [128, 8, D], BF16, pw)
    B_w1 = P.buf("w1")
    win_v = win_d.rearrange("(j p) c -> p j c", p=128)
    wout_v = wout_d.rearrange("(j p) c -> p j c", p=128)
    for j in range(8):
        P.op("pool", (lambda e, j=j: e.dma_start(out=win[:, j, :], in_=win_v[:, j, :])), writes=[B_w1], ndma=1)
    for kv in range(2):
        for r in range(2):
            c0 = (2 * kv + r) * 64
            P.op("pool", (lambda e, kv=kv, c0=c0: e.dma_start(out=wkd[:, :, c0:c0 + 64], in_=win_v[:, :, 512 + 64 * kv:576 + 64 * kv])),
                 writes=[B_w1], ndma=1)
    for j in range(8):
        P.op("pool", (lambda e, j=j: e.dma_start(out=wout[:, j, :], in_=wout_v[:, j, :])), writes=[B_w1], ndma=1)

    st = ExitStack()
    cTs = P.sb("cTs", [128, 8, 2], F32, st)
    ca = P.sb("ca", [128, 8, 2], F32, st)
    badaT = P.sb("badaT", [128, 48], F32, st)
    premix = P.sb("premix", [128, 8], F32, st)
    premlp = P.sb("premlp", [128, 8], F32, st)
    lbT = P.sb("lbT", [128, 2, 4], F32, st)
    lbd = P.sb("lbd", [128, 4], F32, st)
    wa = [P.sb(f"wa{i}", [128, 8, 256], F32, st) for i in range(2)]
    modrow = P.sb("modrow", [2, 6 * D], F32, st)
    bgrow = P.sb("bgrow", [2, 2, D], F32, st)
    pwrow = P.sb("pwrow", [2, 2, D], F32, st)
    grow = P.sb("grow", [2, 2, D], F32, st)
    sel = P.sb("sel", [2, 2, 128], F32, st)
    B_wa = P.bufs(2, "wa")
    B_set = P.buf("setup")
    B_row = P.buf("modrow")

    ld("sp", cTs[:], cT_d[:, :, :], B_set)
    ld("sp", badaT[:], bada_d[:, :], B_set)
    ld("sp", premix[:], premix_d[:, :], B_set)
    ld("sp", premlp[:], premlp_d[:, :], B_set)
    ld("sp", lbT[:], lbT_d[:, :, :], B_set)
    ld("sp", bgrow[:], bgrow_d[:, :, :], B_set)
    ld("sp", pwrow[:], pwrow_d[:, :, :], B_set)
    ld("sp", sel[:], sel_d[:, :, :], B_set)
    P.op("act", lambda e: e.activation(out=ca[:], in_=cTs[:], func=AF.Silu), reads=[B_set], writes=[B_set])
    P.op("dve", lambda e: e.tensor_tensor(lbd[:], lbT[:, 1, :], lbT[:, 0, :], ALU.subtract), reads=[B_set], writes=[B_set])
    P.op("act", lambda e: e.activation(out=lbv[:], in_=lbd[:], func=AF.Sigmoid), reads=[B_set], writes=[B_mod])
    P.op("dve", lambda e: e.tensor_scalar(oml[:], lbv[:], -1.0, 1.0, ALU.mult, ALU.add), reads=[B_mod], writes=[B_mod])
    P.op("dve", lambda e: e.tensor_scalar(lbm1[:], lbv[:], 1.0, -1.0, ALU.mult, ALU.add), reads=[B_mod], writes=[B_mod])

    wada_v = wada_d.rearrange("(j p) c -> p j c", p=128)
    for blk in range(24):
        s = blk % 2
        bkm = blk % 2
        P.op("sp", (lambda e, s=s, blk=blk: e.dma_start(out=wa[s][:], in_=wada_v[:, :, blk * 256:(blk + 1) * 256])),
             writes=[B_wa[s]], ndma=1)

        def mm_mod(e, s=s, bkm=bkm):
            return [e.matmul(pb[bkm][0:2, 0:256], ca[:, k, :], wa[s][:, k, :], start=(k == 0), stop=(k == 7)) for k in range(8)]
        P.op("pe", mm_mod, reads=[B_wa[s], B_set], writes=[B_pb[bkm]])
        P.op("act", (lambda e, bkm=bkm, blk=blk: e.copy(modrow[:, blk * 256:(blk + 1) * 256], pb[bkm][0:2, 0:256])),
             reads=[B_pb[bkm]], writes=[B_row])
    pmod = pb[2][:, 0:96]

    def tr_mod(e):
        return [e.transpose(pmod[:, 2 * m:2 * m + 2], modrow[:, m * 128:(m + 1) * 128], ident_f[0:2, 0:2]) for m in range(48)]
    P.op("pe", tr_mod, reads=[B_row, B_const], writes=[B_pb[2]])
    P.op("dve", lambda e: e.tensor_tensor(modT[:], pmod.rearrange("p (m b) -> p m b", b=2),
                                          badaT[:].unsqueeze(2).broadcast_to([128, 48, 2]), ALU.add),
         reads=[B_pb[2], B_set], writes=[B_mod])

    def bc2(v):
        return v[:].unsqueeze(2).broadcast_to([128, 8, 2])
    P.op("dve", lambda e: e.scalar_tensor_tensor(a1[:], modT[:, 8:16, :], 1.0, bc2(premix), ALU.add, ALU.mult),
         reads=[B_mod, B_set], writes=[B_mod])
    P.op("dve", lambda e: e.scalar_tensor_tensor(a2[:], modT[:, 32:40, :], 1.0, bc2(premlp), ALU.add, ALU.mult),
         reads=[B_mod, B_set], writes=[B_mod])
    for gi, c0 in enumerate((2 * D, 5 * D)):
        P.op("dve", (lambda e, gi=gi, c0=c0: e.tensor_tensor(grow[:, gi, :], modrow[:, c0:c0 + D], bgrow[:, gi, :], ALU.add)),
             reads=[B_row, B_set], writes=[B_set])
        P.op("dve", (lambda e, gi=gi: e.tensor_tensor(grow[:, gi, :], grow[:, gi, :], pwrow[:, gi, :], ALU.mult)),
             reads=[B_set], writes=[B_set])
    cnt = 0
    for gi, gvt in enumerate((gv1, gv2)):
        for b in range(2):
            for nh in range(2):
                bkm = 3 + (cnt % 2)
                cnt += 1
                P.op("pe", (lambda e, bkm=bkm, gi=gi, b=b, nh=nh: e.matmul(pb[bkm][:], sel[:, b, :], grow[:, gi, nh * 512:(nh + 1) * 512],
                                                                            start=True, stop=True)),
                     reads=[B_set], writes=[B_pb[bkm]])
                P.op("act", (lambda e, bkm=bkm, gvt=gvt, b=b, nh=nh: e.copy(gvt[b][:, nh * 512:(nh + 1) * 512], pb[bkm][:])),
                     reads=[B_pb[bkm]], writes=[B_mod])
    P.barrier()
    st.close()

    p1 = ExitStack()
    cosS = [P.sb(f"cosS{i}", [128, GW], F32, p1) for i in range(2)]
    sinS = [P.sb(f"sinS{i}", [128, GW], F32, p1) for i in range(2)]
    qraw = [P.sb(f"qraw{i}", [128, 2, GW], BF16, p1) for i in range(2)]
    _rt1 = P.sb("rt1", [128, 2, GW], F32, p1)
    _rt2 = P.sb("rt2", [128, 2, GW], F32, p1)
    rt1 = [_rt1, _rt1]
    rt2 = [_rt2, _rt2]
    qT = [P.sb(f"qT{i}", [128, 4, GW], BF16, p1) for i in range(2)]
    kT = P.sb("kT", [128, 2, KRING * 128], BF16, p1)
    vaug = P.sb("vaug", [128, KRING, 2, 72], BF16, p1)
    hA = [P.sb(f"hA{i}", [128, 2, GW], F32, p1) for i in range(2)]
    hB = [P.sb(f"hB{i}", [128, 2, GW], F32, p1) for i in range(2)]
    hC = [P.sb(f"hC{i}", [128, 2, GW], F32, p1) for i in range(2)]
    qd = [P.sb(f"qd{i}", [128, 4, GW], BF16, p1) for i in range(2)]
    kd = [P.sb(f"kd{i}", [128, 4, GW], BF16, p1) for i in range(2)]
    dec = [P.sb(f"dec{i}", [128, 4, 2], F32, p1) for i in range(2)]
    vt = [P.sb(f"vt{i}", [128, 512], BF16, p1) for i in range(2)]
    gg = [P.sb(f"gg{i}", [128, 512], F32, p1) for i in range(2)]
    Pe = [[P.sb(f"Pe{i}_{h}", [128, 4, 2, 128], BF16, p1) for h in range(2)] for i in range(2)]
    attn = [P.sb(f"attn{i}", [128, 512], F32, p1) for i in range(2)]
    Am = [P.sb(f"Am{i}", [128, 4, 128], BF16, p1) for i in range(2)]
    kdtok = [P.sb(f"kdtok{i}", [128, 4, 128], BF16, p1) for i in range(2)]
    tmpS = [P.sb(f"tmpS{i}", [128, 4, 128], F32, p1) for i in range(2)]
    sqo = [P.sb(f"sqo{i}", [128, 4, 128], F32, p1) for i in range(2)]
    rect = sqo
    cat = [P.sb(f"cat{i}", [128, D], BF16, p1) for i in range(2)]
    catT = [P.sb(f"catT{i}", [128, 8, 128], BF16, p1) for i in range(2)]
    Sst = P.sb("Sst", [128, 4, 128], F32, p1)
    Sb = P.sb("Sb", [128, 4, 128], BF16, p1)

    B_rope = P.bufs(2, "rope")
    B_qraw = P.bufs(2, "qraw")
    _b1 = P.buf("rt1")
    _b2 = P.buf("rt2")
    B_rt1 = [_b1, _b1]
    B_rt2 = [_b2, _b2]
    B_qT = P.bufs(2, "qT")
    B_kT = P.bufs(KRING, "kT")
    B_va = P.bufs(KRING, "va")
    B_hA = P.bufs(2, "hA")
    B_hB = P.bufs(2, "hB")
    B_hC = P.bufs(2, "hC")
    B_qd = P.bufs(2, "qd")
    B_kd = P.bufs(2, "kd")
    B_dec = P.bufs(2, "dec")
    B_vt = P.bufs(2, "vt")
    B_gg = P.bufs(2, "gg")
    B_Pe = [P.bufs(2, f"Pe{i}_") for i in range(2)]
    B_attn = P.bufs(2, "attn")
    B_Am = P.bufs(2, "Am")
    B_kdtok = P.bufs(2, "kdtok")
    B_tmpS = P.bufs(2, "tmpS")
    B_sqo = P.bufs(2, "sqo")
    B_rect = B_sqo
    B_cat = P.bufs(2, "cat")
    B_catT = P.bufs(2, "catT")
    B_S = P.buf("S")
    B_Sb = P.buf("Sb")

    P.op("pool", lambda e: e.memset(vaug[:], 1.0), writes=B_va)

    pools = {"proj": [0, 1], "o": [4, 5, 6]}
    pcnt = {"proj": 0, "o": 0}

    def bank(pool):
        i = pools[pool][pcnt[pool] % len(pools[pool])]
        pcnt[pool] += 1
        return i

    state = {"av": -1, "s": -1}

    def rstd_ops(ss_ap, lnbuf_ap, out_ap, n, reads, writes):
        P.op("act", lambda e: e.activation(out=lnbuf_ap, in_=ss_ap, func=AF.Ln, scale=1.0 / n, bias=epsc[:, 0:1]),
             reads=reads + [B_const], writes=writes)
        P.op("act", lambda e: e.activation(out=out_ap, in_=lnbuf_ap, func=AF.Exp, scale=-0.5), reads=writes, writes=writes)

    def norm_and_transpose(g, a_mod, sh_lo, src_phase2):
        hs = g % 2
        b = (2 * g) // TPS
        ssx, rsx = (ss2, rs2) if src_phase2 else (ss1, rs1)
        for i in range(2):
            T = 2 * g + i
            xsl = T % 4
            P.op("dve", (lambda e, T=T, xsl=xsl: e.scalar_tensor_tensor(junk[:], xs[xsl][:], 1.0, xs[xsl][:], ALU.mult, ALU.mult,
                                                                         accum_out=ssx[:, T:T + 1])),
                 reads=[B_xs[xsl]], writes=[B_stat[T]])
            rstd_ops(ssx[:, T:T + 1], rsx[:, T, 0:1], rsx[:, T, 1:2], D, [B_stat[T]], [B_stat[T]])
            P.op("dve", (lambda e, T=T, xsl=xsl, i=i: e.tensor_scalar(xn[i][:], xs[xsl][:], rsx[:, T, 1:2], None, ALU.mult)),
                 reads=[B_xs[xsl], B_stat[T]], writes=[B_xn[i]])
        yield
        for rnd in range(2):
            def tr(e, rnd=rnd):
                ins = []
                for jj in range(4):
                    j = rnd * 4 + jj
                    for i in range(2):
                        ins.append(e.transpose(ptr[:, jj * GW + i * 128: jj * GW + (i + 1) * 128], xn[i][:, j * 128:(j + 1) * 128], ident_bf[:]))
                return ins
            P.op("pe", tr, reads=[B_xn[0], B_xn[1], B_const], writes=[B_ptr])
            for jj in range(4):
                j = rnd * 4 + jj
                P.op("dve", (lambda e, j=j, jj=jj: e.tensor_scalar(hT[hs][:, j, :], ptr[:, jj * GW:(jj + 1) * GW],
                                                                  a_mod[:, j, b:b + 1], modT[:, sh_lo + j, b:b + 1], ALU.mult, ALU.add)),
                     reads=[B_ptr, B_mod], writes=[B_hT[hs]])
            yield

    def stage_A(g, delay=0):
        hs = g % 2
        tp0 = (2 * g) % TPS
        for _ in range(delay):
            yield
        for i in range(2):
            T = 2 * g + i
            xsl = T % 4
            P.op("sp", (lambda e, T=T, xsl=xsl: e.dma_start(out=xs[xsl][:], in_=x_d[T * 128:(T + 1) * 128, :])), writes=[B_xs[xsl]], ndma=1)
        P.op("sp", (lambda e: [e.dma_start(out=cosS[hs][:], in_=cosF_d[:, tp0 * 128: tp0 * 128 + GW]),
                               e.dma_start(out=sinS[hs][:], in_=sinF_d[:, tp0 * 128: tp0 * 128 + GW])]), writes=[B_rope[hs]], ndma=2)
        yield from norm_and_transpose(g, a1, 0, False)

        def fm_proj2(w_ap_fn):
            bk = bank("proj")

            def f(e):
                ins = []
                for cc in range(2):
                    for k in range(8):
                        ins.append(e.matmul(pb[bk][:, cc * GW:(cc + 1) * GW], w_ap_fn(cc, k), hT[hs][:, k, :], start=(k == 0), stop=(k == 7)))
                return ins
            P.op("pe", f, reads=[B_hT[hs], B_w1], writes=[B_pb[bk]])
            return bk

        def fl(t):
            return t[:].rearrange("p a t -> p (a t)")

        def sigmoid_act(dst, bk, bdst):
            P.op("act", (lambda e: e.activation(out=fl(dst), in_=pb[bk][:], func=AF.Exp, scale=-1.0)), reads=[B_pb[bk]], writes=[bdst])
            P.op("act", (lambda e: e.activation(out=fl(dst), in_=fl(dst), func=AF.Ln, bias=ones_f[:, 0:1])), reads=[bdst, B_const], writes=[bdst])
            P.op("act", (lambda e: e.activation(out=fl(dst), in_=fl(dst), func=AF.Exp, scale=-1.0)), reads=[bdst], writes=[bdst])

        def hgrn_pair(p):
            s = p % 2
            bk = fm_proj2(lambda cc, k: win[:, k, 1280 + (2 * p + cc) * 128: 1280 + (2 * p + cc + 1) * 128])
            P.op("act", (lambda e: e.activation(out=fl(hA[s]), in_=pb[bk][:], func=AF.Exp, scale=-1.0)), reads=[B_pb[bk]], writes=[B_hA[s]])
            yield
            P.op("act", (lambda e: e.activation(out=fl(hA[s]), in_=fl(hA[s]), func=AF.Ln, bias=ones_f[:, 0:1])), reads=[B_hA[s], B_const], writes=[B_hA[s]])
            P.op("act", (lambda e: e.activation(out=fl(hA[s]), in_=fl(hA[s]), func=AF.Exp, scale=-1.0)), reads=[B_hA[s]], writes=[B_hA[s]])
            yield
            for cc in range(2):
                h = 2 * p + cc
                P.op("act", (lambda e, cc=cc, h=h: e.activation(out=hB[s][:, cc, :], in_=hA[s][:, cc, :], func=AF.Ln, scale=oml[:, h:h + 1], bias=lbv[:, h:h + 1])),
                     reads=[B_hA[s], B_mod], writes=[B_hB[s]])
            for cc in range(2):
                h = 2 * p + cc
                P.op("dve", (lambda e, cc=cc, h=h: e.tensor_scalar(hA[s][:, cc, :], hA[s][:, cc, :], lbm1[:, h:h + 1], oml[:, h:h + 1], ALU.mult, ALU.add)),
                     reads=[B_hA[s], B_mod], writes=[B_hA[s]])
            yield
            P.op("dve", (lambda e: e.tensor_tensor_scan(fl(hC[s]), rmask[:], fl(hB[s]), 0.0, ALU.mult, ALU.add)),
                 reads=[B_hB[s], B_const], writes=[B_hC[s]])
            P.op("act", (lambda e: e.activation(out=fl(hB[s]), in_=fl(hC[s]), func=AF.Exp, scale=-1.0)), reads=[B_hC[s]], writes=[B_hB[s]])
            yield
            P.op("pool", (lambda e: e.tensor_tensor(kd[hs][:, 2 * p:2 * p + 2, :], hA[s][:], hB[s][:], ALU.mult)),
                 reads=[B_hA[s], B_hB[s]], writes=[B_kd[hs]])
            P.op("act", (lambda e: e.activation(out=fl(hB[s]), in_=fl(hC[s]), func=AF.Exp)), reads=[B_hC[s]], writes=[B_hB[s]])
            P.op("pool", (lambda e: e.tensor_copy(dec[hs][:, 2 * p:2 * p + 2, :], hB[s][:, :, 127::128])),
                 reads=[B_hB[s]], writes=[B_dec[hs]])
            yield
            bk2 = fm_proj2(lambda cc, k: win[:, k, 768 + (2 * p + cc) * 128: 768 + (2 * p + cc + 1) * 128])
            P.op("act", (lambda e: e.activation(out=fl(hA[s]), in_=pb[bk2][:], func=AF.Exp, scale=-1.0)), reads=[B_pb[bk2]], writes=[B_hA[s]])
            P.op("dve", (lambda e: e.tensor_copy(fl(hC[s]), pb[bk2][:])), reads=[B_pb[bk2], B_hA[s]], writes=[B_hC[s]])
            yield
            P.op("act", (lambda e: e.activation(out=fl(hA[s]), in_=fl(hA[s]), func=AF.Ln, bias=ones_f[:, 0:1])), reads=[B_hA[s], B_const], writes=[B_hA[s]])
            P.op("act", (lambda e: e.activation(out=fl(hA[s]), in_=fl(hA[s]), func=AF.Exp, scale=-1.0)), reads=[B_hA[s]], writes=[B_hA[s]])
            yield
            P.op("dve", (lambda e: e.tensor_tensor(fl(hA[s]), fl(hC[s]), fl(hA[s]), ALU.mult)), reads=[B_hC[s], B_hA[s]], writes=[B_hA[s]])
            P.op("pool", (lambda e: e.tensor_tensor(qd[hs][:, 2 * p:2 * p + 2, :], hA[s][:], hB[s][:], ALU.mult)),
                 reads=[B_hA[s], B_hB[s]], writes=[B_qd[hs]])
            yield

        def qk_proj(cp):
            s = cp % 2
            if cp < 2:
                bk = fm_proj2(lambda cc, k: win[:, k, (2 * cp + cc) * 128:(2 * cp + cc + 1) * 128])
            else:
                bk = fm_proj2(lambda cc, k: wkd[:, k, cc * 128:(cc + 1) * 128])
            P.op("dve", (lambda e: e.tensor_copy(fl(qraw[s]), pb[bk][:])), reads=[B_pb[bk]], writes=[B_qraw[s]])

        def rope_part(cp):
            s = cp % 2
            bk2 = bank("proj")
            P.op("pe", (lambda e: e.matmul(pb[bk2][:], pswap_bf[:], fl(qraw[s]), start=True, stop=True)),
                 reads=[B_qraw[s], B_const], writes=[B_pb[bk2]])
            sinb = sinS[hs][:].unsqueeze(1).broadcast_to([128, 2, GW])
            cosb = cosS[hs][:].unsqueeze(1).broadcast_to([128, 2, GW])
            P.op("dve", (lambda e: e.tensor_tensor(rt1[s][:], pb[bk2][:].rearrange("p (a t) -> p a t", a=2), sinb, ALU.mult)),
                 reads=[B_pb[bk2], B_rope[hs]], writes=[B_rt1[s]])
            P.op("pool", (lambda e: e.tensor_tensor(rt2[s][:], qraw[s][:], cosb, ALU.mult)),
                 reads=[B_qraw[s], B_rope[hs]], writes=[B_rt2[s]])
            if cp < 2:
                P.op("pool", (lambda e: e.tensor_tensor(qT[hs][:, 2 * cp:2 * cp + 2, :], rt1[s][:], rt2[s][:], ALU.add)),
                     reads=[B_rt1[s], B_rt2[s]], writes=[B_qT[hs]])
            else:
                r0 = ((2 * g) % KRING)
                P.op("pool", (lambda e: e.tensor_tensor(kT[:, :, r0 * 128: r0 * 128 + GW], rt1[s][:], rt2[s][:], ALU.add)),
                     reads=[B_rt1[s], B_rt2[s]], writes=[B_kT[r0], B_kT[r0 + 1]])

        def qk_gen():
            for cp in range(4):
                if cp < 3:
                    qk_proj(cp)
                if cp >= 1:
                    rope_part(cp - 1)
                yield

        subs = [hgrn_pair(0), hgrn_pair(1), qk_gen()]
        while subs:
            for sg in list(subs):
                try:
                    next(sg)
                except StopIteration:
                    subs.remove(sg)
            yield

    def stage_B(T):
        g = T // 2
        i = T % 2
        hs = g % 2
        ts = T % 2
        b = T // TPS
        tp = T % TPS
        xsl = T % 4
        rk = T % KRING
        rkp = (T - 1) % KRING
        tc = slice(i * 128, (i + 1) * 128)
        kbs = (0, 1) if tp > 0 else (1,)
        if i == 1:
            for _ in range(KDELAY):
                yield

        def tm_proj(c0, n, eng_evac_fn):
            bk = bank("proj")

            def f(e):
                return [e.matmul(pb[bk][:, 0:n], hT[hs][:, k, tc], win[:, k, c0:c0 + n], start=(k == 0), stop=(k == 7)) for k in range(8)]
            P.op("pe", f, reads=[B_hT[hs], B_w1], writes=[B_pb[bk]])
            eng_evac_fn(bk)

        def sc_part(half):
            kv = half

            def f_sc(e):
                ins = []
                for hh in range(4):
                    head = 4 * half + hh
                    c = head // 2
                    pr = slice((head % 2) * 64, (head % 2) * 64 + 64)
                    for kb in kbs:
                        rr = rkp if kb == 0 else rk
                        ins.append(e.matmul(pb[2 + hh % 2][:, ((hh // 2) * 2 + kb) * 128:((hh // 2) * 2 + kb + 1) * 128],
                                            kT[pr, kv, rr * 128:(rr + 1) * 128], qT[hs][pr, c, tc], start=True, stop=True))
                return ins
            rd = [B_qT[hs], B_kT[rk]] + ([B_kT[rkp]] if tp > 0 else [])
            P.op("pe", f_sc, reads=rd, writes=[B_pb[2], B_pb[3]])
            for bb in range(2):
                if tp > 0:
                    P.op("act", (lambda e, bb=bb: e.activation(out=Pe[ts][half][:, bb::2, :, :],
                                                               in_=pb[2 + bb][:].rearrange("p (a b q) -> p a b q", a=2, b=2), func=AF.Exp, scale=0.125)),
                         reads=[B_pb[2 + bb]], writes=[B_Pe[ts][half]])
                else:
                    P.op("act", (lambda e, bb=bb: e.activation(out=Pe[ts][half][:, bb::2, 1, :],
                                                               in_=pb[2 + bb][:].rearrange("p (a b q) -> p a b q", a=2, b=2)[:, :, 1, :],
                                                               func=AF.Exp, scale=0.125)),
                         reads=[B_pb[2 + bb]], writes=[B_Pe[ts][half]])
            if tp > 0:
                P.op("dve", (lambda e: e.tensor_tensor(Pe[ts][half][:], Pe[ts][half][:], mask2_bf[:].unsqueeze(1).broadcast_to([128, 4, 2, 128]), ALU.mult)),
                     reads=[B_Pe[ts][half], B_const], writes=[B_Pe[ts][half]])
            else:
                P.op("dve", (lambda e: e.tensor_tensor(Pe[ts][half][:, :, 1, :], Pe[ts][half][:, :, 1, :],
                                                       mask2_bf[:, 1, :].unsqueeze(1).broadcast_to([128, 4, 128]), ALU.mult)),
                     reads=[B_Pe[ts][half], B_const], writes=[B_Pe[ts][half]])

        def pv_part(half):
            kv = half
            bkO = bank("o")

            def f_pv(e):
                ins = []
                for hh in range(4):
                    for n, kb in enumerate(kbs):
                        rr = rkp if kb == 0 else rk
                        ins.append(e.matmul(pb[bkO][:, hh * 128:hh * 128 + 72], Pe[ts][half][:, hh, kb, :], vaug[:, rr, kv, :],
                                            start=(n == 0), stop=(n == len(kbs) - 1)))
                return ins
            rd = [B_Pe[ts][half], B_va[rk]] + ([B_va[rkp]] if tp > 0 else [])
            P.op("pe", f_pv, reads=rd, writes=[B_pb[bkO]])
            pO = pb[bkO][:].rearrange("p (h d) -> p h d", h=4)
            P.op("dve", (lambda e: e.tensor_tensor(den[:, T, half, :], pO[:, :, 64], esink[:, 4 * half:4 * half + 4], ALU.add)),
                 reads=[B_pb[bkO], B_const], writes=[B_stat[T]])
            P.op("dve", (lambda e: e.reciprocal(rden[:, T, half, :], den[:, T, half, :])), reads=[B_stat[T]], writes=[B_stat[T]])
            P.op("dve", (lambda e: e.tensor_tensor(attn[ts][:, half * 256:(half + 1) * 256].rearrange("p (h d) -> p h d", h=4),
                                                   pO[:, :, 0:64], rden[:, T, half, :].unsqueeze(2).broadcast_to([128, 4, 64]), ALU.mult)),
                 reads=[B_pb[bkO], B_stat[T]], writes=[B_attn[ts]])

        tm_proj(640, 128, lambda bk: P.op("dve", (lambda e: e.tensor_copy(vaug[:, rk, :, 0:64], pb[bk][:, 0:128].rearrange("p (a d) -> p a d", a=2))),
                                          reads=[B_pb[bk]], writes=[B_va[rk]]))
        state["av"] = T
        tm_proj(1792, 512, lambda bk: P.op("act", (lambda e: e.copy(vt[ts][:], pb[bk][:])), reads=[B_pb[bk]], writes=[B_vt[ts]]))

        def ev_g(bk):
            P.op("act", (lambda e: e.activation(out=gg[ts][:], in_=pb[bk][:], func=AF.Exp, scale=-1.0)), reads=[B_pb[bk]], writes=[B_gg[ts]])
            P.op("act", (lambda e: e.activation(out=gg[ts][:], in_=gg[ts][:], func=AF.Ln, bias=ones_f[:, 0:1])), reads=[B_gg[ts], B_const], writes=[B_gg[ts]])
            P.op("act", (lambda e: e.activation(out=gg[ts][:], in_=gg[ts][:], func=AF.Exp, scale=-1.0)), reads=[B_gg[ts]], writes=[B_gg[ts]])
            P.op("dve", (lambda e: e.tensor_tensor(gg[ts][:], pb[bk][:], gg[ts][:], ALU.mult)), reads=[B_pb[bk], B_gg[ts]], writes=[B_gg[ts]])
            P.op("pool", (lambda e: e.tensor_tensor(gg[ts][:], gg[ts][:], hgw[:], ALU.mult)), reads=[B_gg[ts], B_const], writes=[B_gg[ts]])
        tm_proj(2304, 512, ev_g)
        bkA = bank("o")

        def f_At(e):
            return [e.matmul(pb[bkA][:, h * 128:(h + 1) * 128], kd[hs][:, h, tc], qd[hs][:, h, tc], start=True, stop=True) for h in range(4)]
        P.op("pe", f_At, reads=[B_kd[hs], B_qd[hs]], writes=[B_pb[bkA]])
        P.op("dve", (lambda e: e.tensor_tensor(Am[ts][:], pb[bkA][:].rearrange("p (h t) -> p h t", h=4),
                                               maskT_f[:].unsqueeze(1).broadcast_to([128, 4, 128]), ALU.mult)),
             reads=[B_pb[bkA], B_const], writes=[B_Am[ts]])

        def f_kt(e):
            return [e.transpose(ptr[:, h * 128:(h + 1) * 128], kd[hs][:, h, tc], ident_bf[:]) for h in range(4)]
        P.op("pe", f_kt, reads=[B_kd[hs], B_const], writes=[B_ptr])
        P.op("dve", (lambda e: e.tensor_copy(kdtok[ts][:].rearrange("p h d -> p (h d)"), ptr[:, 0:512])), reads=[B_ptr], writes=[B_kdtok[ts]])
        yield
        while tp > 0 and state["av"] < T - 1:
            yield
        sc_part(0)
        yield
        sc_part(1)
        pv_part(0)
        yield
        pv_part(1)
        P.op("act", (lambda e: e.activation(out=junk[:, 0:512], in_=attn[ts][:], func=AF.Square, accum_out=ssa[:, T:T + 1])),
             reads=[B_attn[ts]], writes=[B_stat[T]])
        rstd_ops(ssa[:, T:T + 1], rsa[:, T, 0:1], rsa[:, T, 1:2], 512, [B_stat[T]], [B_stat[T]])
        P.op("dve", (lambda e: e.scalar_tensor_tensor(cat[ts][:, 0:512], attn[ts][:], rsa[:, T, 1:2], aow[:], ALU.mult, ALU.mult)),
             reads=[B_attn[ts], B_stat[T], B_const], writes=[B_cat[ts]])
        yield

        while state["s"] < T - 1:
            yield
        if tp == 0:
            P.op("pool", lambda e: e.memset(Sst[:], 0.0), writes=[B_S])
            P.op("pool", lambda e: e.memset(Sb[:], 0.0), writes=[B_Sb])
        bko = bank("o")

        def f_o(e):
            ins = []
            for h in range(4):
                ins.append(e.matmul(pb[bko][:, h * 128:(h + 1) * 128], Am[ts][:, h, :], vt[ts][:, h * 128:(h + 1) * 128], start=True, stop=False))
                ins.append(e.matmul(pb[bko][:, h * 128:(h + 1) * 128], qd[hs][:, h, tc], Sb[:, h, :], start=False, stop=True))
            return ins
        P.op("pe", f_o, reads=[B_Am[ts], B_vt[ts], B_qd[hs], B_Sb], writes=[B_pb[bko]])
        bkK = bank("o")

        def f_kv(e):
            return [e.matmul(pb[bkK][:, h * 128:(h + 1) * 128], kdtok[ts][:, h, :], vt[ts][:, h * 128:(h + 1) * 128], start=True, stop=True)
                    for h in range(4)]
        P.op("pe", f_kv, reads=[B_kdtok[ts], B_vt[ts]], writes=[B_pb[bkK]])
        decb = dec[hs][:, :, i:i + 1].broadcast_to([128, 4, 128])
        P.op("dve", (lambda e: e.tensor_tensor(tmpS[ts][:], pb[bkK][:].rearrange("p (h d) -> p h d", h=4), Sst[:], ALU.add)),
             reads=[B_pb[bkK], B_S], writes=[B_tmpS[ts]])
        P.op("dve", (lambda e: e.tensor_tensor(Sb[:], tmpS[ts][:], decb, ALU.mult)), reads=[B_tmpS[ts], B_dec[hs]], writes=[B_Sb])
        P.op("pool", (lambda e: e.tensor_tensor(Sst[:], tmpS[ts][:], decb, ALU.mult)), reads=[B_tmpS[ts], B_dec[hs]], writes=[B_S])
        state["s"] = T
        P.op("act", (lambda e: e.activation(out=sqo[ts][:].rearrange("p h d -> p (h d)"), in_=pb[bko][:], func=AF.Square)),
             reads=[B_pb[bko]], writes=[B_sqo[ts]])
        P.op("dve", (lambda e: e.tensor_reduce(ssr[:, T, :], sqo[ts][:], AX.X, ALU.add)), reads=[B_sqo[ts]], writes=[B_stat[T]])
        rstd_ops(ssr[:, T, :], rsr[:, T, 0:4], rsr[:, T, 4:8], 128, [B_stat[T]], [B_stat[T]])
        P.op("dve", (lambda e: e.tensor_tensor(rect[ts][:], pb[bko][:].rearrange("p (h d) -> p h d", h=4),
                                               rsr[:, T, 4:8].unsqueeze(2).broadcast_to([128, 4, 128]), ALU.mult)),
             reads=[B_pb[bko], B_stat[T]], writes=[B_rect[ts]])
        P.op("pool", (lambda e: e.tensor_tensor(cat[ts][:, 512:1024], rect[ts][:].rearrange("p h d -> p (h d)"), gg[ts][:], ALU.mult)),
             reads=[B_rect[ts], B_gg[ts]], writes=[B_cat[ts]])
        yield

        def f_ct(e):
            return [e.transpose(ptr[:, j * 128:(j + 1) * 128], cat[ts][:, j * 128:(j + 1) * 128], ident_bf[:]) for j in range(8)]
        P.op("pe", f_ct, reads=[B_cat[ts], B_const], writes=[B_ptr])
        P.op("dve", (lambda e: e.tensor_copy(catT[ts][:].rearrange("p j t -> p (j t)"), ptr[:, :])), reads=[B_ptr], writes=[B_catT[ts]])
        yield

        def f_mix(e):
            ins = []
            for nh in range(2):
                for k in range(8):
                    ins.append(e.matmul(pb[2 + nh][:], catT[ts][:, k, :], wout[:, k, nh * 512:(nh + 1) * 512], start=(k == 0), stop=(k == 7)))
            return ins
        P.op("pe", f_mix, reads=[B_catT[ts], B_w1], writes=[B_pb[2], B_pb[3]])
        for nh in range(2):
            P.op("act", (lambda e, nh=nh: e.activation(out=junk[:, 0:512], in_=pb[2 + nh][:], func=AF.Square, accum_out=ssm[:, T, nh:nh + 1])),
                 reads=[B_pb[2 + nh]], writes=[B_stat[T]])
        P.op("dve", (lambda e: e.tensor_tensor(ssm[:, T, 2:3], ssm[:, T, 0:1], ssm[:, T, 1:2], ALU.add)), reads=[B_stat[T]], writes=[B_stat[T]])
        rstd_ops(ssm[:, T, 2:3], ssm[:, T, 3:4], ssm[:, T, 0:1], D, [B_stat[T]], [B_stat[T]])
        for nh in range(2):
            P.op("dve", (lambda e, nh=nh: e.scalar_tensor_tensor(tmpx[ts][:, nh * 512:(nh + 1) * 512], pb[2 + nh][:], ssm[:, T, 0:1],
                                                                 gv1[b][:, nh * 512:(nh + 1) * 512], ALU.mult, ALU.mult)),
                 reads=[B_pb[2 + nh], B_stat[T], B_mod], writes=[B_tmpx[ts]])
        P.op("pool", (lambda e: e.tensor_tensor(xs[xsl][:], tmpx[ts][:], xs[xsl][:], ALU.add)), reads=[B_tmpx[ts], B_xs[xsl]], writes=[B_xs[xsl]])
        P.op("sp", (lambda e: e.dma_start(out=x1_d[T * 128:(T + 1) * 128, :], in_=xs[xsl][:])), reads=[B_xs[xsl]], writes=[B_x1d[T]], ndma=1)
        yield

    if stop >= 2:
        for _ in stage_A(0):
            pass
    for g in range(ngroups if stop >= 3 else 0):
        gens = [stage_B(2 * g), stage_B(2 * g + 1)]
        if g + 1 < ngroups:
            gens.append(stage_A(g + 1, KADELAY))
        run_interleaved(gens, [1, 1, KWA])

    P.barrier()
    p1.close()
    pw.close()
    pp.close()

    p2 = ExitStack()
    wup = P.sb("wup", [128, 8, DFF], BF16, p2)
    wdn = P.sb("wdn", [128, 32, D], BF16, p2)
    rl = [P.sb(f"rl{i}", [128, 512], BF16, p2) for i in range(2)]
    uT = P.sb("uT", [128, 32, GW], BF16, p2)
    B_wup = P.bufs(4, "wup")
    B_wdn = P.bufs(8, "wdn")
    B_rl = P.bufs(2, "rl")
    B_uT = P.bufs(16, "uT")

    wup_v = wup_d.rearrange("(j p) c -> p j c", p=128)
    wdn_v = wdn_d.rearrange("(j p) c -> p j c", p=128)
    for q in range(4 if stop >= 4 else 0):
        P.op("pool", (lambda e, q=q: e.dma_start(out=wup[:, :, q * 1024:(q + 1) * 1024], in_=wup_v[:, :, q * 1024:(q + 1) * 1024])),
             writes=[B_wup[q]], ndma=1)
    for j4 in range(8 if stop >= 4 else 0):
        P.op("pool", (lambda e, j4=j4: e.dma_start(out=wdn[:, 4 * j4:4 * j4 + 4, :], in_=wdn_v[:, 4 * j4:4 * j4 + 4, :])), writes=[B_wdn[j4]], ndma=1)

    up_banks = [4, 5, 6]
    upc = [0]
    down_units = [(0, 1), (2, 3)]

    def mlp_prep(g):
        for i in range(2):
            T = 2 * g + i
            xsl = T % 4
            P.op("sp", (lambda e, T=T, xsl=xsl: e.dma_start(out=xs[xsl][:], in_=x1_d[T * 128:(T + 1) * 128, :])),
                 reads=[B_x1d[T]], writes=[B_xs[xsl]], ndma=1)
        yield from norm_and_transpose(g, a2, 24, True)

    def mlp_main(g):
        hs = g % 2
        for cp in range(16):
            bk = up_banks[upc[0] % 3]
            upc[0] += 1
            s = cp % 2

            def f_up(e, cp=cp, bk=bk):
                ins = []
                for cc in range(2):
                    c = 2 * cp + cc
                    for k in range(8):
                        ins.append(e.matmul(pb[bk][:, cc * GW:(cc + 1) * GW], wup[:, k, c * 128:(c + 1) * 128], hT[hs][:, k, :],
                                            start=(k == 0), stop=(k == 7)))
                return ins
            P.op("pe", f_up, reads=[B_hT[hs], B_wup[cp // 4]], writes=[B_pb[bk]])
            P.op("act", (lambda e, bk=bk, s=s: e.activation(out=rl[s][:], in_=pb[bk][:], func=AF.Relu)), reads=[B_pb[bk]], writes=[B_rl[s]])
            P.op("pool", (lambda e, s=s, cp=cp: e.tensor_tensor(uT[:, 2 * cp:2 * cp + 2, :].rearrange("p a t -> p (a t)"), rl[s][:], rl[s][:], ALU.mult)),
                 reads=[B_rl[s]], writes=[B_uT[cp]])
            if cp % 4 == 3:
                yield
        for i in range(2):
            T = 2 * g + i
            xsl = T % 4
            b = T // TPS
            ts = T % 2
            du = down_units[i]

            for cb in range(4):
                def f_dn(e, i=i, du=du, cb=cb):
                    ins = []
                    for nh in range(2):
                        for c in range(8 * cb, 8 * cb + 8):
                            ins.append(e.matmul(pb[du[nh]][:], uT[:, c, i * 128:(i + 1) * 128], wdn[:, c, nh * 512:(nh + 1) * 512],
                                                start=(c == 0), stop=(c == 31)))
                    return ins
                P.op("pe", f_dn, reads=B_uT[4 * cb:4 * cb + 4] + [B_wdn[2 * cb], B_wdn[2 * cb + 1]], writes=[B_pb[du[0]], B_pb[du[1]]])
            for nh in range(2):
                P.op("act", (lambda e, nh=nh, du=du, T=T: e.activation(out=junk[:, 0:512], in_=pb[du[nh]][:], func=AF.Square, accum_out=ssy[:, T, nh:nh + 1])),
                     reads=[B_pb[du[nh]]], writes=[B_stat[T]])
            P.op("dve", (lambda e, T=T: e.tensor_tensor(ssy[:, T, 2:3], ssy[:, T, 0:1], ssy[:, T, 1:2], ALU.add)), reads=[B_stat[T]], writes=[B_stat[T]])
            rstd_ops(ssy[:, T, 2:3], ssy[:, T, 3:4], ssy[:, T, 0:1], D, [B_stat[T]], [B_stat[T]])
            for nh in range(2):
                P.op("dve", (lambda e, nh=nh, du=du, T=T, ts=ts, b=b: e.scalar_tensor_tensor(
                    tmpx[ts][:, nh * 512:(nh + 1) * 512], pb[du[nh]][:], ssy[:, T, 0:1], gv2[b][:, nh * 512:(nh + 1) * 512], ALU.mult, ALU.mult)),
                    reads=[B_pb[du[nh]], B_stat[T], B_mod], writes=[B_tmpx[ts]])
            P.op("pool", (lambda e, ts=ts, xsl=xsl: e.tensor_tensor(xs[xsl][:], tmpx[ts][:], xs[xsl][:], ALU.add)),
                 reads=[B_tmpx[ts], B_xs[xsl]], writes=[B_xs[xsl]])
            P.op("sp", (lambda e, T=T, xsl=xsl: e.dma_start(out=out_d[T * 128:(T + 1) * 128, :], in_=xs[xsl][:])),
                 reads=[B_xs[xsl]], writes=[B_out[T]], ndma=1)
            yield

    n2 = ngroups if stop >= 5 else 0
    if n2:
        for _ in mlp_prep(0):
            pass
    for g in range(n2):
        gens = [mlp_main(g)]
        if g + 1 < n2:
            gens.append(mlp_prep(g + 1))
        run_interleaved(gens)

    P.op("sp", None, reads=B_out)
    p2.close()
    P.emit()
    return nc


_NC_CACHE = {}


def _layout_inputs(inp):
    f = lambda a: np.ascontiguousarray(np.asarray(a, dtype=np.float32))
    consts = _host_consts()
    colT = lambda v, n: f(np.asarray(v).reshape(n, 128).T)
    shared = {
        "w_ada": f(inp["w_ada"][0]),
        "b_adaT": colT(inp["b_ada"][0], 48),
        "pre_mixT": colT(inp["pre_w_mix"][0], 8),
        "pre_mlpT": colT(inp["pre_w_mlp"][0], 8),
        "bgrow": f(np.broadcast_to(np.stack([np.asarray(inp["b_ada"][0])[2 * D:3 * D], np.asarray(inp["b_ada"][0])[5 * D:6 * D]])[None], (2, 2, D))),
        "pwrow": f(np.broadcast_to(np.stack([np.asarray(inp["post_w_mix"][0]), np.asarray(inp["post_w_mlp"][0])])[None], (2, 2, D))),
        "w_in": f(inp["w_in"][0]),
        "w_out": f(inp["w_out"][0]),
        "w_up": f(inp["w_up"][0]),
        "w_down": f(inp["w_down"][0]),
        "sinks_rep": f(np.broadcast_to(np.asarray(inp["attn_sinks"][0])[None, :], (128, 8))),
        "aow_rep": f(np.broadcast_to(np.asarray(inp["attn_out_w"][0])[None, :], (128, 512))),
        "hgw_rep": f(np.broadcast_to(np.tile(np.asarray(inp["hg_norm_w"][0]), 4)[None, :], (128, 512))),
        "lbT": f(np.asarray(inp["lb_table"]).reshape(2, 4, 128).transpose(2, 0, 1)),
    }
    shared.update(consts)
    x = np.asarray(inp["x"], dtype=np.float32)
    c = np.asarray(inp["c"], dtype=np.float32)
    maps = []
    for i in range(NCORES):
        m = dict(shared)
        m["x"] = f(x[NSEQ * i:NSEQ * (i + 1)].reshape(TOK, D))
        cc = c[NSEQ * i:NSEQ * (i + 1)]
        m["cT"] = f(cc.reshape(2, 8, 128).transpose(2, 1, 0))
        maps.append(m)
    return maps


def kernel(**inputs):
    if "nc" not in _NC_CACHE:
        _NC_CACHE["nc"] = build_program()
    nc = _NC_CACHE["nc"]
    maps = _layout_inputs(inputs)
    res = run_bass_kernel_spmd(nc, maps, core_ids=list(range(NCORES)))
    outs = [np.asarray(r["out"], dtype=np.float32).reshape(NSEQ, SEQ, D) for r in res.results]
    return np.concatenate(outs, axis=0)
```

```python
import numpy as np
from contextlib import ExitStack
import concourse.bass as bass
import concourse.mybir as mybir
from concourse.bass_utils import run_bass_kernel_spmd

F32 = mybir.dt.float32
BF16 = mybir.dt.bfloat16
AF = mybir.ActivationFunctionType
ALU = mybir.AluOpType
AX = mybir.AxisListType

NCORES = 8
D = 1024
SEQ = 2048
NSEQ = 2
TOK = NSEQ * SEQ
NT = TOK // 128
TPS = SEQ // 128
NG = NT // 2
GW = 256
DFF = 4096
INC = 2816
EPS = 1e-6
KRING = 6
import os as _os
BSTOP = float(_os.environ.get('KBSTOP', '99'))
KDELAY = int(_os.environ.get('KDELAY', '2'))
KWA = int(_os.environ.get('KWA', '1'))
KADELAY = int(_os.environ.get('KADELAY', '0'))


class Buf:
    __slots__ = ("name", "last_w", "readers")

    def __init__(self, name):
        self.name = name
        self.last_w = None
        self.readers = []


class Op:
    __slots__ = ("eng", "fn", "deps", "idx", "ndma", "signal", "semval", "sem", "name", "prevval")


class Prog:
    ENGS = ("pe", "act", "dve", "pool", "sp")
    NDMASEM = 24

    def __init__(self, nc):
        self.nc = nc
        self.streams = {e: [] for e in self.ENGS}
        self.es = ExitStack()
        self.nbuf = 0
        self.dma_since_barrier = []

    def sb(self, name, shape, dt, stack=None):
        return (stack or self.es).enter_context(self.nc.sbuf_tensor("s_" + name, list(shape), dt))

    def ps(self, name, shape, dt):
        return self.es.enter_context(self.nc.psum_tensor("p_" + name, list(shape), dt))

    def buf(self, name=None):
        self.nbuf += 1
        return Buf(name or f"b{self.nbuf}")

    def bufs(self, n, name="b"):
        return [self.buf(f"{name}{i}") for i in range(n)]

    def op(self, eng, fn, reads=(), writes=(), ndma=0, name=None):
        o = Op()
        o.eng = eng
        o.fn = fn
        o.ndma = ndma
        o.signal = False
        o.semval = None
        o.sem = None
        o.name = name
        o.prevval = 0
        deps = []
        for b in reads:
            if b.last_w is not None:
                deps.append(b.last_w)
        for b in writes:
            if b.last_w is not None:
                deps.append(b.last_w)
            deps.extend(b.readers)
        seen = set()
        dd = []
        for d in deps:
            if id(d) in seen:
                continue
            seen.add(id(d))
            if eng == "pe" and d.eng == "pe" and d.ndma == 0:
                continue
            dd.append(d)
        o.deps = dd
        o.idx = len(self.streams[eng])
        self.streams[eng].append(o)
        for b in reads:
            b.readers.append(o)
        for b in writes:
            b.last_w = o
            b.readers = []
        if ndma > 0:
            self.dma_since_barrier.append(o)
        return o

    def barrier(self):
        lasts = {}
        for e in self.ENGS:
            for o in reversed(self.streams[e]):
                if o.ndma == 0 and o.fn is not None:
                    lasts[e] = o
                    break
        dmas = list(self.dma_since_barrier)
        self.dma_since_barrier = []
        for e in self.ENGS:
            o = Op()
            o.eng = e
            o.fn = None
            o.ndma = 0
            o.signal = False
            o.semval = None
            o.sem = None
            o.name = "barrier"
            o.prevval = 0
            o.deps = [lasts[x] for x in lasts if x != e] + dmas
            o.idx = len(self.streams[e])
            self.streams[e].append(o)

    def emit(self):
        nc = self.nc
        for e in self.ENGS:
            for o in self.streams[e]:
                for d in o.deps:
                    d.signal = True
        for e in self.ENGS:
            cnt = 0
            for o in self.streams[e]:
                if o.ndma == 0 and o.signal:
                    cnt += 1
                    o.semval = cnt
        dcount = [0] * self.NDMASEM
        dpool = {"sp": list(range(0, 16)), "pool": list(range(16, 24))}
        for e in self.ENGS:
            rr = 0
            for o in self.streams[e]:
                if o.ndma > 0:
                    s = dpool[e][rr % len(dpool[e])]
                    rr += 1
                    o.sem = s
                    o.prevval = dcount[s]
                    dcount[s] += 16 * o.ndma
                    o.semval = dcount[s]
        es = self.es
        esem = {e: es.enter_context(nc.semaphore(f"tl_{e}")) for e in self.ENGS}
        dsem = [es.enter_context(nc.semaphore(f"dma_{i}")) for i in range(self.NDMASEM)]
        block = es.enter_context(nc.Block())
        streams = self.streams

        def run_stream(e, engobj):
            seen = {}
            for o in streams[e]:
                waits = []
                for d in o.deps:
                    if d.ndma > 0:
                        key = ("d", d.sem)
                        sem = dsem[d.sem]
                    else:
                        key = ("e", d.eng)
                        sem = esem[d.eng]
                    v = d.semval
                    if seen.get(key, 0) >= v:
                        continue
                    waits.append((key, sem, v))
                if o.ndma > 0:
                    key = ("d", o.sem)
                    if o.prevval > 0 and seen.get(key, 0) < o.prevval:
                        waits.append((key, dsem[o.sem], o.prevval))
                best = {}
                for key, sem, v in waits:
                    if key not in best or best[key][1] < v:
                        best[key] = (sem, v)
                wl = list(best.items())
                for key, (sem, v) in wl:
                    seen[key] = v
                if o.ndma > 0:
                    for key, (sem, v) in wl:
                        engobj.wait_ge(sem, v)
                    ins = o.fn(engobj)
                    if not isinstance(ins, (list, tuple)):
                        ins = [ins]
                    assert len(ins) == o.ndma, (o.name, len(ins), o.ndma)
                    for i in ins:
                        i.then_inc(dsem[o.sem], 16)
                elif o.fn is None:
                    for key, (sem, v) in wl:
                        engobj.wait_ge(sem, v)
                else:
                    for key, (sem, v) in wl[1:]:
                        engobj.wait_ge(sem, v)
                    ins = o.fn(engobj)
                    if not isinstance(ins, (list, tuple)):
                        ins = [ins]
                    if wl:
                        ins[0]._wait_ge(wl[0][1][0], wl[0][1][1])
                    if o.signal:
                        ins[-1].then_inc(esem[e], 1)

        @block.tensor
        def _(eng):
            run_stream("pe", eng)

        @block.scalar
        def _(eng):
            run_stream("act", eng)

        @block.vector
        def _(eng):
            run_stream("dve", eng)

        @block.gpsimd
        def _(eng):
            run_stream("pool", eng)

        @block.sync
        def _(eng):
            run_stream("sp", eng)

        es.close()


def run_interleaved(gens, weights=None):
    gens = list(gens)
    weights = list(weights) if weights else [1] * len(gens)
    live = list(range(len(gens)))
    while live:
        for gi in list(live):
            for _ in range(weights[gi]):
                try:
                    next(gens[gi])
                except StopIteration:
                    live.remove(gi)
                    break


def _host_consts():
    c = {}
    c["ident"] = np.eye(128, dtype=np.float32)
    c["ones"] = np.ones((128, 128), dtype=np.float32)
    sel = np.zeros((2, 2, 128), dtype=np.float32)
    sel[0, 0, :] = 1.0
    sel[1, 1, :] = 1.0
    c["sel"] = sel
    k = np.arange(128)[:, None]
    q = np.arange(128)[None, :]
    cur = (k <= q).astype(np.float32)
    prev = (k > q).astype(np.float32)
    c["mask2"] = np.ascontiguousarray(np.stack([prev, cur], axis=1).reshape(128, 256))
    c["maskT"] = np.ascontiguousarray(cur)
    rm = np.ones((128, 2 * GW), dtype=np.float32)
    rm[:, 0::128] = 0.0
    c["rmask"] = rm
    ps = np.zeros((128, 128), dtype=np.float32)
    for m in range(128):
        d = m % 64
        base = m - d
        if d < 8:
            s = base + d + 8
        elif d < 16:
            s = base + d - 8
        else:
            s = m
        ps[s, m] = 1.0
    c["pswap"] = ps
    inv_freq = (np.float32(500000.0) ** (-np.arange(0, 16, 2, dtype=np.float32) / np.float32(16))).astype(np.float32)
    ang = (np.arange(SEQ, dtype=np.float32)[:, None] * inv_freq[None, :]).astype(np.float32)
    cs = np.cos(ang).astype(np.float32)
    sn = np.sin(ang).astype(np.float32)
    cosF = np.ones((128, SEQ), dtype=np.float32)
    sinF = np.zeros((128, SEQ), dtype=np.float32)
    for p in range(128):
        d = p % 64
        if d < 8:
            cosF[p] = cs[:, d]
            sinF[p] = -sn[:, d]
        elif d < 16:
            cosF[p] = cs[:, d - 8]
            sinF[p] = sn[:, d - 8]
    c["cosF"] = cosF
    c["sinF"] = sinF
    return c


def build_program(stop=99, ngroups=NG):
    nc = bass.Bass("TRN2", target_bir_lowering=False)
    P = Prog(nc)

    def din(name, shape):
        return nc.dram_tensor(name, list(shape), F32, kind="ExternalInput").ap()

    x_d = din("x", [TOK, D])
    cT_d = din("cT", [128, 8, 2])
    wada_d = din("w_ada", [D, 6 * D])
    bada_d = din("b_adaT", [128, 48])
    premix_d = din("pre_mixT", [128, 8])
    premlp_d = din("pre_mlpT", [128, 8])
    win_d = din("w_in", [D, INC])
    wout_d = din("w_out", [D, D])
    wup_d = din("w_up", [D, DFF])
    wdn_d = din("w_down", [DFF, D])
    sinks_d = din("sinks_rep", [128, 8])
    aow_d = din("aow_rep", [128, 512])
    hgw_d = din("hgw_rep", [128, 512])
    lbT_d = din("lbT", [128, 2, 4])
    bgrow_d = din("bgrow", [2, 2, D])
    pwrow_d = din("pwrow", [2, 2, D])
    sel_d = din("sel", [2, 2, 128])
    ident_d = din("ident", [128, 128])
    ones_d = din("ones", [128, 128])
    mask2_d = din("mask2", [128, 256])
    maskT_d = din("maskT", [128, 128])
    rmask_d = din("rmask", [128, 2 * GW])
    pswap_d = din("pswap", [128, 128])
    cosF_d = din("cosF", [128, SEQ])
    sinF_d = din("sinF", [128, SEQ])
    out_d = nc.dram_tensor("out", [TOK, D], F32, kind="ExternalOutput").ap()
    x1_d = nc.dram_tensor("x1_scratch", [TOK, D], F32).ap()

    ident_bf = P.sb("ident_bf", [128, 128], BF16)
    ident_f = P.sb("ident_f", [128, 2], F32)
    ones_f = P.sb("ones_f", [128, 1], F32)
    pswap_bf = P.sb("pswap_bf", [128, 128], BF16)
    mask2_bf = P.sb("mask2_bf", [128, 2, 128], BF16)
    maskT_f = P.sb("maskT_f", [128, 128], F32)
    rmask = P.sb("rmask", [128, 2 * GW], F32)
    epsc = P.sb("epsc", [128, 1], F32)
    esink = P.sb("esink", [128, 8], F32)
    modT = P.sb("modT", [128, 48, 2], F32)
    a1 = P.sb("a1", [128, 8, 2], F32)
    a2 = P.sb("a2", [128, 8, 2], F32)
    lbv = P.sb("lbv", [128, 4], F32)
    oml = P.sb("oml", [128, 4], F32)
    lbm1 = P.sb("lbm1", [128, 4], F32)
    gv2 = [P.sb(f"gv2_{b}", [128, D], F32) for b in range(2)]
    ss1 = P.sb("ss1", [128, NT], F32)
    rs1 = P.sb("rs1", [128, NT, 2], F32)
    ssa = P.sb("ssa", [128, NT], F32)
    rsa = P.sb("rsa", [128, NT, 2], F32)
    ssm = P.sb("ssm", [128, NT, 4], F32)
    ssr = P.sb("ssr", [128, NT, 4], F32)
    rsr = P.sb("rsr", [128, NT, 8], F32)
    den = P.sb("den", [128, NT, 2, 4], F32)
    rden = P.sb("rden", [128, NT, 2, 4], F32)
    ss2 = P.sb("ss2", [128, NT], F32)
    rs2 = P.sb("rs2", [128, NT, 2], F32)
    ssy = P.sb("ssy", [128, NT, 4], F32)
    xs = [P.sb(f"xs{i}", [128, D], F32) for i in range(4)]
    xn = [P.sb(f"xn{i}", [128, D], BF16) for i in range(2)]
    junk = P.sb("junk", [128, D], BF16)
    hT = [P.sb(f"hT{i}", [128, 8, GW], BF16) for i in range(2)]
    _tmpx = P.sb("tmpx", [128, D], F32)
    tmpx = [_tmpx, _tmpx]

    pp = ExitStack()
    aow = P.sb("aow", [128, 512], F32, pp)
    hgw = P.sb("hgw", [128, 512], F32, pp)
    gv1 = [P.sb(f"gv1_{b}", [128, D], F32, pp) for b in range(2)]

    pb = [P.ps(f"pb{i}", [128, 512], F32) for i in range(7)]
    ptr = P.ps("ptr", [128, 1024], BF16)
    B_pb = P.bufs(7, "pb")
    B_ptr = P.buf("ptr")

    B_const = P.buf("const")
    B_mod = P.buf("mod")
    B_xs = P.bufs(4, "xs")
    B_xn = P.bufs(2, "xn")
    B_junk = P.buf("junk")
    B_hT = P.bufs(2, "hT")
    _B_tmpx = P.buf("tmpx")
    B_tmpx = [_B_tmpx, _B_tmpx]
    B_stat = [P.buf(f"stat{t}") for t in range(NT)]
    B_x1d = [P.buf(f"x1d{t}") for t in range(NT)]
    B_out = [P.buf(f"outd{t}") for t in range(NT)]

    def ld(eng, dst, src, b=B_const, n=1):
        P.op(eng, lambda e: e.dma_start(out=dst, in_=src), writes=[b], ndma=n)

    ld("pool", ident_bf[:], ident_d[:, :])
    P.op("pool", lambda e: e.memset(ident_f[:], 0.0), writes=[B_const])
    P.op("pool", lambda e: e.memset(ident_f[0:1, 0:1], 1.0), writes=[B_const])
    P.op("sp", lambda e: e.dma_start(out=ident_f[1:2, 0:2], in_=ident_d[1:2, 0:2]), writes=[B_const], ndma=1)
    P.op("pool", lambda e: e.memset(ones_f[:], 1.0), writes=[B_const])
    ld("pool", pswap_bf[:], pswap_d[:, :])
    ld("pool", mask2_bf[:].rearrange("p a b -> p (a b)"), mask2_d[:, :])
    ld("sp", maskT_f[:], maskT_d[:, :])
    ld("sp", rmask[:], rmask_d[:, :])
    ld("sp", esink[:], sinks_d[:, :])
    ld("sp", aow[:], aow_d[:, :])
    ld("sp", hgw[:], hgw_d[:, :])
    P.op("pool", lambda e: e.memset(epsc[:], EPS), writes=[B_const])
    P.op("act", lambda e: e.activation(out=esink[:], in_=esink[:], func=AF.Exp), reads=[B_const], writes=[B_const])

    pw = ExitStack()
    win = P.sb("win", [128, 8, INC], BF16, pw)
    wkd = P.sb("wkd", [128, 8, 256], BF16, pw)
    wout = P.sb("wout", [128, 8, D], BF16, pw)
    B_w1 = P.buf("w1")
    win_v = win_d.rearrange("(j p) c -> p j c", p=128)
    wout_v = wout_d.rearrange("(j p) c -> p j c", p=128)
    for j in range(8):
        P.op("pool", (lambda e, j=j: e.dma_start(out=win[:, j, :], in_=win_v[:, j, :])), writes=[B_w1], ndma=1)
    for kv in range(2):
        for r in range(2):
            c0 = (2 * kv + r) * 64
            P.op("pool", (lambda e, kv=kv, c0=c0: e.dma_start(out=wkd[:, :, c0:c0 + 64], in_=win_v[:, :, 512 + 64 * kv:576 + 64 * kv])),
                 writes=[B_w1], ndma=1)
    for j in range(8):
        P.op("pool", (lambda e, j=j: e.dma_start(out=wout[:, j, :], in_=wout_v[:, j, :])), writes=[B_w1], ndma=1)

    st = ExitStack()
    cTs = P.sb("cTs", [128, 8, 2], F32, st)
    ca = P.sb("ca", [128, 8, 2], F32, st)
    badaT = P.sb("badaT", [128, 48], F32, st)
    premix = P.sb("premix", [128, 8], F32, st)
    premlp = P.sb("premlp", [128, 8], F32, st)
    lbT = P.sb("lbT", [128, 2, 4], F32, st)
    lbd = P.sb("lbd", [128, 4], F32, st)
    wa = [P.sb(f"wa{i}", [128, 8, 256], F32, st) for i in range(2)]
    modrow = P.sb("modrow", [2, 6 * D], F32, st)
    bgrow = P.sb("bgrow", [2, 2, D], F32, st)
    pwrow = P.sb("pwrow", [2, 2, D], F32, st)
    grow = P.sb("grow", [2, 2, D], F32, st)
    sel = P.sb("sel", [2, 2, 128], F32, st)
    B_wa = P.bufs(2, "wa")
    B_set = P.buf("setup")
    B_row = P.buf("modrow")

    ld("sp", cTs[:], cT_d[:, :, :], B_set)
    ld("sp", badaT[:], bada_d[:, :], B_set)
    ld("sp", premix[:], premix_d[:, :], B_set)
    ld("sp", premlp[:], premlp_d[:, :], B_set)
    ld("sp", lbT[:], lbT_d[:, :, :], B_set)
    ld("sp", bgrow[:], bgrow_d[:, :, :], B_set)
    ld("sp", pwrow[:], pwrow_d[:, :, :], B_set)
    ld("sp", sel[:], sel_d[:, :, :], B_set)
    P.op("act", lambda e: e.activation(out=ca[:], in_=cTs[:], func=AF.Silu), reads=[B_set], writes=[B_set])
    P.op("dve", lambda e: e.tensor_tensor(lbd[:], lbT[:, 1, :], lbT[:, 0, :], ALU.subtract), reads=[B_set], writes=[B_set])
    P.op("act", lambda e: e.activation(out=lbv[:], in_=lbd[:], func=AF.Sigmoid), reads=[B_set], writes=[B_mod])
    P.op("dve", lambda e: e.tensor_scalar(oml[:], lbv[:], -1.0, 1.0, ALU.mult, ALU.add), reads=[B_mod], writes=[B_mod])
    P.op("dve", lambda e: e.tensor_scalar(lbm1[:], lbv[:], 1.0, -1.0, ALU.mult, ALU.add), reads=[B_mod], writes=[B_mod])

    wada_v = wada_d.rearrange("(j p) c -> p j c", p=128)
    for blk in range(24):
        s = blk % 2
        bkm = blk % 2
        P.op("sp", (lambda e, s=s, blk=blk: e.dma_start(out=wa[s][:], in_=wada_v[:, :, blk * 256:(blk + 1) * 256])),
             writes=[B_wa[s]], ndma=1)

        def mm_mod(e, s=s, bkm=bkm):
            return [e.matmul(pb[bkm][0:2, 0:256], ca[:, k, :], wa[s][:, k, :], start=(k == 0), stop=(k == 7)) for k in range(8)]
        P.op("pe", mm_mod, reads=[B_wa[s], B_set], writes=[B_pb[bkm]])
        P.op("act", (lambda e, bkm=bkm, blk=blk: e.copy(modrow[:, blk * 256:(blk + 1) * 256], pb[bkm][0:2, 0:256])),
             reads=[B_pb[bkm]], writes=[B_row])
    pmod = pb[2][:, 0:96]

    def tr_mod(e):
        return [e.transpose(pmod[:, 2 * m:2 * m + 2], modrow[:, m * 128:(m + 1) * 128], ident_f[0:2, 0:2]) for m in range(48)]
    P.op("pe", tr_mod, reads=[B_row, B_const], writes=[B_pb[2]])
    P.op("dve", lambda e: e.tensor_tensor(modT[:], pmod.rearrange("p (m b) -> p m b", b=2),
                                          badaT[:].unsqueeze(2).broadcast_to([128, 48, 2]), ALU.add),
         reads=[B_pb[2], B_set], writes=[B_mod])

    def bc2(v):
        return v[:].unsqueeze(2).broadcast_to([128, 8, 2])
    P.op("dve", lambda e: e.scalar_tensor_tensor(a1[:], modT[:, 8:16, :], 1.0, bc2(premix), ALU.add, ALU.mult),
         reads=[B_mod, B_set], writes=[B_mod])
    P.op("dve", lambda e: e.scalar_tensor_tensor(a2[:], modT[:, 32:40, :], 1.0, bc2(premlp), ALU.add, ALU.mult),
         reads=[B_mod, B_set], writes=[B_mod])
    for gi, c0 in enumerate((2 * D, 5 * D)):
        P.op("dve", (lambda e, gi=gi, c0=c0: e.tensor_tensor(grow[:, gi, :], modrow[:, c0:c0 + D], bgrow[:, gi, :], ALU.add)),
             reads=[B_row, B_set], writes=[B_set])
        P.op("dve", (lambda e, gi=gi: e.tensor_tensor(grow[:, gi, :], grow[:, gi, :], pwrow[:, gi, :], ALU.mult)),
             reads=[B_set], writes=[B_set])
    cnt = 0
    for gi, gvt in enumerate((gv1, gv2)):
        for b in range(2):
            for nh in range(2):
                bkm = 3 + (cnt % 2)
                cnt += 1
                P.op("pe", (lambda e, bkm=bkm, gi=gi, b=b, nh=nh: e.matmul(pb[bkm][:], sel[:, b, :], grow[:, gi, nh * 512:(nh + 1) * 512],
                                                                            start=True, stop=True)),
                     reads=[B_set], writes=[B_pb[bkm]])
                P.op("act", (lambda e, bkm=bkm, gvt=gvt, b=b, nh=nh: e.copy(gvt[b][:, nh * 512:(nh + 1) * 512], pb[bkm][:])),
                     reads=[B_pb[bkm]], writes=[B_mod])
    P.barrier()
    st.close()

    p1 = ExitStack()
    cosS = [P.sb(f"cosS{i}", [128, GW], F32, p1) for i in range(2)]
    sinS = [P.sb(f"sinS{i}", [128, GW], F32, p1) for i in range(2)]
    qraw = [P.sb(f"qraw{i}", [128, 2, GW], BF16, p1) for i in range(2)]
    _rt1 = P.sb("rt1", [128, 2, GW], F32, p1)
    _rt2 = P.sb("rt2", [128, 2, GW], F32, p1)
    rt1 = [_rt1, _rt1]
    rt2 = [_rt2, _rt2]
    qT = [P.sb(f"qT{i}", [128, 4, GW], BF16, p1) for i in range(2)]
    kT = P.sb("kT", [128, 2, KRING * 128], BF16, p1)
    vaug = P.sb("vaug", [128, KRING, 2, 72], BF16, p1)
    hA = [P.sb(f"hA{i}", [128, 2, GW], F32, p1) for i in range(2)]
    hB = [P.sb(f"hB{i}", [128, 2, GW], F32, p1) for i in range(2)]
    hC = [P.sb(f"hC{i}", [128, 2, GW], F32, p1) for i in range(2)]
    qd = [P.sb(f"qd{i}", [128, 4, GW], BF16, p1) for i in range(2)]
    kd = [P.sb(f"kd{i}", [128, 4, GW], BF16, p1) for i in range(2)]
    dec = [P.sb(f"dec{i}", [128, 4, 2], F32, p1) for i in range(2)]
    vt = [P.sb(f"vt{i}", [128, 512], BF16, p1) for i in range(2)]
    gg = [P.sb(f"gg{i}", [128, 512], F32, p1) for i in range(2)]
    Pe = [[P.sb(f"Pe{i}_{h}", [128, 4, 2, 128], BF16, p1) for h in range(2)] for i in range(2)]
    attn = [P.sb(f"attn{i}", [128, 512], F32, p1) for i in range(2)]
    Am = [P.sb(f"Am{i}", [128, 4, 128], BF16, p1) for i in range(2)]
    kdtok = [P.sb(f"kdtok{i}", [128, 4, 128], BF16, p1) for i in range(2)]
    tmpS = [P.sb(f"tmpS{i}", [128, 4, 128], F32, p1) for i in range(2)]
    sqo = [P.sb(f"sqo{i}", [128, 4, 128], F32, p1) for i in range(2)]
    rect = sqo
    cat = [P.sb(f"cat{i}", [128, D], BF16, p1) for i in range(2)]
    catT = [P.sb(f"catT{i}", [128, 8, 128], BF16, p1) for i in range(2)]
    Sst = P.sb("Sst", [128, 4, 128], F32, p1)
    Sb = P.sb("Sb", [128, 4, 128], BF16, p1)

    B_rope = P.bufs(2, "rope")
    B_qraw = P.bufs(2, "qraw")
    _b1 = P.buf("rt1")
    _b2 = P.buf("rt2")
    B_rt1 = [_b1, _b1]
    B_rt2 = [_b2, _b2]
    B_qT = P.bufs(2, "qT")
    B_kT = P.bufs(KRING, "kT")
    B_va = P.bufs(KRING, "va")
    B_hA = P.bufs(2, "hA")
    B_hB = P.bufs(2, "hB")
    B_hC = P.bufs(2, "hC")
    B_qd = P.bufs(2, "qd")
    B_kd = P.bufs(2, "kd")
    B_dec = P.bufs(2, "dec")
    B_vt = P.bufs(2, "vt")
    B_gg = P.bufs(2, "gg")
    B_Pe = [P.bufs(2, f"Pe{i}_") for i in range(2)]
    B_attn = P.bufs(2, "attn")
    B_Am = P.bufs(2, "Am")
    B_kdtok = P.bufs(2, "kdtok")
    B_tmpS = P.bufs(2, "tmpS")
    B_sqo = P.bufs(2, "sqo")
    B_rect = B_sqo
    B_cat = P.bufs(2, "cat")
    B_catT = P.bufs(2, "catT")
    B_S = P.buf("S")
    B_Sb = P.buf("Sb")

    P.op("pool", lambda e: e.memset(vaug[:], 1.0), writes=B_va)

    pools = {"proj": [0, 1], "o": [4, 5, 6]}
    pcnt = {"proj": 0, "o": 0}

    def bank(pool):
        i = pools[pool][pcnt[pool] % len(pools[pool])]
        pcnt[pool] += 1
        return i

    state = {"av": -1, "s": -1}

    def rstd_ops(ss_ap, lnbuf_ap, out_ap, n, reads, writes):
        P.op("act", lambda e: e.activation(out=lnbuf_ap, in_=ss_ap, func=AF.Ln, scale=1.0 / n, bias=epsc[:, 0:1]),
             reads=reads + [B_const], writes=writes)
        P.op("act", lambda e: e.activation(out=out_ap, in_=lnbuf_ap, func=AF.Exp, scale=-0.5), reads=writes, writes=writes)

    def norm_and_transpose(g, a_mod, sh_lo, src_phase2):
        hs = g % 2
        b = (2 * g) // TPS
        ssx, rsx = (ss2, rs2) if src_phase2 else (ss1, rs1)
        for i in range(2):
            T = 2 * g + i
            xsl = T % 4
            P.op("act", (lambda e, T=T, xsl=xsl: e.activation(out=junk[:], in_=xs[xsl][:], func=AF.Square, accum_out=ssx[:, T:T + 1])),
                 reads=[B_xs[xsl]], writes=[B_stat[T]])
            rstd_ops(ssx[:, T:T + 1], rsx[:, T, 0:1], rsx[:, T, 1:2], D, [B_stat[T]], [B_stat[T]])
            P.op("dve", (lambda e, T=T, xsl=xsl, i=i: e.tensor_scalar(xn[i][:], xs[xsl][:], rsx[:, T, 1:2], None, ALU.mult)),
                 reads=[B_xs[xsl], B_stat[T]], writes=[B_xn[i]])
        yield
        for rnd in range(2):
            def tr(e, rnd=rnd):
                ins = []
                for jj in range(4):
                    j = rnd * 4 + jj
                    for i in range(2):
                        ins.append(e.transpose(ptr[:, jj * GW + i * 128: jj * GW + (i + 1) * 128], xn[i][:, j * 128:(j + 1) * 128], ident_bf[:]))
                return ins
            P.op("pe", tr, reads=[B_xn[0], B_xn[1], B_const], writes=[B_ptr])
            for jj in range(4):
                j = rnd * 4 + jj
                P.op("dve", (lambda e, j=j, jj=jj: e.tensor_scalar(hT[hs][:, j, :], ptr[:, jj * GW:(jj + 1) * GW],
                                                                  a_mod[:, j, b:b + 1], modT[:, sh_lo + j, b:b + 1], ALU.mult, ALU.add)),
                     reads=[B_ptr, B_mod], writes=[B_hT[hs]])
            yield

    def stage_A(g, delay=0):
        hs = g % 2
        tp0 = (2 * g) % TPS
        for _ in range(delay):
            yield
        for i in range(2):
            T = 2 * g + i
            xsl = T % 4
            P.op("sp", (lambda e, T=T, xsl=xsl: e.dma_start(out=xs[xsl][:], in_=x_d[T * 128:(T + 1) * 128, :])), writes=[B_xs[xsl]], ndma=1)
        P.op("sp", (lambda e: [e.dma_start(out=cosS[hs][:], in_=cosF_d[:, tp0 * 128: tp0 * 128 + GW]),
                               e.dma_start(out=sinS[hs][:], in_=sinF_d[:, tp0 * 128: tp0 * 128 + GW])]), writes=[B_rope[hs]], ndma=2)
        yield from norm_and_transpose(g, a1, 0, False)

        def fm_proj2(w_ap_fn):
            bk = bank("proj")

            def f(e):
                ins = []
                for cc in range(2):
                    for k in range(8):
                        ins.append(e.matmul(pb[bk][:, cc * GW:(cc + 1) * GW], w_ap_fn(cc, k), hT[hs][:, k, :], start=(k == 0), stop=(k == 7)))
                return ins
            P.op("pe", f, reads=[B_hT[hs], B_w1], writes=[B_pb[bk]])
            return bk

        def fl(t):
            return t[:].rearrange("p a t -> p (a t)")

        def sigmoid_act(dst, bk, bdst):
            P.op("act", (lambda e: e.activation(out=fl(dst), in_=pb[bk][:], func=AF.Exp, scale=-1.0)), reads=[B_pb[bk]], writes=[bdst])
            P.op("act", (lambda e: e.activation(out=fl(dst), in_=fl(dst), func=AF.Ln, bias=ones_f[:, 0:1])), reads=[bdst, B_const], writes=[bdst])
            P.op("act", (lambda e: e.activation(out=fl(dst), in_=fl(dst), func=AF.Exp, scale=-1.0)), reads=[bdst], writes=[bdst])

        def hgrn_pair(p):
            s = p % 2
            bk = fm_proj2(lambda cc, k: win[:, k, 1280 + (2 * p + cc) * 128: 1280 + (2 * p + cc + 1) * 128])
            P.op("act", (lambda e: e.activation(out=fl(hA[s]), in_=pb[bk][:], func=AF.Exp, scale=-1.0)), reads=[B_pb[bk]], writes=[B_hA[s]])
            yield
            P.op("act", (lambda e: e.activation(out=fl(hA[s]), in_=fl(hA[s]), func=AF.Ln, bias=ones_f[:, 0:1])), reads=[B_hA[s], B_const], writes=[B_hA[s]])
            P.op("act", (lambda e: e.activation(out=fl(hA[s]), in_=fl(hA[s]), func=AF.Exp, scale=-1.0)), reads=[B_hA[s]], writes=[B_hA[s]])
            yield
            for cc in range(2):
                h = 2 * p + cc
                P.op("act", (lambda e, cc=cc, h=h: e.activation(out=hB[s][:, cc, :], in_=hA[s][:, cc, :], func=AF.Ln, scale=oml[:, h:h + 1], bias=lbv[:, h:h + 1])),
                     reads=[B_hA[s], B_mod], writes=[B_hB[s]])
            for cc in range(2):
                h = 2 * p + cc
                P.op("dve", (lambda e, cc=cc, h=h: e.tensor_scalar(hA[s][:, cc, :], hA[s][:, cc, :], lbm1[:, h:h + 1], oml[:, h:h + 1], ALU.mult, ALU.add)),
                     reads=[B_hA[s], B_mod], writes=[B_hA[s]])
            yield
            P.op("dve", (lambda e: e.tensor_tensor_scan(fl(hC[s]), rmask[:], fl(hB[s]), 0.0, ALU.mult, ALU.add)),
                 reads=[B_hB[s], B_const], writes=[B_hC[s]])
            P.op("act", (lambda e: e.activation(out=fl(hB[s]), in_=fl(hC[s]), func=AF.Exp, scale=-1.0)), reads=[B_hC[s]], writes=[B_hB[s]])
            yield
            P.op("pool", (lambda e: e.tensor_tensor(kd[hs][:, 2 * p:2 * p + 2, :], hA[s][:], hB[s][:], ALU.mult)),
                 reads=[B_hA[s], B_hB[s]], writes=[B_kd[hs]])
            P.op("act", (lambda e: e.activation(out=fl(hB[s]), in_=fl(hC[s]), func=AF.Exp)), reads=[B_hC[s]], writes=[B_hB[s]])
            P.op("pool", (lambda e: e.tensor_copy(dec[hs][:, 2 * p:2 * p + 2, :], hB[s][:, :, 127::128])),
                 reads=[B_hB[s]], writes=[B_dec[hs]])
            yield
            bk2 = fm_proj2(lambda cc, k: win[:, k, 768 + (2 * p + cc) * 128: 768 + (2 * p + cc + 1) * 128])
            P.op("act", (lambda e: e.activation(out=fl(hA[s]), in_=pb[bk2][:], func=AF.Exp, scale=-1.0)), reads=[B_pb[bk2]], writes=[B_hA[s]])
            P.op("dve", (lambda e: e.tensor_copy(fl(hC[s]), pb[bk2][:])), reads=[B_pb[bk2], B_hA[s]], writes=[B_hC[s]])
            yield
            P.op("act", (lambda e: e.activation(out=fl(hA[s]), in_=fl(hA[s]), func=AF.Ln, bias=ones_f[:, 0:1])), reads=[B_hA[s], B_const], writes=[B_hA[s]])
            P.op("act", (lambda e: e.activation(out=fl(hA[s]), in_=fl(hA[s]), func=AF.Exp, scale=-1.0)), reads=[B_hA[s]], writes=[B_hA[s]])
            yield
            P.op("dve", (lambda e: e.tensor_tensor(fl(hA[s]), fl(hC[s]), fl(hA[s]), ALU.mult)), reads=[B_hC[s], B_hA[s]], writes=[B_hA[s]])
            P.op("pool", (lambda e: e.tensor_tensor(qd[hs][:, 2 * p:2 * p + 2, :], hA[s][:], hB[s][:], ALU.mult)),
                 reads=[B_hA[s], B_hB[s]], writes=[B_qd[hs]])
            yield

        def qk_proj(cp):
            s = cp % 2
            if cp < 2:
                bk = fm_proj2(lambda cc, k: win[:, k, (2 * cp + cc) * 128:(2 * cp + cc + 1) * 128])
            else:
                bk = fm_proj2(lambda cc, k: wkd[:, k, cc * 128:(cc + 1) * 128])
            P.op("dve", (lambda e: e.tensor_copy(fl(qraw[s]), pb[bk][:])), reads=[B_pb[bk]], writes=[B_qraw[s]])

        def rope_part(cp):
            s = cp % 2
            bk2 = bank("proj")
            P.op("pe", (lambda e: e.matmul(pb[bk2][:], pswap_bf[:], fl(qraw[s]), start=True, stop=True)),
                 reads=[B_qraw[s], B_const], writes=[B_pb[bk2]])
            sinb = sinS[hs][:].unsqueeze(1).broadcast_to([128, 2, GW])
            cosb = cosS[hs][:].unsqueeze(1).broadcast_to([128, 2, GW])
            P.op("dve", (lambda e: e.tensor_tensor(rt1[s][:], pb[bk2][:].rearrange("p (a t) -> p a t", a=2), sinb, ALU.mult)),
                 reads=[B_pb[bk2], B_rope[hs]], writes=[B_rt1[s]])
            P.op("pool", (lambda e: e.tensor_tensor(rt2[s][:], qraw[s][:], cosb, ALU.mult)),
                 reads=[B_qraw[s], B_rope[hs]], writes=[B_rt2[s]])
            if cp < 2:
                P.op("pool", (lambda e: e.tensor_tensor(qT[hs][:, 2 * cp:2 * cp + 2, :], rt1[s][:], rt2[s][:], ALU.add)),
                     reads=[B_rt1[s], B_rt2[s]], writes=[B_qT[hs]])
            else:
                r0 = ((2 * g) % KRING)
                P.op("pool", (lambda e: e.tensor_tensor(kT[:, :, r0 * 128: r0 * 128 + GW], rt1[s][:], rt2[s][:], ALU.add)),
                     reads=[B_rt1[s], B_rt2[s]], writes=[B_kT[r0], B_kT[r0 + 1]])

        def qk_gen():
            for cp in range(4):
                if cp < 3:
                    qk_proj(cp)
                if cp >= 1:
                    rope_part(cp - 1)
                yield

        subs = [hgrn_pair(0), hgrn_pair(1), qk_gen()]
        while subs:
            for sg in list(subs):
                try:
                    next(sg)
                except StopIteration:
                    subs.remove(sg)
            yield

    def stage_B(T):
        g = T // 2
        i = T % 2
        hs = g % 2
        ts = T % 2
        b = T // TPS
        tp = T % TPS
        xsl = T % 4
        rk = T % KRING
        rkp = (T - 1) % KRING
        tc = slice(i * 128, (i + 1) * 128)
        kbs = (0, 1) if tp > 0 else (1,)
        if i == 1:
            for _ in range(KDELAY):
                yield

        def tm_proj(c0, n, eng_evac_fn):
            bk = bank("proj")

            def f(e):
                return [e.matmul(pb[bk][:, 0:n], hT[hs][:, k, tc], win[:, k, c0:c0 + n], start=(k == 0), stop=(k == 7)) for k in range(8)]
            P.op("pe", f, reads=[B_hT[hs], B_w1], writes=[B_pb[bk]])
            eng_evac_fn(bk)

        def sc_part(half):
            kv = half

            def f_sc(e):
                ins = []
                for hh in range(4):
                    head = 4 * half + hh
                    c = head // 2
                    pr = slice((head % 2) * 64, (head % 2) * 64 + 64)
                    for kb in kbs:
                        rr = rkp if kb == 0 else rk
                        ins.append(e.matmul(pb[2 + hh % 2][:, ((hh // 2) * 2 + kb) * 128:((hh // 2) * 2 + kb + 1) * 128],
                                            kT[pr, kv, rr * 128:(rr + 1) * 128], qT[hs][pr, c, tc], start=True, stop=True))
                return ins
            rd = [B_qT[hs], B_kT[rk]] + ([B_kT[rkp]] if tp > 0 else [])
            P.op("pe", f_sc, reads=rd, writes=[B_pb[2], B_pb[3]])
            for bb in range(2):
                if tp > 0:
                    P.op("act", (lambda e, bb=bb: e.activation(out=Pe[ts][half][:, bb::2, :, :],
                                                               in_=pb[2 + bb][:].rearrange("p (a b q) -> p a b q", a=2, b=2), func=AF.Exp, scale=0.125)),
                         reads=[B_pb[2 + bb]], writes=[B_Pe[ts][half]])
                else:
                    P.op("act", (lambda e, bb=bb: e.activation(out=Pe[ts][half][:, bb::2, 1, :],
                                                               in_=pb[2 + bb][:].rearrange("p (a b q) -> p a b q", a=2, b=2)[:, :, 1, :],
                                                               func=AF.Exp, scale=0.125)),
                         reads=[B_pb[2 + bb]], writes=[B_Pe[ts][half]])
            if tp > 0:
                P.op("dve", (lambda e: e.tensor_tensor(Pe[ts][half][:], Pe[ts][half][:], mask2_bf[:].unsqueeze(1).broadcast_to([128, 4, 2, 128]), ALU.mult)),
                     reads=[B_Pe[ts][half], B_const], writes=[B_Pe[ts][half]])
            else:
                P.op("dve", (lambda e: e.tensor_tensor(Pe[ts][half][:, :, 1, :], Pe[ts][half][:, :, 1, :],
                                                       mask2_bf[:, 1, :].unsqueeze(1).broadcast_to([128, 4, 128]), ALU.mult)),
                     reads=[B_Pe[ts][half], B_const], writes=[B_Pe[ts][half]])

        def pv_part(half):
            kv = half
            bkO = bank("o")

            def f_pv(e):
                ins = []
                for hh in range(4):
                    for n, kb in enumerate(kbs):
                        rr = rkp if kb == 0 else rk
                        ins.append(e.matmul(pb[bkO][:, hh * 128:hh * 128 + 72], Pe[ts][half][:, hh, kb, :], vaug[:, rr, kv, :],
                                            start=(n == 0), stop=(n == len(kbs) - 1)))
                return ins
            rd = [B_Pe[ts][half], B_va[rk]] + ([B_va[rkp]] if tp > 0 else [])
            P.op("pe", f_pv, reads=rd, writes=[B_pb[bkO]])
            pO = pb[bkO][:].rearrange("p (h d) -> p h d", h=4)
            P.op("dve", (lambda e: e.tensor_tensor(den[:, T, half, :], pO[:, :, 64], esink[:, 4 * half:4 * half + 4], ALU.add)),
                 reads=[B_pb[bkO], B_const], writes=[B_stat[T]])
            P.op("dve", (lambda e: e.reciprocal(rden[:, T, half, :], den[:, T, half, :])), reads=[B_stat[T]], writes=[B_stat[T]])
            P.op("dve", (lambda e: e.tensor_tensor(attn[ts][:, half * 256:(half + 1) * 256].rearrange("p (h d) -> p h d", h=4),
                                                   pO[:, :, 0:64], rden[:, T, half, :].unsqueeze(2).broadcast_to([128, 4, 64]), ALU.mult)),
                 reads=[B_pb[bkO], B_stat[T]], writes=[B_attn[ts]])

        tm_proj(640, 128, lambda bk: P.op("dve", (lambda e: e.tensor_copy(vaug[:, rk, :, 0:64], pb[bk][:, 0:128].rearrange("p (a d) -> p a d", a=2))),
                                          reads=[B_pb[bk]], writes=[B_va[rk]]))
        state["av"] = T
        tm_proj(1792, 512, lambda bk: P.op("act", (lambda e: e.copy(vt[ts][:], pb[bk][:])), reads=[B_pb[bk]], writes=[B_vt[ts]]))

        def ev_g(bk):
            P.op("act", (lambda e: e.activation(out=gg[ts][:], in_=pb[bk][:], func=AF.Exp, scale=-1.0)), reads=[B_pb[bk]], writes=[B_gg[ts]])
            P.op("act", (lambda e: e.activation(out=gg[ts][:], in_=gg[ts][:], func=AF.Ln, bias=ones_f[:, 0:1])), reads=[B_gg[ts], B_const], writes=[B_gg[ts]])
            P.op("act", (lambda e: e.activation(out=gg[ts][:], in_=gg[ts][:], func=AF.Exp, scale=-1.0)), reads=[B_gg[ts]], writes=[B_gg[ts]])
            P.op("dve", (lambda e: e.tensor_tensor(gg[ts][:], pb[bk][:], gg[ts][:], ALU.mult)), reads=[B_pb[bk], B_gg[ts]], writes=[B_gg[ts]])
            P.op("pool", (lambda e: e.tensor_tensor(gg[ts][:], gg[ts][:], hgw[:], ALU.mult)), reads=[B_gg[ts], B_const], writes=[B_gg[ts]])
        tm_proj(2304, 512, ev_g)
        bkA = bank("o")

        def f_At(e):
            return [e.matmul(pb[bkA][:, h * 128:(h + 1) * 128], kd[hs][:, h, tc], qd[hs][:, h, tc], start=True, stop=True) for h in range(4)]
        P.op("pe", f_At, reads=[B_kd[hs], B_qd[hs]], writes=[B_pb[bkA]])
        P.op("dve", (lambda e: e.tensor_tensor(Am[ts][:], pb[bkA][:].rearrange("p (h t) -> p h t", h=4),
                                               maskT_f[:].unsqueeze(1).broadcast_to([128, 4, 128]), ALU.mult)),
             reads=[B_pb[bkA], B_const], writes=[B_Am[ts]])

        def f_kt(e):
            return [e.transpose(ptr[:, h * 128:(h + 1) * 128], kd[hs][:, h, tc], ident_bf[:]) for h in range(4)]
        P.op("pe", f_kt, reads=[B_kd[hs], B_const], writes=[B_ptr])
        P.op("dve", (lambda e: e.tensor_copy(kdtok[ts][:].rearrange("p h d -> p (h d)"), ptr[:, 0:512])), reads=[B_ptr], writes=[B_kdtok[ts]])
        yield
        while tp > 0 and state["av"] < T - 1:
            yield
        sc_part(0)
        yield
        sc_part(1)
        pv_part(0)
        yield
        pv_part(1)
        P.op("act", (lambda e: e.activation(out=junk[:, 0:512], in_=attn[ts][:], func=AF.Square, accum_out=ssa[:, T:T + 1])),
             reads=[B_attn[ts]], writes=[B_stat[T]])
        rstd_ops(ssa[:, T:T + 1], rsa[:, T, 0:1], rsa[:, T, 1:2], 512, [B_stat[T]], [B_stat[T]])
        P.op("dve", (lambda e: e.scalar_tensor_tensor(cat[ts][:, 0:512], attn[ts][:], rsa[:, T, 1:2], aow[:], ALU.mult, ALU.mult)),
             reads=[B_attn[ts], B_stat[T], B_const], writes=[B_cat[ts]])
        yield

        while state["s"] < T - 1:
            yield
        if tp == 0:
            P.op("pool", lambda e: e.memset(Sst[:], 0.0), writes=[B_S])
            P.op("pool", lambda e: e.memset(Sb[:], 0.0), writes=[B_Sb])
        bko = bank("o")

        def f_o(e):
            ins = []
            for h in range(4):
                ins.append(e.matmul(pb[bko][:, h * 128:(h + 1) * 128], Am[ts][:, h, :], vt[ts][:, h * 128:(h + 1) * 128], start=True, stop=False))
                ins.append(e.matmul(pb[bko][:, h * 128:(h + 1) * 128], qd[hs][:, h, tc], Sb[:, h, :], start=False, stop=True))
            return ins
        P.op("pe", f_o, reads=[B_Am[ts], B_vt[ts], B_qd[hs], B_Sb], writes=[B_pb[bko]])
        bkK = bank("o")

        def f_kv(e):
            return [e.matmul(pb[bkK][:, h * 128:(h + 1) * 128], kdtok[ts][:, h, :], vt[ts][:, h * 128:(h + 1) * 128], start=True, stop=True)
                    for h in range(4)]
        P.op("pe", f_kv, reads=[B_kdtok[ts], B_vt[ts]], writes=[B_pb[bkK]])
        decb = dec[hs][:, :, i:i + 1].broadcast_to([128, 4, 128])
        P.op("dve", (lambda e: e.tensor_tensor(tmpS[ts][:], pb[bkK][:].rearrange("p (h d) -> p h d", h=4), Sst[:], ALU.add)),
             reads=[B_pb[bkK], B_S], writes=[B_tmpS[ts]])
        P.op("dve", (lambda e: e.tensor_tensor(Sb[:], tmpS[ts][:], decb, ALU.mult)), reads=[B_tmpS[ts], B_dec[hs]], writes=[B_Sb])
        P.op("pool", (lambda e: e.tensor_tensor(Sst[:], tmpS[ts][:], decb, ALU.mult)), reads=[B_tmpS[ts], B_dec[hs]], writes=[B_S])
        state["s"] = T
        P.op("act", (lambda e: e.activation(out=sqo[ts][:].rearrange("p h d -> p (h d)"), in_=pb[bko][:], func=AF.Square)),
             reads=[B_pb[bko]], writes=[B_sqo[ts]])
        P.op("dve", (lambda e: e.tensor_reduce(ssr[:, T, :], sqo[ts][:], AX.X, ALU.add)), reads=[B_sqo[ts]], writes=[B_stat[T]])
        rstd_ops(ssr[:, T, :], rsr[:, T, 0:4], rsr[:, T, 4:8], 128, [B_stat[T]], [B_stat[T]])
        P.op("dve", (lambda e: e.tensor_tensor(rect[ts][:], pb[bko][:].rearrange("p (h d) -> p h d", h=4),
                                               rsr[:, T, 4:8].unsqueeze(2).broadcast_to([128, 4, 128]), ALU.mult)),
             reads=[B_pb[bko], B_stat[T]], writes=[B_rect[ts]])
        P.op("pool", (lambda e: e.tensor_tensor(cat[ts][:, 512:1024], rect[ts][:].rearrange("p h d -> p (h d)"), gg[ts][:], ALU.mult)),
             reads=[B_rect[ts], B_gg[ts]], writes=[B_cat[ts]])
        yield

        def f_ct(e):
            return [e.transpose(ptr[:, j * 128:(j + 1) * 128], cat[ts][:, j * 128:(j + 1) * 128], ident_bf[:]) for j in range(8)]
        P.op("pe", f_ct, reads=[B_cat[ts], B_const], writes=[B_ptr])
        P.op("dve", (lambda e: e.tensor_copy(catT[ts][:].rearrange("p j t -> p (j t)"), ptr[:, :])), reads=[B_ptr], writes=[B_catT[ts]])
        yield

        def f_mix(e):
            ins = []
            for nh in range(2):
                for k in range(8):
                    ins.append(e.matmul(pb[2 + nh][:], catT[ts][:, k, :], wout[:, k, nh * 512:(nh + 1) * 512], start=(k == 0), stop=(k == 7)))
            return ins
        P.op("pe", f_mix, reads=[B_catT[ts], B_w1], writes=[B_pb[2], B_pb[3]])
        for nh in range(2):
            P.op("act", (lambda e, nh=nh: e.activation(out=junk[:, 0:512], in_=pb[2 + nh][:], func=AF.Square, accum_out=ssm[:, T, nh:nh + 1])),
                 reads=[B_pb[2 + nh]], writes=[B_stat[T]])
        P.op("dve", (lambda e: e.tensor_tensor(ssm[:, T, 2:3], ssm[:, T, 0:1], ssm[:, T, 1:2], ALU.add)), reads=[B_stat[T]], writes=[B_stat[T]])
        rstd_ops(ssm[:, T, 2:3], ssm[:, T, 3:4], ssm[:, T, 0:1], D, [B_stat[T]], [B_stat[T]])
        for nh in range(2):
            P.op("dve", (lambda e, nh=nh: e.scalar_tensor_tensor(tmpx[ts][:, nh * 512:(nh + 1) * 512], pb[2 + nh][:], ssm[:, T, 0:1],
                                                                 gv1[b][:, nh * 512:(nh + 1) * 512], ALU.mult, ALU.mult)),
                 reads=[B_pb[2 + nh], B_stat[T], B_mod], writes=[B_tmpx[ts]])
        P.op("pool", (lambda e: e.tensor_tensor(xs[xsl][:], tmpx[ts][:], xs[xsl][:], ALU.add)), reads=[B_tmpx[ts], B_xs[xsl]], writes=[B_xs[xsl]])
        P.op("sp", (lambda e: e.dma_start(out=x1_d[T * 128:(T + 1) * 128, :], in_=xs[xsl][:])), reads=[B_xs[xsl]], writes=[B_x1d[T]], ndma=1)
        yield

    if stop >= 2:
        for _ in stage_A(0):
            pass
    for g in range(ngroups if stop >= 3 else 0):
        gens = [stage_B(2 * g), stage_B(2 * g + 1)]
        if g + 1 < ngroups:
            gens.append(stage_A(g + 1, KADELAY))
        run_interleaved(gens, [1, 1, KWA])

    P.barrier()
    p1.close()
    pw.close()
    pp.close()

    p2 = ExitStack()
    wup = P.sb("wup", [128, 8, DFF], BF16, p2)
    wdn = P.sb("wdn", [128, 32, D], BF16, p2)
    rl = [P.sb(f"rl{i}", [128, 512], BF16, p2) for i in range(2)]
    uT = P.sb("uT", [128, 32, GW], BF16, p2)
    B_wup = P.bufs(4, "wup")
    B_wdn = P.bufs(8, "wdn")
    B_rl = P.bufs(2, "rl")
    B_uT = P.bufs(16, "uT")

    wup_v = wup_d.rearrange("(j p) c -> p j c", p=128)
    wdn_v = wdn_d.rearrange("(j p) c -> p j c", p=128)
    for q in range(4 if stop >= 4 else 0):
        P.op("pool", (lambda e, q=q: e.dma_start(out=wup[:, :, q * 1024:(q + 1) * 1024], in_=wup_v[:, :, q * 1024:(q + 1) * 1024])),
             writes=[B_wup[q]], ndma=1)
    for j4 in range(8 if stop >= 4 else 0):
        P.op("pool", (lambda e, j4=j4: e.dma_start(out=wdn[:, 4 * j4:4 * j4 + 4, :], in_=wdn_v[:, 4 * j4:4 * j4 + 4, :])), writes=[B_wdn[j4]], ndma=1)

    up_banks = [4, 5, 6]
    upc = [0]
    down_units = [(0, 1), (2, 3)]

    def mlp_prep(g):
        for i in range(2):
            T = 2 * g + i
            xsl = T % 4
            P.op("sp", (lambda e, T=T, xsl=xsl: e.dma_start(out=xs[xsl][:], in_=x1_d[T * 128:(T + 1) * 128, :])),
                 reads=[B_x1d[T]], writes=[B_xs[xsl]], ndma=1)
        yield from norm_and_transpose(g, a2, 24, True)

    def mlp_main(g):
        hs = g % 2
        for cp in range(16):
            bk = up_banks[upc[0] % 3]
            upc[0] += 1
            s = cp % 2

            def f_up(e, cp=cp, bk=bk):
                ins = []
                for cc in range(2):
                    c = 2 * cp + cc
                    for k in range(8):
                        ins.append(e.matmul(pb[bk][:, cc * GW:(cc + 1) * GW], wup[:, k, c * 128:(c + 1) * 128], hT[hs][:, k, :],
                                            start=(k == 0), stop=(k == 7)))
                return ins
            P.op("pe", f_up, reads=[B_hT[hs], B_wup[cp // 4]], writes=[B_pb[bk]])
            P.op("act", (lambda e, bk=bk, s=s: e.activation(out=rl[s][:], in_=pb[bk][:], func=AF.Relu)), reads=[B_pb[bk]], writes=[B_rl[s]])
            P.op("pool", (lambda e, s=s, cp=cp: e.tensor_tensor(uT[:, 2 * cp:2 * cp + 2, :].rearrange("p a t -> p (a t)"), rl[s][:], rl[s][:], ALU.mult)),
                 reads=[B_rl[s]], writes=[B_uT[cp]])
            if cp % 4 == 3:
                yield
        for i in range(2):
            T = 2 * g + i
            xsl = T % 4
            b = T // TPS
            ts = T % 2
            du = down_units[i]

            for cb in range(4):
                def f_dn(e, i=i, du=du, cb=cb):
                    ins = []
                    for nh in range(2):
                        for c in range(8 * cb, 8 * cb + 8):
                            ins.append(e.matmul(pb[du[nh]][:], uT[:, c, i * 128:(i + 1) * 128], wdn[:, c, nh * 512:(nh + 1) * 512],
                                                start=(c == 0), stop=(c == 31)))
                    return ins
                P.op("pe", f_dn, reads=B_uT[4 * cb:4 * cb + 4] + [B_wdn[2 * cb], B_wdn[2 * cb + 1]], writes=[B_pb[du[0]], B_pb[du[1]]])
            for nh in range(2):
                P.op("act", (lambda e, nh=nh, du=du, T=T: e.activation(out=junk[:, 0:512], in_=pb[du[nh]][:], func=AF.Square, accum_out=ssy[:, T, nh:nh + 1])),
                     reads=[B_pb[du[nh]]], writes=[B_stat[T]])
            P.op("dve", (lambda e, T=T: e.tensor_tensor(ssy[:, T, 2:3], ssy[:, T, 0:1], ssy[:, T, 1:2], ALU.add)), reads=[B_stat[T]], writes=[B_stat[T]])
            rstd_ops(ssy[:, T, 2:3], ssy[:, T, 3:4], ssy[:, T, 0:1], D, [B_stat[T]], [B_stat[T]])
            for nh in range(2):
                P.op("dve", (lambda e, nh=nh, du=du, T=T, ts=ts, b=b: e.scalar_tensor_tensor(
                    tmpx[ts][:, nh * 512:(nh + 1) * 512], pb[du[nh]][:], ssy[:, T, 0:1], gv2[b][:, nh * 512:(nh + 1) * 512], ALU.mult, ALU.mult)),
                    reads=[B_pb[du[nh]], B_stat[T], B_mod], writes=[B_tmpx[ts]])
            P.op("pool", (lambda e, ts=ts, xsl=xsl: e.tensor_tensor(xs[xsl][:], tmpx[ts][:], xs[xsl][:], ALU.add)),
                 reads=[B_tmpx[ts], B_xs[xsl]], writes=[B_xs[xsl]])
            P.op("sp", (lambda e, T=T, xsl=xsl: e.dma_start(out=out_d[T * 128:(T + 1) * 128, :], in_=xs[xsl][:])),
                 reads=[B_xs[xsl]], writes=[B_out[T]], ndma=1)
            yield

    n2 = ngroups if stop >= 5 else 0
    if n2:
        for _ in mlp_prep(0):
            pass
    for g in range(n2):
        gens = [mlp_main(g)]
        if g + 1 < n2:
            gens.append(mlp_prep(g + 1))
        run_interleaved(gens)

    P.op("sp", None, reads=B_out)
    p2.close()
    P.emit()
    return nc


_NC_CACHE = {}


def _layout_inputs(inp):
    f = lambda a: np.ascontiguousarray(np.asarray(a, dtype=np.float32))
    consts = _host_consts()
    colT = lambda v, n: f(np.asarray(v).reshape(n, 128).T)
    shared = {
        "w_ada": f(inp["w_ada"][0]),
        "b_adaT": colT(inp["b_ada"][0], 48),
        "pre_mixT": colT(inp["pre_w_mix"][0], 8),
        "pre_mlpT": colT(inp["pre_w_mlp"][0], 8),
        "bgrow": f(np.broadcast_to(np.stack([np.asarray(inp["b_ada"][0])[2 * D:3 * D], np.asarray(inp["b_ada"][0])[5 * D:6 * D]])[None], (2, 2, D))),
        "pwrow": f(np.broadcast_to(np.stack([np.asarray(inp["post_w_mix"][0]), np.asarray(inp["post_w_mlp"][0])])[None], (2, 2, D))),
        "w_in": f(inp["w_in"][0]),
        "w_out": f(inp["w_out"][0]),
        "w_up": f(inp["w_up"][0]),
        "w_down": f(inp["w_down"][0]),
        "sinks_rep": f(np.broadcast_to(np.asarray(inp["attn_sinks"][0])[None, :], (128, 8))),
        "aow_rep": f(np.broadcast_to(np.asarray(inp["attn_out_w"][0])[None, :], (128, 512))),
        "hgw_rep": f(np.broadcast_to(np.tile(np.asarray(inp["hg_norm_w"][0]), 4)[None, :], (128, 512))),
        "lbT": f(np.asarray(inp["lb_table"]).reshape(2, 4, 128).transpose(2, 0, 1)),
    }
    shared.update(consts)
    x = np.asarray(inp["x"], dtype=np.float32)
    c = np.asarray(inp["c"], dtype=np.float32)
    maps = []
    for i in range(NCORES):
        m = dict(shared)
        m["x"] = f(x[NSEQ * i:NSEQ * (i + 1)].reshape(TOK, D))
        cc = c[NSEQ * i:NSEQ * (i + 1)]
        m["cT"] = f(cc.reshape(2, 8, 128).transpose(2, 1, 0))
        maps.append(m)
    return maps


def kernel(**inputs):
    if "nc" not in _NC_CACHE:
        _NC_CACHE["nc"] = build_program()
    nc = _NC_CACHE["nc"]
    maps = _layout_inputs(inputs)
    res = run_bass_kernel_spmd(nc, maps, core_ids=list(range(NCORES)))
    outs = [np.asarray(r["out"], dtype=np.float32).reshape(NSEQ, SEQ, D) for r in res.results]
    return np.concatenate(outs, axis=0)
```
